# Optimizing a Trainium2 kernel written in Bass

```python
import math
import jax, jax.numpy as jnp
from jax import lax
import numpy as np

D_MODEL = 2048
BATCH = 8
SEQ = 4096
DEPTH = 1

N_META = 16
CHUNK = 64
NORM_EPS = 1e-6
DN_HEADS = D_MODEL // 128
DN_DK = 128
DN_DV = 128
DN_W = DN_HEADS * DN_DV
DN_CONV = 4
M2_HEAD_DIM = 64
M2_HEADS = D_MODEL // M2_HEAD_DIM
M2_GROUPS = 4
M2_HPG = M2_HEADS // M2_GROUPS
M2_STATE = 128
M2_W = M2_HEADS * M2_HEAD_DIM
M2_CONV = 4
D_MIX = DN_W + M2_W
D_IN_PROJ = 4 * DN_W + 2 * DN_HEADS + 2 * M2_W + 2 * M2_GROUPS * M2_STATE + M2_HEADS
D_FF = 11 * D_MODEL // 4
FFN_CONV = 3

kernel_name = "hybrid_gdn_mamba2_convffn_meta"


def rms_norm(x, w):
    xf = x.astype(jnp.float32)
    y = xf * lax.rsqrt(jnp.mean(xf * xf, axis=-1, keepdims=True) + NORM_EPS)
    return (y * w.astype(jnp.float32)).astype(x.dtype)


def causal_dwconv(x, w):
    k, c = w.shape
    return lax.conv_general_dilated(x, w[:, None, :].astype(x.dtype), window_strides=(1,),
                                    padding=[(k - 1, 0)],
                                    dimension_numbers=('NWC', 'WIO', 'NWC'),
                                    feature_group_count=c)


def to_chunks(t):
    pad = [(0, 0), (CHUNK - N_META, 0)] + [(0, 0)] * (t.ndim - 2)
    t = jnp.pad(t, pad)
    b, lp = t.shape[:2]
    t = t.reshape((b, lp // CHUNK, CHUNK) + t.shape[2:])
    return jnp.moveaxis(t, 1, 0)


def from_chunks(t):
    t = jnp.moveaxis(t, 0, 1)
    t = t.reshape((t.shape[0], t.shape[1] * t.shape[2]) + t.shape[3:])
    return t[:, CHUNK - N_META:]


def gated_delta_rule(q, k, v, beta, g):
    causal = jnp.tril(jnp.ones((CHUNK, CHUNK), dtype=bool))
    strict = jnp.tril(jnp.ones((CHUNK, CHUNK), dtype=bool), -1)

    def step(state, inp):
        qc, kc, vc, bc, gc = inp
        gcum = jnp.cumsum(gc, axis=1)
        gh = jnp.swapaxes(gcum, 1, 2)
        decay = jnp.exp(jnp.where(causal, gh[..., :, None] - gh[..., None, :], -jnp.inf))
        kk = jnp.einsum('blhd,bshd->bhls', kc, kc)
        bh = jnp.swapaxes(bc, 1, 2)
        a_mat = jnp.where(strict, bh[..., :, None] * kk * decay, 0.0)
        rhs = jnp.concatenate([vc * bc[..., None], kc * (bc * jnp.exp(gcum))[..., None]], axis=-1)
        rhs = jnp.swapaxes(rhs, 1, 2)
        sol = lax.linalg.triangular_solve(a_mat, rhs, left_side=True, lower=True,
                                          unit_diagonal=True)
        u, w = sol[..., :DN_DV], sol[..., DN_DV:]
        v_new = u - jnp.einsum('bhlk,bhkv->bhlv', w, state)
        o_inter = jnp.einsum('blhk,bhkv->bhlv', qc * jnp.exp(gcum)[..., None], state)
        qk = jnp.einsum('blhd,bshd->bhls', qc, kc) * decay
        o = o_inter + jnp.einsum('bhls,bhsv->bhlv', qk, v_new)
        g_last = gcum[:, -1:, :]
        state = (state * jnp.exp(g_last[:, 0])[..., None, None]
                 + jnp.einsum('bshk,bhsv->bhkv', kc * jnp.exp(g_last - gcum)[..., None], v_new))
        return state, jnp.swapaxes(o, 1, 2)

    b = q.shape[0]
    s0 = jnp.zeros((b, DN_HEADS, DN_DK, DN_DV), jnp.float32)
    _, ys = lax.scan(step, s0, (to_chunks(q), to_chunks(k), to_chunks(v),
                                to_chunks(beta), to_chunks(g)))
    return from_chunks(ys)


def ssd_chunked(xs, dt, a, bm, cm):
    causal = jnp.tril(jnp.ones((CHUNK, CHUNK), dtype=bool))[None, :, :, None, None]

    def step(state, inp):
        xc, dtc, ac, bc, cc = inp
        acs = jnp.cumsum(ac, axis=1)
        lmat = jnp.exp(jnp.where(causal, acs[:, :, None] - acs[:, None, :], -jnp.inf))
        xdt = xc * dtc[..., None]
        cb = jnp.einsum('blgn,bsgn->blsg', cc, bc)
        y_diag = jnp.einsum('blsg,blsgr,bsgrp->blgrp', cb, lmat, xdt)
        y_off = jnp.einsum('blgn,bgrpn->blgrp', cc, state) * jnp.exp(acs)[..., None]
        a_last = acs[:, -1]
        state = (state * jnp.exp(a_last)[..., None, None]
                 + jnp.einsum('bsgn,bsgr,bsgrp->bgrpn', bc, jnp.exp(a_last[:, None] - acs), xdt))
        return state, y_diag + y_off

    b = xs.shape[0]
    s0 = jnp.zeros((b, M2_GROUPS, M2_HPG, M2_HEAD_DIM, M2_STATE), jnp.float32)
    _, ys = lax.scan(step, s0, (to_chunks(xs), to_chunks(dt), to_chunks(a),
                                to_chunks(bm), to_chunks(cm)))
    return from_chunks(ys)


def hybrid_mixer(h, w_in, dn_conv_w, dn_a_log, dn_dt_bias, dn_norm_w,
                 m2_conv_w, m2_conv_b, m2_a_log, m2_dt_bias, m2_d, m2_norm_w, w_out):
    b, l, _ = h.shape
    f32 = jnp.float32
    proj = h @ w_in
    sizes = [3 * DN_W, DN_W, DN_HEADS, DN_HEADS, M2_W, M2_W + 2 * M2_GROUPS * M2_STATE]
    dn_qkv, dn_z, dn_b, dn_a, m2_z, m2_xbc, m2_dt = jnp.split(proj, list(np.cumsum(sizes)), axis=-1)

    qkv = jax.nn.silu(causal_dwconv(dn_qkv, dn_conv_w)).astype(f32)
    q, k, v = jnp.split(qkv, 3, axis=-1)
    q = q.reshape(b, l, DN_HEADS, DN_DK)
    k = k.reshape(b, l, DN_HEADS, DN_DK)
    v = v.reshape(b, l, DN_HEADS, DN_DV)
    q = q * lax.rsqrt(jnp.sum(q * q, -1, keepdims=True) + NORM_EPS) * (DN_DK ** -0.5)
    k = k * lax.rsqrt(jnp.sum(k * k, -1, keepdims=True) + NORM_EPS)
    beta = jax.nn.sigmoid(dn_b.astype(f32))
    g = -jnp.exp(dn_a_log.astype(f32)) * jax.nn.softplus(dn_a.astype(f32) + dn_dt_bias.astype(f32))
    o = gated_delta_rule(q, k, v, beta, g)
    z = dn_z.astype(f32).reshape(b, l, DN_HEADS, DN_DV)
    o = (o * lax.rsqrt(jnp.mean(o * o, -1, keepdims=True) + NORM_EPS)
         * dn_norm_w.astype(f32) * jax.nn.silu(z)).reshape(b, l, DN_W)

    xbc = jax.nn.silu(causal_dwconv(m2_xbc, m2_conv_w) + m2_conv_b.astype(h.dtype)).astype(f32)
    xs, bm, cm = jnp.split(xbc, [M2_W, M2_W + M2_GROUPS * M2_STATE], axis=-1)
    xs = xs.reshape(b, l, M2_GROUPS, M2_HPG, M2_HEAD_DIM)
    bm = bm.reshape(b, l, M2_GROUPS, M2_STATE)
    cm = cm.reshape(b, l, M2_GROUPS, M2_STATE)
    dt = jax.nn.softplus(m2_dt.astype(f32) + m2_dt_bias.astype(f32)).reshape(b, l, M2_GROUPS, M2_HPG)
    a_head = (-jnp.exp(m2_a_log.astype(f32))).reshape(M2_GROUPS, M2_HPG)
    y = ssd_chunked(xs, dt, dt * a_head, bm, cm)
    y = y + m2_d.astype(f32).reshape(M2_GROUPS, M2_HPG)[:, :, None] * xs
    y = y.reshape(b, l, M2_W) * jax.nn.silu(m2_z.astype(f32))
    y = y.reshape(b, l, M2_GROUPS, M2_W // M2_GROUPS)
    y = (y * lax.rsqrt(jnp.mean(y * y, -1, keepdims=True) + NORM_EPS)).reshape(b, l, M2_W)
    y = y * m2_norm_w.astype(f32)

    mixed = jnp.concatenate([o, y], axis=-1).astype(h.dtype)
    return mixed @ w_out


def conv_ffn(h, w_up, conv_w, w_down):
    u = causal_dwconv(h @ w_up, conv_w)
    gate, val = jnp.split(u, 2, axis=-1)
    return (jax.nn.silu(gate) * val) @ w_down


def setup_inputs(seed: int = 0) -> dict:
    key = jax.random.key(seed)
    ks = jax.random.split(key, 24)
    nrm = jax.random.normal

    def dt_bias(k, n):
        dt = jnp.exp(jax.random.uniform(k, (DEPTH, n), minval=math.log(1e-3), maxval=math.log(1e-1)))
        return dt + jnp.log(-jnp.expm1(-dt))

    def a_log(k, n):
        return jnp.log(jax.random.uniform(k, (DEPTH, n), minval=1.0, maxval=16.0))

    return {
        "x": nrm(ks[0], (BATCH, SEQ, D_MODEL), jnp.float32),
        "meta_tokens": nrm(ks[1], (N_META, D_MODEL), jnp.float32),
        "norm_mix_w": 1.0 + 0.02 * nrm(ks[2], (DEPTH, D_MODEL), jnp.float32),
        "w_in": nrm(ks[3], (DEPTH, D_MODEL, D_IN_PROJ), jnp.float32) * D_MODEL ** -0.5,
        "dn_conv_w": nrm(ks[4], (DEPTH, DN_CONV, 3 * DN_W), jnp.float32) * DN_CONV ** -0.5,
        "dn_a_log": a_log(ks[5], DN_HEADS),
        "dn_dt_bias": dt_bias(ks[6], DN_HEADS),
        "dn_norm_w": 1.0 + 0.02 * nrm(ks[7], (DEPTH, DN_DV), jnp.float32),
        "m2_conv_w": nrm(ks[8], (DEPTH, M2_CONV, M2_W + 2 * M2_GROUPS * M2_STATE), jnp.float32) * M2_CONV ** -0.5,
        "m2_conv_b": 0.02 * nrm(ks[9], (DEPTH, M2_W + 2 * M2_GROUPS * M2_STATE), jnp.float32),
        "m2_a_log": a_log(ks[10], M2_HEADS),
        "m2_dt_bias": dt_bias(ks[11], M2_HEADS),
        "m2_d": 1.0 + 0.1 * nrm(ks[12], (DEPTH, M2_HEADS), jnp.float32),
        "m2_norm_w": 1.0 + 0.02 * nrm(ks[13], (DEPTH, M2_W), jnp.float32),
        "w_out": nrm(ks[14], (DEPTH, D_MIX, D_MODEL), jnp.float32) * D_MIX ** -0.5,
        "norm_ffn_w": 1.0 + 0.02 * nrm(ks[15], (DEPTH, D_MODEL), jnp.float32),
        "ffn_up": nrm(ks[16], (DEPTH, D_MODEL, 2 * D_FF), jnp.float32) * D_MODEL ** -0.5,
        "ffn_conv_w": nrm(ks[17], (DEPTH, FFN_CONV, 2 * D_FF), jnp.float32) * FFN_CONV ** -0.5,
        "ffn_down": nrm(ks[18], (DEPTH, D_FF, D_MODEL), jnp.float32) * D_FF ** -0.5,
        "norm_final_w": 1.0 + 0.02 * nrm(ks[19], (D_MODEL,), jnp.float32),
    }


def reference(x, meta_tokens, norm_mix_w, w_in, dn_conv_w, dn_a_log, dn_dt_bias, dn_norm_w,
              m2_conv_w, m2_conv_b, m2_a_log, m2_dt_bias, m2_d, m2_norm_w, w_out,
              norm_ffn_w, ffn_up, ffn_conv_w, ffn_down, norm_final_w):
    b = x.shape[0]
    meta = jnp.broadcast_to(meta_tokens[None].astype(x.dtype), (b, N_META, D_MODEL))
    h = jnp.concatenate([meta, x], axis=1)
    for layer in range(DEPTH):
        h = h + hybrid_mixer(rms_norm(h, norm_mix_w[layer]), w_in[layer], dn_conv_w[layer],
                             dn_a_log[layer], dn_dt_bias[layer], dn_norm_w[layer],
                             m2_conv_w[layer], m2_conv_b[layer], m2_a_log[layer],
                             m2_dt_bias[layer], m2_d[layer], m2_norm_w[layer], w_out[layer])
        h = h + conv_ffn(rms_norm(h, norm_ffn_w[layer]), ffn_up[layer], ffn_conv_w[layer],
                         ffn_down[layer])
    return rms_norm(h, norm_final_w)[:, N_META:]
```

```python
import numpy as np
import concourse.bass as bass
import concourse.mybir as mybir

F32 = mybir.dt.float32
BF16 = mybir.dt.bfloat16
AF = mybir.ActivationFunctionType
ALU = mybir.AluOpType

PE, ACT, DVE, POOL, SP = "pe", "act", "dve", "pool", "sp"
ENGS = [PE, ACT, DVE, POOL, SP]


class Acc:
    __slots__ = ("op", "eng", "w", "p0", "p1", "f0", "f1")

    def __init__(self, op, eng, w, p0, p1, f0, f1):
        self.op = op; self.eng = eng; self.w = w
        self.p0 = p0; self.p1 = p1; self.f0 = f0; self.f1 = f1


def region(ap):
    sp = str(ap.space)
    if "DRAM" in sp.upper():
        return None
    t = ap.tensor
    a = ap.ap
    ps = a[0][0]
    off = ap.offset
    if ps > 0:
        p0 = off // ps
        f0 = off % ps
    else:
        p0 = 0
        f0 = off
    p1 = p0 + a[0][1]
    ext = 0
    for st, cnt in a[1:]:
        ext += abs(st) * (cnt - 1)
    f1 = f0 + ext + 1
    return (t.name, "PSUM" in sp.upper() or sp.upper().startswith("PS"), p0, p1, f0, f1)


class Sched:
    def __init__(self, nc):
        self.nc = nc
        self.ops = []
        self.hist = {}
        self.dma_count = {}

    def add(self, eng, fn, reads, writes, stream=None):
        oid = len(self.ops)
        deps = set()
        for ap, is_w in [(r, False) for r in reads] + [(w, True) for w in writes]:
            if ap is None:
                continue
            rg = region(ap)
            if rg is None:
                continue
            name, is_ps, p0, p1, f0, f1 = rg
            lst = self.hist.setdefault(name, [])
            keep = []
            for a in lst:
                if a.op == oid:
                    keep.append(a)
                    continue
                overlap = not (a.p1 <= p0 or p1 <= a.p0 or a.f1 <= f0 or f1 <= a.f0)
                if is_ps and a.eng != eng:
                    conflict = True
                else:
                    conflict = overlap and (is_w or a.w)
                if conflict and not (a.eng == PE and eng == PE):
                    deps.add(a.op)
                covered = is_w and p0 <= a.p0 and a.p1 <= p1 and f0 <= a.f0 and a.f1 <= f1
                if covered or (is_ps and a.eng != eng):
                    continue
                keep.append(a)
            keep.append(Acc(oid, eng, is_w, p0, p1, f0, f1))
            self.hist[name] = keep
        best = {}
        nd = set()
        for d in deps:
            dop = self.ops[d]
            if dop["stream"] is not None:
                nd.add(d)
            else:
                e2 = dop["eng"]
                if best.get(e2, -1) < d:
                    best[e2] = d
        nd.update(best.values())
        deps = nd
        op = dict(id=oid, eng=eng, fn=fn, deps=deps, stream=stream, sig=False, dn=None)
        if stream is not None:
            n = self.dma_count.get(stream, 0) + 1
            self.dma_count[stream] = n
            op["dn"] = n
        self.ops.append(op)
        return oid

    def emit(self, final_wait_streams):
        nc = self.nc
        ops = self.ops
        for op in ops:
            for d in op["deps"]:
                ops[d]["sig"] = True
        cnt = {e: 0 for e in ENGS}
        for op in ops:
            if op["stream"] is None and op["sig"]:
                cnt[op["eng"]] += 1
                op["sv"] = cnt[op["eng"]]
        from contextlib import ExitStack
        with ExitStack() as es:
            sems = {e: es.enter_context(nc.semaphore("sem_" + e)) for e in ENGS}
            dsems = {s: es.enter_context(nc.semaphore("dsem_" + s)) for s in self.dma_count}
            block = es.enter_context(nc.Block())
            per_eng = {e: [op for op in ops if op["eng"] == e] for e in ENGS}

            def run(eng_name, eng):
                waited = {}
                for op in per_eng[eng_name]:
                    need = {}
                    for d in op["deps"]:
                        dop = ops[d]
                        if dop["stream"] is not None:
                            key = ("d", dop["stream"]); val = 16 * dop["dn"]
                        else:
                            key = ("e", dop["eng"]); val = dop["sv"]
                        if need.get(key, 0) < val:
                            need[key] = val
                    for key, val in need.items():
                        if waited.get(key, 0) >= val:
                            continue
                        waited[key] = val
                        sem = dsems[key[1]] if key[0] == "d" else sems[key[1]]
                        eng.wait_ge(sem, val)
                    ins = op["fn"](eng)
                    if op["stream"] is not None:
                        ins.then_inc(dsems[op["stream"]], 16)
                    elif op["sig"]:
                        ins.then_inc(sems[eng_name], 1)
                if eng_name == SP:
                    for s in final_wait_streams:
                        eng.wait_ge(dsems[s], 16 * self.dma_count[s])

            block.tensor(lambda e: run(PE, e))
            block.scalar(lambda e: run(ACT, e))
            block.vector(lambda e: run(DVE, e))
            block.gpsimd(lambda e: run(POOL, e))
            block.sync(lambda e: run(SP, e))
        self.stats = {e: len(per_eng[e]) for e in ENGS}
        self.stats["sig"] = dict(cnt)

    def mm(self, out, lhsT, rhs, start=True, stop=True):
        return self.add(PE, lambda e: e.matmul(out, lhsT=lhsT, rhs=rhs, start=start, stop=stop,
                                               skip_group_check=True),
                        [lhsT, rhs], [out])

    def tr(self, out, in_, ident):
        return self.add(PE, lambda e: e.transpose(out, in_, ident), [in_, ident], [out])

    def act(self, out, in_, func, bias=None, scale=None, accum_out=None, eng=ACT):
        kw = {}
        rd = [in_]
        if bias is not None:
            kw["bias"] = bias
            if not isinstance(bias, (int, float)):
                rd.append(bias)
        if scale is not None:
            kw["scale"] = scale
            if not isinstance(scale, (int, float)):
                rd.append(scale)
        wr = [out]
        if accum_out is not None:
            kw["accum_out"] = accum_out
            wr.append(accum_out)
        return self.add(ACT, lambda e: e.activation(out, in_, func, **kw), rd, wr)

    def ts(self, eng, out, in0, s1, s2, op0, op1=None, accum_out=None):
        rd = [in0]
        if not isinstance(s1, (int, float)):
            rd.append(s1)
        if s2 is not None and not isinstance(s2, (int, float)):
            rd.append(s2)
        kw = {}
        wr = [out]
        if op1 is not None:
            kw["op1"] = op1
        if accum_out is not None:
            kw["accum_out"] = accum_out
            wr.append(accum_out)
        return self.add(eng, lambda e: e.tensor_scalar(out, in0, s1, s2, op0, **kw), rd, wr)

    def stt(self, out, in0, scalar, in1, op0, op1, eng=DVE):
        rd = [in0, in1]
        if not isinstance(scalar, (int, float)):
            rd.append(scalar)
        return self.add(eng, lambda e: e.scalar_tensor_tensor(out, in0, scalar, in1, op0, op1), rd, [out])

    def tt(self, eng, out, in0, in1, op):
        return self.add(eng, lambda e: e.tensor_tensor(out, in0, in1, op), [in0, in1], [out])

    def copy(self, eng, out, in_):
        if eng == ACT:
            return self.add(ACT, lambda e: e.copy(out, in_), [in_], [out])
        return self.add(eng, lambda e: e.tensor_copy(out, in_), [in_], [out])

    def memset(self, eng, out, val):
        return self.add(eng, lambda e: e.memset(out, val), [], [out])

    def dma(self, queue, out, in_, stream):
        return self.add(queue, lambda e: e.dma_start(out=out, in_=in_), [in_], [out], stream=stream)


import numpy as np
import concourse.bass as bass
import concourse.mybir as mybir
from concourse.bass_utils import run_bass_kernel_spmd

D = 2048
KC = 16
SEQ = 4096
NMETA = 16
DFF = 5632
EPS = 1e-6
GW = 8192

OQ, OK_, OV, OZ, OB, OA, OMZ, OXS, OBM, OCM, ODT = 0, 2048, 4096, 6144, 8192, 8208, 8224, 10272, 12320, 12832, 13344


def group_list():
    gl = []
    for h in range(16):
        gl.append(("gdn", h))
    gl.append(("mbc", 0)); gl.append(("mbc", 1))
    for g in range(4):
        gl.append(("mx", g)); gl.append(("mz", g))
    for cb in range(4):
        for kg in range(2):
            gl.append(("wo", cb, kg))
    for u in range(22):
        gl.append(("up", u))
    for cb in range(4):
        for kg in range(3):
            gl.append(("dn", cb, kg))
    return gl


GL = group_list()
GIDX = {g: i for i, g in enumerate(GL)}
NG = len(GL)

PP = {}
_o = 0
for nm, n in [("nw_mix", 16), ("nw_ffn", 16), ("m2nw", 16), ("dnnw", 1), ("cw_dn", 48 * 4), ("cw_m2", 24 * 4),
              ("cb_m2", 24), ("cw_ff", 88 * 3)]:
    PP[nm] = (_o, n); _o += n
NPP = _o
RV = {}
_o = 0
for nm, n in [("nwf", 2048), ("dn_alog", 16), ("dn_dtb", 16), ("m2_alog", 32), ("m2_dtb", 32), ("m2_d", 32)]:
    RV[nm] = (_o, n); _o += n
NRV = _o


def prep_weights(inp):
    w_in = np.asarray(inp["w_in"][0]); w_out = np.asarray(inp["w_out"][0])
    up = np.asarray(inp["ffn_up"][0]); dn = np.asarray(inp["ffn_down"][0])
    wbig = np.zeros((NG, 128, GW), np.float32)

    def put(gi, W, rows0, nk, cols):
        blk = W[rows0:rows0 + nk * 128][:, cols]
        blk = blk.reshape(nk, 128, len(cols)).transpose(1, 0, 2)
        wbig[gi, :, :nk * 512] = blk.reshape(128, nk * 512)

    ar = np.arange
    for gi, g in enumerate(GL):
        if g[0] == "gdn":
            h = g[1]
            cols = np.concatenate([OQ + h * 128 + ar(128), OK_ + h * 128 + ar(128), OV + h * 128 + ar(128), OZ + h * 128 + ar(128)])
            put(gi, w_in, 0, 16, cols)
        elif g[0] == "mbc":
            p = g[1]
            cols = np.concatenate([OBM + (2 * p) * 128 + ar(128), OCM + (2 * p) * 128 + ar(128),
                                   OBM + (2 * p + 1) * 128 + ar(128), OCM + (2 * p + 1) * 128 + ar(128)])
            put(gi, w_in, 0, 16, cols)
        elif g[0] == "mx":
            put(gi, w_in, 0, 16, OXS + g[1] * 512 + ar(512))
        elif g[0] == "mz":
            put(gi, w_in, 0, 16, OMZ + g[1] * 512 + ar(512))
        elif g[0] == "wo":
            put(gi, w_out, g[2] * 2048, 16, g[1] * 512 + ar(512))
        elif g[0] == "up":
            u = g[1]
            cols = np.concatenate([u * 256 + ar(256), DFF + u * 256 + ar(256)])
            put(gi, up, 0, 16, cols)
        elif g[0] == "dn":
            kg = g[2]
            nk = 16 if kg < 2 else 12
            put(gi, dn, kg * 2048, nk, g[1] * 512 + ar(512))
    cols = np.concatenate([OB + ar(16), OA + ar(16), ODT + ar(32)])
    wsm = w_in[:, cols].reshape(16, 128, 64).transpose(1, 0, 2).reshape(128, 16 * 64).copy()
    pp = np.zeros((128, NPP), np.float32)

    def fm(v):
        return np.asarray(v).reshape(-1, 128).T

    def setp(nm, a):
        o, n = PP[nm]; pp[:, o:o + n] = a.reshape(128, n)
    setp("nw_mix", fm(inp["norm_mix_w"][0])); setp("nw_ffn", fm(inp["norm_ffn_w"][0]))
    setp("m2nw", fm(inp["m2_norm_w"][0])); setp("dnnw", np.asarray(inp["dn_norm_w"][0]).reshape(128, 1))

    def cw(w):
        w = np.asarray(w); K, C = w.shape
        return w.reshape(K, C // 128, 128).transpose(2, 1, 0).reshape(128, -1)
    setp("cw_dn", cw(inp["dn_conv_w"][0])); setp("cw_m2", cw(inp["m2_conv_w"][0]))
    setp("cb_m2", fm(inp["m2_conv_b"][0])); setp("cw_ff", cw(inp["ffn_conv_w"][0]))
    rv = np.zeros((1, NRV), np.float32)

    def setr(nm, a):
        o, n = RV[nm]; rv[0, o:o + n] = np.asarray(a).reshape(n)
    setr("nwf", inp["norm_final_w"]); setr("dn_alog", inp["dn_a_log"][0]); setr("dn_dtb", inp["dn_dt_bias"][0])
    setr("m2_alog", inp["m2_a_log"][0]); setr("m2_dtb", inp["m2_dt_bias"][0]); setr("m2_d", inp["m2_d"][0])
    return dict(wbig=wbig, wsm=wsm, pp=pp, rv=rv, meta=np.asarray(inp["meta_tokens"], np.float32))


import math

C_RAW, C_LNB, C_BETA, C_G, C_AM, C_GC, C_ACS, C_GLAST, C_ALAST = 0, 64, 80, 96, 112, 144, 160, 192, 208
C_EGC, C_EACS, C_EGLAST, C_EALAST, C_ED, C_ED2, C_BEGE, C_DT, C_TMP = 240, 256, 288, 304, 336, 352, 384, 400, 432
C_STA, C_STB = 432, 512
NS = 576


def build(seq=SEQ, T=384, do_gdn=True, do_m2=True, NWB=2, do_ffn=True):
    ntok = NMETA + seq
    nblk = (ntok + T - 1) // T
    blocks = []
    t0 = 0
    for b in range(nblk):
        tb = min(T, ((ntok - t0 + 127) // 128) * 128)
        blocks.append((t0, tb)); t0 += tb
    nc = bass.Bass("TRN2", target_bir_lowering=False)
    x = nc.dram_tensor("x", [seq, D], F32, kind="ExternalInput").ap()
    meta = nc.dram_tensor("meta", [NMETA, D], F32, kind="ExternalInput").ap()
    wbig = nc.dram_tensor("wbig", [NG, 128, GW], F32, kind="ExternalInput").ap()
    wsm_d = nc.dram_tensor("wsm", [128, 16 * 64], F32, kind="ExternalInput").ap()
    pp_d = nc.dram_tensor("pp", [128, NPP], F32, kind="ExternalInput").ap()
    rv_d = nc.dram_tensor("rv", [1, NRV], F32, kind="ExternalInput").ap()
    out = nc.dram_tensor("out", [seq, D], F32, kind="ExternalOutput").ap()
    NT = T // 128
    from contextlib import ExitStack
    with ExitStack() as es:
        def sb(name, shape, dt=F32):
            return es.enter_context(nc.sbuf_tensor(name, shape, dt))

        def psb(name, shape, dt=F32):
            return es.enter_context(nc.psum_tensor(name, shape, dt))
        S = Sched(nc)
        ident_f = sb("ident_f", [128, 128]); ident_b = sb("ident_b", [128, 128], BF16)
        ones_f = sb("ones_f", [128, 128]); ones_b = sb("ones_b", [128, 128], BF16)
        nones2 = sb("nones2", [128, 256])
        UT_f = sb("UT_f", [128, 128])
        masks = sb("masks", [128, 256])
        pp = sb("pp_sb", [128, NPP]); rv = sb("rv_sb", [128, NRV])
        nA = sb("nA", [128, 48])
        wsm_b = sb("wsm_b", [128, 16 * 64], BF16)
        h = sb("h", [128, NT, D])
        hnT = sb("hnT", [128, KC, T], BF16)
        bigT = sb("bigT", [128, 44, T], BF16)
        wb = [sb(f"wb{i}", [128, GW], BF16) for i in range(NWB)]
        hn_tm = sb("hn_tm", [128, D], BF16)
        stat = sb("stat", [128, 32])
        tails_ff = sb("tails_ff", [128, 88, 3])
        tails_dn = sb("tails_dn", [128, 48, 3])
        tails_m2 = sb("tails_m2", [128, 24, 3])
        pre = [sb(f"pre{i}", [128, 3 + T]) for i in range(4)]
        cv = [sb(f"cv{i}", [128, T]) for i in range(4)]
        ft = [sb(f"ft{i}", [128, 512]) for i in range(6)]
        bt = [None] * 4 + [sb(f"bt{i}", [128, 512], BF16) for i in range(4, 9)]
        negmask2 = sb("negmask2", [128, 256])
        Sst = sb("Sst", [128, 16, 128]); S_b = sb("S_b", [128, 16, 128], BF16)
        stT = sb("stT", [128, 4, 512]); stT_b = sb("stT_b", [128, 4, 512], BF16)
        tokS = [sb(f"tokS{i}", [128, NS]) for i in range(NT)]
        glT = sb("glT", [32, T]); nglT = sb("nglT", [16, T]); egcT = sb("egcT", [16, T])
        acsT = sb("acsT", [32, T]); nacsT = sb("nacsT", [32, T])
        bcT = sb("bcT", [128, 8, T], BF16)
        btm = sb("btm", [128, NT, 512], BF16)
        xs_tm = sb("xs_tm", [128, NT, 512], BF16)
        zs_tm = sb("zs_tm", [128, NT, 512], BF16)
        ps = [psb(f"ps{i}", [128, 512]) for i in range(8)]
        print("sbuf remaining", nc.sbuf_bytes_remaining)

        S.memset(POOL, ones_f[:], 1.0)
        S.memset(POOL, nones2[:], -1.0)
        S.memset(POOL, ones_b[:], 1.0)

        def asel(out_ap, in_ap, cmp, base):
            S.add(POOL, lambda e: e.affine_select(out_ap, in_ap, [[1, 128]], cmp, 0.0, base=base,
                                                  channel_multiplier=-1), [in_ap], [out_ap])
        asel(ident_f[:], ones_f[:], ALU.is_equal, 0)
        asel(UT_f[:], ones_f[:], ALU.is_ge, 0)
        S.copy(POOL, masks[:, 0:128], UT_f[:])
        asel(masks[:, 128:256], nones2[:, 0:128], ALU.is_ge, -1)
        S.copy(POOL, ident_b[:], ident_f[:])
        S.ts(DVE, negmask2[:, 0:128], masks[:, 0:128], -1.0, 30000.0, ALU.add, ALU.mult)
        S.ts(DVE, negmask2[:, 128:256], masks[:, 128:256], 1.0, -30000.0, ALU.add, ALU.mult)
        S.dma(SP, pp[:], pp_d[:, :], "pp")
        S.dma(SP, rv[:], rv_d[0:1, :].to_broadcast([128, NRV]), "rv")
        S.dma(POOL, wsm_b[:], wsm_d[:, :], "wsm")
        for t_ in (tails_ff, tails_dn, tails_m2, Sst, S_b, stT, stT_b):
            S.memset(POOL, t_[:], 0.0)

        def ppc(nm, i=0, n=1):
            o, _ = PP[nm]
            return pp[:, o + i:o + i + n]

        def rvc(nm):
            o, n = RV[nm]
            return rv[:, o:o + n]
        ncb = sb("ncb", [128, 24])
        S.ts(DVE, ncb[:], ppc("cb_m2", 0, 24), -1.0, None, ALU.mult)
        S.act(nA[:, 0:16], rvc("dn_alog"), AF.Exp)
        S.act(nA[:, 16:48], rvc("m2_alog"), AF.Exp)
        S.ts(DVE, nA[:], nA[:], -1.0, None, ALU.mult)

        wstate = dict(next=0)
        total_groups = []

        def plan_groups(bi):
            return [g for g in GL if (g[0] in ("up", "dn") and do_ffn) or (g[0] == "wo" and (do_gdn or do_m2))
                    or (g[0] == "gdn" and do_gdn) or (g[0] in ("mbc", "mx", "mz") and do_m2)]
        for bi in range(nblk):
            total_groups += plan_groups(bi)

        def issue_w(n):
            if n >= len(total_groups):
                return
            g = total_groups[n]
            nk = 12 if (g[0] == "dn" and g[2] == 2) else 16
            S.dma(POOL, wb[n % NWB][:, 0:nk * 512], wbig[GIDX[g], :, 0:nk * 512], f"w{n % NWB}")

        for n in range(NWB - 1):
            issue_w(n)

        def next_w(expect):
            n = wstate["next"]
            assert total_groups[n][0] == expect, (total_groups[n], expect)
            issue_w(n + NWB - 1)
            wstate["next"] = n + 1
            return wb[n % NWB][:].rearrange("p (k c) -> p k c", k=16)

        def rms_stats(src, tt, scale):
            c0 = 3 * tt
            S.act(hn_tm[:, 0:src.shape[1]], src, AF.Square, accum_out=stat[:, c0:c0 + 1])
            S.act(stat[:, c0 + 1:c0 + 2], stat[:, c0:c0 + 1], AF.Ln, bias=EPS, scale=scale)
            S.act(stat[:, c0 + 2:c0 + 3], stat[:, c0 + 1:c0 + 2], AF.Exp, scale=-0.5)
            return stat[:, c0 + 2:c0 + 3]

        def rmsnorm_to_T(nw_name, tb):
            ntl = tb // 128
            for tt in range(ntl):
                r = rms_stats(h[:, tt, :], tt, 1.0 / D)
                S.ts(DVE, hn_tm[:], h[:, tt, :], r, None, ALU.mult)
                for kq in range(4):
                    pv = ps[6 + (kq % 2)][:].bitcast(BF16)
                    for j in range(4):
                        kc = kq * 4 + j
                        S.tr(pv[:, j * 128:(j + 1) * 128], hn_tm[:, kc * 128:(kc + 1) * 128], ident_b[:])
                    for j in range(4):
                        kc = kq * 4 + j
                        dst = hnT[:, kc, tt * 128:(tt + 1) * 128]
                        if j % 2 == 0:
                            S.ts(DVE, dst, pv[:, j * 128:(j + 1) * 128], ppc(nw_name, kc), None, ALU.mult)
                        else:
                            S.act(dst, pv[:, j * 128:(j + 1) * 128], AF.Copy, scale=ppc(nw_name, kc))

        def silu_exp(out, src, tmp, bias=None, nbias=None):
            if bias is None:
                S.act(tmp, src, AF.Exp, scale=-1.0)
            else:
                S.act(tmp, src, AF.Exp, scale=-1.0, bias=nbias)
            S.act(tmp, tmp, AF.Ln, bias=1.0)
            S.act(tmp, tmp, AF.Exp, scale=-1.0)
            if bias is None:
                S.tt(DVE, out, src, tmp, ALU.mult)
            else:
                S.stt(out, src, bias, tmp, ALU.add, ALU.mult)

        def conv_chunk(psrc, tb, pr, c, tails, ch, cwname, K):
            S.copy(POOL, pr[:, 0:3], tails[:, ch, :])
            S.act(pr[:, 3:3 + tb], psrc, AF.Copy)
            S.copy(POOL, tails[:, ch, :], pr[:, tb:tb + 3])
            o, _ = PP[cwname]
            b0 = 4 - K
            S.ts(DVE, c[:, 0:tb], pr[:, b0:b0 + tb], pp[:, o + ch * K:o + ch * K + 1], None, ALU.mult)
            for j in range(1, K):
                S.stt(c[:, 0:tb], pr[:, b0 + j:b0 + j + tb], pp[:, o + ch * K + j:o + ch * K + j + 1], c[:, 0:tb],
                      ALU.mult, ALU.add)

        def small_proj(tb):
            ntl = tb // 128
            for tt in range(ntl):
                tk = tokS[tt]
                cs = slice(tt * 128, (tt + 1) * 128)
                for kc in range(16):
                    S.mm(ps[7][:, 0:64], hnT[:, kc, cs], wsm_b[:, kc * 64:(kc + 1) * 64], start=(kc == 0), stop=(kc == 15))
                S.copy(ACT, tk[:, 0:64], ps[7][:, 0:64])
                tmp = tk[:, C_TMP:C_TMP + 96]
                S.act(tmp[:, 0:16], tk[:, 0:16], AF.Exp, scale=-1.0)
                S.tt(DVE, tmp[:, 16:32], tk[:, 16:32], rvc("dn_dtb"), ALU.add)
                S.tt(DVE, tmp[:, 32:64], tk[:, 32:64], rvc("m2_dtb"), ALU.add)
                S.act(tmp[:, 16:64], tmp[:, 16:64], AF.Exp)
                S.act(tmp[:, 0:64], tmp[:, 0:64], AF.Ln, bias=1.0)
                S.ts(DVE, tk[:, C_LNB:C_LNB + 16], tmp[:, 0:16], -1.0, None, ALU.mult)
                S.act(tk[:, C_BETA:C_BETA + 16], tk[:, C_LNB:C_LNB + 16], AF.Exp)
                S.tt(DVE, tk[:, C_G:C_G + 16], tmp[:, 16:32], nA[:, 0:16], ALU.mult)
                S.copy(DVE, tk[:, C_DT:C_DT + 32], tmp[:, 32:64])
                S.tt(DVE, tk[:, C_AM:C_AM + 32], tmp[:, 32:64], nA[:, 16:48], ALU.mult)
                S.mm(ps[7][:, 64:112], UT_f[:], tk[:, C_G:C_G + 48])
                S.mm(ps[7][:, 112:160], ones_f[:], tk[:, C_G:C_G + 48])
                S.copy(ACT, tk[:, C_GC:C_GC + 96], ps[7][:, 64:160])
                S.act(tk[:, C_EGC:C_EGC + 96], tk[:, C_GC:C_GC + 96], AF.Exp)
                S.tt(DVE, tmp[:, 0:48], tk[:, C_GLAST:C_GLAST + 48], tk[:, C_GC:C_GC + 48], ALU.subtract)
                S.act(tk[:, C_ED:C_ED + 48], tmp[:, 0:48], AF.Exp)
                S.tt(DVE, tk[:, C_BEGE:C_BEGE + 16], tk[:, C_BETA:C_BETA + 16], tk[:, C_EGC:C_EGC + 16], ALU.mult)
                stA = tk[:, C_STA:C_STA + 80]; stB = tk[:, C_STB:C_STB + 64]
                S.copy(DVE, stA[:, 0:16], tk[:, C_GC:C_GC + 16])
                S.tt(DVE, stA[:, 16:32], tk[:, C_GC:C_GC + 16], tk[:, C_LNB:C_LNB + 16], ALU.add)
                S.ts(DVE, stA[:, 32:48], tk[:, C_GC:C_GC + 16], -1.0, None, ALU.mult)
                S.memset(DVE, stA[:, 48:64], 0.0)
                S.copy(DVE, stA[:, 64:80], tk[:, C_EGC:C_EGC + 16])
                S.copy(DVE, stB[:, 0:32], tk[:, C_ACS:C_ACS + 32])
                S.ts(DVE, stB[:, 32:64], tk[:, C_ACS:C_ACS + 32], -1.0, None, ALU.mult)
                for (dst, src, n) in [(glT, stA[:, 0:32], 32), (nglT, stA[:, 32:48], 16), (egcT, stA[:, 64:80], 16),
                                      (acsT, stB[:, 0:32], 32), (nacsT, stB[:, 32:64], 32)]:
                    S.tr(ps[7][0:n, 0:128], src, ident_f[:])
                    S.copy(ACT, dst[0:n, cs], ps[7][0:n, 0:128])

        def run_rr(gens):
            gens = list(gens)
            while gens:
                nxt = []
                for g_ in gens:
                    try:
                        next(g_); nxt.append(g_)
                    except StopIteration:
                        pass
                gens = nxt

        def slot128(i):
            if i < 12:
                return btm[:, i // 4, (i % 4) * 128:(i % 4 + 1) * 128]
            i -= 12
            return zs_tm[:, i // 4, (i % 4) * 128:(i % 4 + 1) * 128]

        def xp_tile(c, i):
            k = 3 * c + i
            return bcT[:, k, 0:384] if k < 8 else xs_tm[:, 0, 0:384]

        def gdn_prep_block():
            for c in range(3):
                S.copy(POOL, xp_tile(c, 0)[:, 0:128], ident_b[:])

        def gset(sidx):
            b0 = 32 + 4 * sidx
            return dict(qn=bigT[:, b0, :], kn=bigT[:, b0 + 1, :], v=bigT[:, b0 + 2, :], Qg=bigT[:, b0 + 3, :],
                        zsil=(cv[3] if sidx == 0 else ft[1]))

        def gdn_stage1(hd, tb):
            G = gset(hd % 2)
            wv = next_w("gdn")
            PA, PBk = ps[6], ps[7]
            sq_b = bigT[:, 40, :]
            tmpf = ft[0]

            def proj(i, bank):
                for kc in range(16):
                    S.mm(bank[:, 0:tb], wv[:, kc, i * 128:(i + 1) * 128], hnT[:, kc, 0:tb], start=(kc == 0), stop=(kc == 15))
                    if kc % 4 == 3:
                        yield

            def convsilu(i, bank):
                conv_chunk(bank[:, 0:tb], tb, pre[i], cv[i], tails_dn, i * 16 + hd, "cw_dn", 4)
                yield
                silu_exp(cv[i][:, 0:tb], cv[i][:, 0:tb], tmpf[:, 0:tb])
                yield
            yield from proj(0, PA)
            yield from proj(1, PBk)
            yield from convsilu(0, PA)
            yield from proj(2, PA)
            yield from convsilu(1, PBk)
            yield from proj(3, PBk)
            yield from convsilu(2, PA)
            S.copy(ACT, ft[4][:, 0:tb], PBk[:, 0:tb]) if False else None
            silu_exp(G["zsil"][:, 0:tb], PBk[:, 0:tb], tmpf[:, 0:tb])
            S.mm(PA[:, 0:tb], ident_f[0:16, hd:hd + 1].to_broadcast([16, 128]), egcT[0:16, 0:tb])
            yield
            for i, dstb in enumerate([G["qn"], G["kn"]]):
                S.act(sq_b[:, 0:tb], cv[i][:, 0:tb], AF.Square)
                S.mm(PBk[:, 0:tb], ones_b[:], sq_b[:, 0:tb])
                yield
                S.act(tmpf[:, 0:tb], PBk[:, 0:tb], AF.Ln, bias=EPS)
                S.act(tmpf[:, 0:tb], tmpf[:, 0:tb], AF.Exp, scale=-0.5, bias=(math.log(128 ** -0.5) if i == 0 else 0.0))
                yield
                S.tt(DVE, dstb[:, 0:tb], cv[i][:, 0:tb], tmpf[:, 0:tb], ALU.mult)
                yield
            S.tt(DVE, G["Qg"][:, 0:tb], G["qn"][:, 0:tb], PA[:, 0:tb], ALU.mult)
            S.copy(ACT, G["v"][:, 0:tb], cv[2][:, 0:tb])
            yield

        def gdn_stage23(hd, tb):
            ntl = tb // 128
            G = gset(hd % 2)
            qn_b, kn_b, v_b, Qg_b, zsil = G["qn"], G["kn"], G["v"], G["Qg"], G["zsil"]
            oT = ps[3]
            Ecls = [ft[2][:, 0:256], ft[2][:, 256:512], ft[3][:, 0:256]]

            def bufs(c):
                b0 = 8 * c
                return dict(Kb=slot128(b0), Kd=slot128(b0 + 1), Vb=slot128(b0 + 2), QKm=slot128(b0 + 3),
                            nWT=slot128(b0 + 4), TT=slot128(b0 + 5), Em0=slot128(b0 + 6), Em1=slot128(b0 + 7))

            def chainA(c):
                tt = c
                cs = slice(tt * 128, (tt + 1) * 128)
                tk = tokS[tt]
                B = bufs(c)
                bank = ps[c]
                trb = bank[:].bitcast(BF16)
                ev = ACT if (c % 2 == 0) else DVE

                def col(o):
                    return tk[:, o + hd:o + hd + 1]
                S.tr(trb[:, 768:896], kn_b[:, cs], ident_b[:])
                S.tr(trb[:, 896:1024], v_b[:, cs], ident_b[:])
                S.ts(DVE, B["Kb"], trb[:, 768:896], col(C_BEGE), None, ALU.mult)
                S.act(B["Kd"], trb[:, 768:896], AF.Copy, scale=col(C_ED))
                S.ts(DVE, B["Vb"], trb[:, 896:1024], col(C_BETA), None, ALU.mult)
                yield
                S.mm(bank[:, 0:128], kn_b[:, cs], qn_b[:, cs])
                S.mm(bank[:, 128:256], kn_b[:, cs], kn_b[:, cs])
                S.mm(bank[:, 256:512], nglT[0:16, cs], ident_f[0:16, hd:hd + 1].to_broadcast([16, 256]), start=True, stop=False)
                S.mm(bank[:, 256:384], ident_f[0:32, hd:hd + 1].to_broadcast([32, 128]), glT[0:32, cs], start=False, stop=False)
                S.mm(bank[:, 384:512], ident_f[0:32, 16 + hd:17 + hd].to_broadcast([32, 128]), glT[0:32, cs], start=False, stop=True)
                yield
                S.tt(DVE, Ecls[c], bank[:, 256:512], negmask2[:], ALU.min)
                yield
                S.act(B["Em0"], Ecls[c][:, 0:128], AF.Exp)
                S.act(B["Em1"], Ecls[c][:, 128:256], AF.Exp)
                yield
                XP = xp_tile(c, 0)
                S.tt(DVE, B["QKm"], bank[:, 0:128], B["Em0"], ALU.mult)
                S.stt(XP[:, 128:256], bank[:, 128:256], -1.0, B["Em1"], ALU.mult, ALU.mult)
                yield
                S.tr(trb[:, 0:128], XP[:, 128:256], ident_b[:])
                S.copy(ev, XP[:, 256:384], trb[:, 0:128])
                yield
                for j in range(7):
                    last = (j == 6)
                    if not last:
                        XPn = xp_tile(c, 1 + (j % 2))
                        S.mm(bank[:, 0:256], XP[:, 256:384], XP[:, 0:256], start=True, stop=False)
                        S.mm(bank[:, 0:128], ident_b[:], XP[:, 0:128], start=False, stop=True)
                        S.mm(bank[:, 256:384], XP[:, 128:256], XP[:, 256:384], start=True, stop=True)
                        S.copy(ev, XPn, bank[:, 0:384])
                        XP = XPn
                    else:
                        S.mm(bank[:, 0:128], XP[:, 256:384], XP[:, 0:128], start=True, stop=False)
                        S.mm(bank[:, 0:128], ident_b[:], XP[:, 0:128], start=False, stop=True)
                        S.copy(ev, B["TT"], bank[:, 0:128])
                    yield
                S.mm(bank[:, 0:128], B["Kb"], B["TT"])
                S.act(B["nWT"], bank[:, 0:128], AF.Copy, scale=-1.0)
                yield

            gens = [chainA(c) for c in range(ntl)]
            while gens:
                nxt = []
                for g_ in gens:
                    try:
                        next(g_); nxt.append(g_)
                    except StopIteration:
                        pass
                gens = nxt
                yield
            vnew_b = slot128(23)
            for tt in range(ntl):
                cs = slice(tt * 128, (tt + 1) * 128)
                tk = tokS[tt]
                B = bufs(tt)
                S.mm(ps[4][:, 0:128], B["TT"], B["Vb"], start=True, stop=False)
                S.mm(ps[4][:, 0:128], B["nWT"], S_b[:, hd, :], start=False, stop=True)
                S.copy(ACT, vnew_b, ps[4][:, 0:128])
                yield
                S.mm(oT[:, cs], S_b[:, hd, :], Qg_b[:, cs], start=True, stop=False)
                S.mm(oT[:, cs], vnew_b, B["QKm"], start=False, stop=True)
                S.mm(ps[5][:, 0:128], B["Kd"], vnew_b)
                yield
                S.stt(Sst[:, hd, :], Sst[:, hd, :], tk[:, C_EGLAST + hd:C_EGLAST + hd + 1], ps[5][:, 0:128], ALU.mult, ALU.add)
                S.copy(ACT, S_b[:, hd, :], Sst[:, hd, :])
                yield
            sq3 = bigT[:, 41, :]
            tmpf3, tmpf2 = ft[4], ft[5]
            S.act(sq3[:, 0:tb], oT[:, 0:tb], AF.Square)
            S.mm(ps[4][:, 0:tb], ones_b[:], sq3[:, 0:tb])
            yield
            S.act(tmpf3[:, 0:tb], ps[4][:, 0:tb], AF.Ln, bias=EPS, scale=1.0 / 128)
            S.act(tmpf3[:, 0:tb], tmpf3[:, 0:tb], AF.Exp, scale=-0.5)
            yield
            S.tt(DVE, tmpf2[:, 0:tb], oT[:, 0:tb], tmpf3[:, 0:tb], ALU.mult)
            S.stt(bigT[:, hd, 0:tb], tmpf2[:, 0:tb], ppc("dnnw"), zsil[:, 0:tb], ALU.mult, ALU.mult)
            yield

        def gdn_all(tb):
            gdn_prep_block()
            run_rr([gdn_stage1(0, tb)])
            for hd in range(16):
                gens = [gdn_stage23(hd, tb)]
                if hd + 1 < 16:
                    gens.append(gdn_stage1(hd + 1, tb))
                run_rr(gens)

        def mamba(tb):
            ntl = tb // 128
            for p in range(2):
                wv = next_w("mbc")
                for i in range(4):
                    for kc in range(16):
                        S.mm(ps[i][:, 0:tb], wv[:, kc, i * 128:(i + 1) * 128], hnT[:, kc, 0:tb], start=(kc == 0), stop=(kc == 15))
                for i in range(4):
                    g = 2 * p + i // 2
                    isC = i % 2
                    ch = (20 if isC else 16) + g
                    conv_chunk(ps[i][:, 0:tb], tb, pre[i], cv[i], tails_m2, ch, "cw_m2", 4)
                    silu_exp(bcT[:, 2 * g + isC, 0:tb], cv[i][:, 0:tb], ft[0][:, 0:tb], bias=ppc("cb_m2", ch), nbias=ncb[:, ch:ch + 1])
            for tt in range(ntl):
                cs = slice(tt * 128, (tt + 1) * 128)
                trb = ps[4][:].bitcast(BF16)
                for g in range(4):
                    S.tr(trb[:, g * 128:(g + 1) * 128], bcT[:, 2 * g, cs], ident_b[:])
                S.copy(ACT, btm[:, tt, :], trb[:, 0:512])
            Ecl4, y1, y2 = ft[0], ft[1], ft[2]
            cbs = ft[3][:, 0:128]
            xdt_b, xdtd_b, MT4_b, y_b, Eex4 = bt[4], bt[5], bt[6], bt[7], bt[8]
            for g in range(4):
                wv = next_w("mx")
                for i in range(4):
                    for kc in range(16):
                        S.mm(ps[i][:, 0:tb], wv[:, kc, i * 128:(i + 1) * 128], hnT[:, kc, 0:tb], start=(kc == 0), stop=(kc == 15))
                for i in range(4):
                    ch = 4 * g + i
                    conv_chunk(ps[i][:, 0:tb], tb, pre[i], cv[i], tails_m2, ch, "cw_m2", 4)
                    silu_exp(bigT[:, 32 + i, 0:tb], cv[i][:, 0:tb], ft[0][:, 0:tb], bias=ppc("cb_m2", ch), nbias=ncb[:, ch:ch + 1])
                for tt in range(ntl):
                    cs = slice(tt * 128, (tt + 1) * 128)
                    trb = ps[4][:].bitcast(BF16)
                    for i in range(4):
                        S.tr(trb[:, i * 128:(i + 1) * 128], bigT[:, 32 + i, cs], ident_b[:])
                    S.copy(ACT, xs_tm[:, tt, :], trb[:, 0:512])
                wv = next_w("mz")
                for tt in range(ntl):
                    cs = slice(tt * 128, (tt + 1) * 128)
                    for kc in range(16):
                        S.mm(ps[5][:, :], hnT[:, kc, cs], wv[:, kc, :], start=(kc == 0), stop=(kc == 15))
                    silu_exp(zs_tm[:, tt, :], ps[5][:, :], ft[3][:, :])
                for tt in range(ntl):
                    cs = slice(tt * 128, (tt + 1) * 128)
                    tk = tokS[tt]

                    def hb(o):
                        return tk[:, o + 8 * g:o + 8 * g + 8].unsqueeze(2).to_broadcast([128, 8, 64])

                    def v3(ap):
                        return ap.rearrange("p (h c) -> p h c", h=8)
                    xv = v3(xs_tm[:, tt, :])
                    S.tt(POOL, v3(xdt_b[:]), xv, hb(C_DT), ALU.mult)
                    S.tt(POOL, v3(xdtd_b[:]), v3(xdt_b[:]), hb(C_ED2), ALU.mult)
                    S.mm(ps[6][:, 0:128], bcT[:, 2 * g, cs], bcT[:, 2 * g + 1, cs])
                    S.copy(ACT, cbs, ps[6][:, 0:128])
                    for hq in range(2):
                        h0 = 8 * g + 4 * hq
                        S.mm(ps[7][:, :], nacsT[0:32, cs],
                             ident_f[0:32, h0:h0 + 4].unsqueeze(2).to_broadcast([32, 4, 128]), start=True, stop=False)
                        for i in range(4):
                            S.mm(ps[7][:, i * 128:(i + 1) * 128], ident_f[0:32, h0 + i:h0 + i + 1].to_broadcast([32, 128]),
                                 acsT[0:32, cs], start=False, stop=(i == 3))
                        S.tt(DVE, Ecl4[:].rearrange("p (i l) -> p i l", i=4), ps[7][:, :].rearrange("p (i l) -> p i l", i=4),
                             negmask2[:, 0:128].unsqueeze(1).to_broadcast([128, 4, 128]), ALU.min)
                        S.act(Eex4[:], Ecl4[:], AF.Exp)
                        M4 = MT4_b[:].rearrange("p (i l) -> p i l", i=4)
                        S.tt(DVE, M4, Eex4[:].rearrange("p (i l) -> p i l", i=4),
                             cbs.unsqueeze(1).to_broadcast([128, 4, 128]), ALU.mult)
                        for i in range(4):
                            h8 = 4 * hq + i
                            S.mm(ps[0][:, h8 * 64:(h8 + 1) * 64], M4[:, i, :], xdt_b[:, h8 * 64:(h8 + 1) * 64], start=True, stop=True)
                    S.mm(ps[1][:, :], bcT[:, 2 * g + 1, cs], stT_b[:, g, :])
                    S.tt(DVE, v3(y1[:]), v3(ps[1][:, :]), hb(C_EACS), ALU.mult)
                    S.tt(DVE, y1[:], y1[:], ps[0][:, :], ALU.add)
                    o_d, _ = RV["m2_d"]
                    dbc = rv[:, o_d + 8 * g:o_d + 8 * g + 8].unsqueeze(2).to_broadcast([128, 8, 64])
                    S.tt(POOL, v3(y2[:]), xv, dbc, ALU.mult)
                    S.tt(POOL, y2[:], y2[:], y1[:], ALU.add)
                    S.tt(DVE, y2[:], y2[:], zs_tm[:, tt, :], ALU.mult)
                    r = rms_stats(y2[:], 4, 1.0 / 512)
                    S.act(y_b[:], y2[:], AF.Copy, scale=r)
                    trb = ps[2][:].bitcast(BF16)
                    for i in range(4):
                        S.tr(trb[:, i * 128:(i + 1) * 128], y_b[:, i * 128:(i + 1) * 128], ident_b[:])
                    for i in range(4):
                        dst = bigT[:, 16 + 4 * g + i, cs]
                        if i % 2 == 0:
                            S.ts(DVE, dst, trb[:, i * 128:(i + 1) * 128], ppc("m2nw", 4 * g + i), None, ALU.mult)
                        else:
                            S.act(dst, trb[:, i * 128:(i + 1) * 128], AF.Copy, scale=ppc("m2nw", 4 * g + i))
                    S.mm(ps[3][:, :], btm[:, tt, g * 128:(g + 1) * 128], xdtd_b[:])
                    S.tt(DVE, v3(stT[:, g, :]), v3(stT[:, g, :]), hb(C_EALAST), ALU.mult)
                    S.tt(DVE, stT[:, g, :], stT[:, g, :], ps[3][:, :], ALU.add)
                    S.copy(ACT, stT_b[:, g, :], stT[:, g, :])

        store_streams = []
        for bi, (tok0, tb) in enumerate(blocks):
            ntl = tb // 128
            for tt in range(ntl):
                a0 = tok0 + tt * 128
                r0 = a0 - NMETA
                lo, hi = max(r0, 0), min(r0 + 128, seq)
                if a0 == 0:
                    S.dma(SP, h[0:NMETA, tt, :], meta[:, :], f"h{tt}")
                    S.dma(SP, h[NMETA:128, tt, :], x[0:128 - NMETA, :], f"h{tt}")
                else:
                    if hi - lo < 128:
                        S.memset(POOL, h[:, tt, :], 0.0)
                    if hi > lo:
                        S.dma(SP, h[0:hi - lo, tt, :], x[lo:hi, :], f"h{tt}")
            if do_gdn or do_m2:
                rmsnorm_to_T("nw_mix", tb)
                small_proj(tb)
                if do_gdn:
                    gdn_all(tb)
                else:
                    S.memset(POOL, bigT[:, 0:16, :], 0.0)
                if do_m2:
                    mamba(tb)
                else:
                    S.memset(POOL, bigT[:, 16:32, :], 0.0)
                for cb in range(4):
                    for kg in range(2):
                        wv = next_w("wo")
                        for tt in range(ntl):
                            for k in range(16):
                                kc = kg * 16 + k
                                S.mm(ps[tt][:, :], bigT[:, kc, tt * 128:(tt + 1) * 128], wv[:, k, :], start=(kc == 0), stop=(kc == 31))
                    for tt in range(ntl):
                        S.tt(DVE, h[:, tt, cb * 512:(cb + 1) * 512], h[:, tt, cb * 512:(cb + 1) * 512], ps[tt][:, :], ALU.add)
            if do_ffn:
                rmsnorm_to_T("nw_ffn", tb)
                for u in range(22):
                    wv = next_w("up")
                    for half in range(2):
                        f_g = 2 * u + half
                        for which in range(2):
                            idx = half * 2 + which
                            pst = ps[idx]
                            co = which * 256 + half * 128
                            for kc in range(16):
                                S.mm(pst[:, 0:tb], wv[:, kc, co:co + 128], hnT[:, kc, 0:tb], start=(kc == 0), stop=(kc == 15))
                            conv_chunk(pst[:, 0:tb], tb, pre[idx], cv[idx], tails_ff, f_g + which * 44, "cw_ff", 3)
                        cg, cvv = cv[half * 2], cv[half * 2 + 1]
                        S.act(cg[:, 0:tb], cg[:, 0:tb], AF.Silu)
                        S.tt(DVE, bigT[:, f_g, 0:tb], cg[:, 0:tb], cvv[:, 0:tb], ALU.mult)
                for cb in range(4):
                    for kg in range(3):
                        wv = next_w("dn")
                        nk = 16 if kg < 2 else 12
                        for tt in range(ntl):
                            for k in range(nk):
                                kc = kg * 16 + k
                                S.mm(ps[tt][:, :], bigT[:, kc, tt * 128:(tt + 1) * 128], wv[:, k, :], start=(kc == 0), stop=(kc == 43))
                    for tt in range(ntl):
                        S.tt(DVE, h[:, tt, cb * 512:(cb + 1) * 512], h[:, tt, cb * 512:(cb + 1) * 512], ps[tt][:, :], ALU.add)
            for tt in range(ntl):
                r = rms_stats(h[:, tt, :], tt, 1.0 / D)
                S.stt(h[:, tt, :], h[:, tt, :], r, rvc("nwf"), ALU.mult, ALU.mult)
                a0 = tok0 + tt * 128
                r0 = a0 - NMETA
                lo, hi = max(r0, 0), min(r0 + 128, seq)
                if hi > lo:
                    S.dma(SP, out[lo:hi, :], h[lo - r0:hi - r0, tt, :], f"o{tt}")
                    if f"o{tt}" not in store_streams:
                        store_streams.append(f"o{tt}")
        S.emit(store_streams)
        print("ops", S.stats)
    return nc


_NC_CACHE = {}


def kernel(**inputs):
    inp = {k: np.asarray(v) for k, v in inputs.items()}
    x = inp["x"]
    B = x.shape[0]
    W = prep_weights(inp)
    if "nc" not in _NC_CACHE:
        _NC_CACHE["nc"] = build(seq=SEQ)
    nc = _NC_CACHE["nc"]
    in_maps = []
    for b in range(B):
        in_maps.append(dict(x=np.ascontiguousarray(x[b], dtype=np.float32), meta=W["meta"], wbig=W["wbig"],
                            wsm=W["wsm"], pp=W["pp"], rv=W["rv"]))
    res = run_bass_kernel_spmd(nc, in_maps, core_ids=list(range(B)))
    return np.stack([np.asarray(r["out"], dtype=np.float32) for r in res.results], axis=0)
```

```python
import numpy as np
import concourse.bass as bass
import concourse.mybir as mybir

F32 = mybir.dt.float32
BF16 = mybir.dt.bfloat16
AF = mybir.ActivationFunctionType
ALU = mybir.AluOpType

PE, ACT, DVE, POOL, SP = "pe", "act", "dve", "pool", "sp"
ENGS = [PE, ACT, DVE, POOL, SP]


class Acc:
    __slots__ = ("op", "eng", "w", "p0", "p1", "f0", "f1")

    def __init__(self, op, eng, w, p0, p1, f0, f1):
        self.op = op; self.eng = eng; self.w = w
        self.p0 = p0; self.p1 = p1; self.f0 = f0; self.f1 = f1


def region(ap):
    sp = str(ap.space)
    if "DRAM" in sp.upper():
        return None
    t = ap.tensor
    a = ap.ap
    ps = a[0][0]
    off = ap.offset
    if ps > 0:
        p0 = off // ps
        f0 = off % ps
    else:
        p0 = 0
        f0 = off
    p1 = p0 + a[0][1]
    ext = 0
    for st, cnt in a[1:]:
        ext += abs(st) * (cnt - 1)
    f1 = f0 + ext + 1
    return (t.name, "PSUM" in sp.upper() or sp.upper().startswith("PS"), p0, p1, f0, f1)


class Sched:
    def __init__(self, nc):
        self.nc = nc
        self.ops = []
        self.hist = {}
        self.dma_count = {}

    def add(self, eng, fn, reads, writes, stream=None):
        oid = len(self.ops)
        deps = set()
        for ap, is_w in [(r, False) for r in reads] + [(w, True) for w in writes]:
            if ap is None:
                continue
            rg = region(ap)
            if rg is None:
                continue
            name, is_ps, p0, p1, f0, f1 = rg
            lst = self.hist.setdefault(name, [])
            keep = []
            for a in lst:
                if a.op == oid:
                    keep.append(a)
                    continue
                overlap = not (a.p1 <= p0 or p1 <= a.p0 or a.f1 <= f0 or f1 <= a.f0)
                if is_ps and a.eng != eng:
                    conflict = True
                else:
                    conflict = overlap and (is_w or a.w)
                if conflict and not (a.eng == PE and eng == PE):
                    deps.add(a.op)
                covered = is_w and p0 <= a.p0 and a.p1 <= p1 and f0 <= a.f0 and a.f1 <= f1
                if covered or (is_ps and a.eng != eng):
                    continue
                keep.append(a)
            keep.append(Acc(oid, eng, is_w, p0, p1, f0, f1))
            self.hist[name] = keep
        best = {}
        nd = set()
        for d in deps:
            dop = self.ops[d]
            if dop["stream"] is not None:
                nd.add(d)
            else:
                e2 = dop["eng"]
                if best.get(e2, -1) < d:
                    best[e2] = d
        nd.update(best.values())
        deps = nd
        op = dict(id=oid, eng=eng, fn=fn, deps=deps, stream=stream, sig=False, dn=None)
        if stream is not None:
            n = self.dma_count.get(stream, 0) + 1
            self.dma_count[stream] = n
            op["dn"] = n
        self.ops.append(op)
        return oid

    def emit(self, final_wait_streams):
        nc = self.nc
        ops = self.ops
        for op in ops:
            for d in op["deps"]:
                ops[d]["sig"] = True
        cnt = {e: 0 for e in ENGS}
        for op in ops:
            if op["stream"] is None and op["sig"]:
                cnt[op["eng"]] += 1
                op["sv"] = cnt[op["eng"]]
        from contextlib import ExitStack
        with ExitStack() as es:
            sems = {e: es.enter_context(nc.semaphore("sem_" + e)) for e in ENGS}
            dsems = {s: es.enter_context(nc.semaphore("dsem_" + s)) for s in self.dma_count}
            block = es.enter_context(nc.Block())
            per_eng = {e: [op for op in ops if op["eng"] == e] for e in ENGS}

            def run(eng_name, eng):
                waited = {}
                for op in per_eng[eng_name]:
                    need = {}
                    for d in op["deps"]:
                        dop = ops[d]
                        if dop["stream"] is not None:
                            key = ("d", dop["stream"]); val = 16 * dop["dn"]
                        else:
                            key = ("e", dop["eng"]); val = dop["sv"]
                        if need.get(key, 0) < val:
                            need[key] = val
                    for key, val in need.items():
                        if waited.get(key, 0) >= val:
                            continue
                        waited[key] = val
                        sem = dsems[key[1]] if key[0] == "d" else sems[key[1]]
                        eng.wait_ge(sem, val)
                    ins = op["fn"](eng)
                    if op["stream"] is not None:
                        ins.then_inc(dsems[op["stream"]], 16)
                    elif op["sig"]:
                        ins.then_inc(sems[eng_name], 1)
                if eng_name == SP:
                    for s in final_wait_streams:
                        eng.wait_ge(dsems[s], 16 * self.dma_count[s])

            block.tensor(lambda e: run(PE, e))
            block.scalar(lambda e: run(ACT, e))
            block.vector(lambda e: run(DVE, e))
            block.gpsimd(lambda e: run(POOL, e))
            block.sync(lambda e: run(SP, e))
        self.stats = {e: len(per_eng[e]) for e in ENGS}
        self.stats["sig"] = dict(cnt)

    def mm(self, out, lhsT, rhs, start=True, stop=True):
        return self.add(PE, lambda e: e.matmul(out, lhsT=lhsT, rhs=rhs, start=start, stop=stop,
                                               skip_group_check=True),
                        [lhsT, rhs], [out])

    def tr(self, out, in_, ident):
        return self.add(PE, lambda e: e.transpose(out, in_, ident), [in_, ident], [out])

    def act(self, out, in_, func, bias=None, scale=None, accum_out=None, eng=ACT):
        kw = {}
        rd = [in_]
        if bias is not None:
            kw["bias"] = bias
            if not isinstance(bias, (int, float)):
                rd.append(bias)
        if scale is not None:
            kw["scale"] = scale
            if not isinstance(scale, (int, float)):
                rd.append(scale)
        wr = [out]
        if accum_out is not None:
            kw["accum_out"] = accum_out
            wr.append(accum_out)
        return self.add(ACT, lambda e: e.activation(out, in_, func, **kw), rd, wr)

    def ts(self, eng, out, in0, s1, s2, op0, op1=None, accum_out=None):
        rd = [in0]
        if not isinstance(s1, (int, float)):
            rd.append(s1)
        if s2 is not None and not isinstance(s2, (int, float)):
            rd.append(s2)
        kw = {}
        wr = [out]
        if op1 is not None:
            kw["op1"] = op1
        if accum_out is not None:
            kw["accum_out"] = accum_out
            wr.append(accum_out)
        return self.add(eng, lambda e: e.tensor_scalar(out, in0, s1, s2, op0, **kw), rd, wr)

    def stt(self, out, in0, scalar, in1, op0, op1, eng=DVE):
        rd = [in0, in1]
        if not isinstance(scalar, (int, float)):
            rd.append(scalar)
        return self.add(eng, lambda e: e.scalar_tensor_tensor(out, in0, scalar, in1, op0, op1), rd, [out])

    def tt(self, eng, out, in0, in1, op):
        return self.add(eng, lambda e: e.tensor_tensor(out, in0, in1, op), [in0, in1], [out])

    def copy(self, eng, out, in_):
        if eng == ACT:
            return self.add(ACT, lambda e: e.copy(out, in_), [in_], [out])
        return self.add(eng, lambda e: e.tensor_copy(out, in_), [in_], [out])

    def memset(self, eng, out, val):
        return self.add(eng, lambda e: e.memset(out, val), [], [out])

    def dma(self, queue, out, in_, stream):
        return self.add(queue, lambda e: e.dma_start(out=out, in_=in_), [in_], [out], stream=stream)


import numpy as np
import concourse.bass as bass
import concourse.mybir as mybir
from concourse.bass_utils import run_bass_kernel_spmd

D = 2048
KC = 16
SEQ = 4096
NMETA = 16
DFF = 5632
EPS = 1e-6
GW = 8192

OQ, OK_, OV, OZ, OB, OA, OMZ, OXS, OBM, OCM, ODT = 0, 2048, 4096, 6144, 8192, 8208, 8224, 10272, 12320, 12832, 13344


def group_list():
    gl = []
    for h in range(16):
        gl.append(("gdn", h))
    gl.append(("mbc", 0)); gl.append(("mbc", 1))
    for g in range(4):
        gl.append(("mx", g)); gl.append(("mz", g))
    for cb in range(4):
        for kg in range(2):
            gl.append(("wo", cb, kg))
    for u in range(22):
        gl.append(("up", u))
    for cb in range(4):
        for kg in range(3):
            gl.append(("dn", cb, kg))
    return gl


GL = group_list()
GIDX = {g: i for i, g in enumerate(GL)}
NG = len(GL)

PP = {}
_o = 0
for nm, n in [("nw_mix", 16), ("nw_ffn", 16), ("m2nw", 16), ("dnnw", 1), ("cw_dn", 48 * 4), ("cw_m2", 24 * 4),
              ("cb_m2", 24), ("cw_ff", 88 * 3)]:
    PP[nm] = (_o, n); _o += n
NPP = _o
RV = {}
_o = 0
for nm, n in [("nwf", 2048), ("dn_alog", 16), ("dn_dtb", 16), ("m2_alog", 32), ("m2_dtb", 32), ("m2_d", 32)]:
    RV[nm] = (_o, n); _o += n
NRV = _o


def prep_weights(inp):
    w_in = np.asarray(inp["w_in"][0]); w_out = np.asarray(inp["w_out"][0])
    up = np.asarray(inp["ffn_up"][0]); dn = np.asarray(inp["ffn_down"][0])
    wbig = np.zeros((NG, 128, GW), np.float32)

    def put(gi, W, rows0, nk, cols):
        blk = W[rows0:rows0 + nk * 128][:, cols]
        blk = blk.reshape(nk, 128, len(cols)).transpose(1, 0, 2)
        wbig[gi, :, :nk * 512] = blk.reshape(128, nk * 512)

    ar = np.arange
    for gi, g in enumerate(GL):
        if g[0] == "gdn":
            h = g[1]
            cols = np.concatenate([OQ + h * 128 + ar(128), OK_ + h * 128 + ar(128), OV + h * 128 + ar(128), OZ + h * 128 + ar(128)])
            put(gi, w_in, 0, 16, cols)
        elif g[0] == "mbc":
            p = g[1]
            cols = np.concatenate([OBM + (2 * p) * 128 + ar(128), OCM + (2 * p) * 128 + ar(128),
                                   OBM + (2 * p + 1) * 128 + ar(128), OCM + (2 * p + 1) * 128 + ar(128)])
            put(gi, w_in, 0, 16, cols)
        elif g[0] == "mx":
            put(gi, w_in, 0, 16, OXS + g[1] * 512 + ar(512))
        elif g[0] == "mz":
            put(gi, w_in, 0, 16, OMZ + g[1] * 512 + ar(512))
        elif g[0] == "wo":
            put(gi, w_out, g[2] * 2048, 16, g[1] * 512 + ar(512))
        elif g[0] == "up":
            u = g[1]
            cols = np.concatenate([u * 256 + ar(256), DFF + u * 256 + ar(256)])
            put(gi, up, 0, 16, cols)
        elif g[0] == "dn":
            kg = g[2]
            nk = 16 if kg < 2 else 12
            put(gi, dn, kg * 2048, nk, g[1] * 512 + ar(512))
    cols = np.concatenate([OB + ar(16), OA + ar(16), ODT + ar(32)])
    wsm = w_in[:, cols].reshape(16, 128, 64).transpose(1, 0, 2).reshape(128, 16 * 64).copy()
    pp = np.zeros((128, NPP), np.float32)

    def fm(v):
        return np.asarray(v).reshape(-1, 128).T

    def setp(nm, a):
        o, n = PP[nm]; pp[:, o:o + n] = a.reshape(128, n)
    setp("nw_mix", fm(inp["norm_mix_w"][0])); setp("nw_ffn", fm(inp["norm_ffn_w"][0]))
    setp("m2nw", fm(inp["m2_norm_w"][0])); setp("dnnw", np.asarray(inp["dn_norm_w"][0]).reshape(128, 1))

    def cw(w):
        w = np.asarray(w); K, C = w.shape
        return w.reshape(K, C // 128, 128).transpose(2, 1, 0).reshape(128, -1)
    setp("cw_dn", cw(inp["dn_conv_w"][0])); setp("cw_m2", cw(inp["m2_conv_w"][0]))
    setp("cb_m2", fm(inp["m2_conv_b"][0])); setp("cw_ff", cw(inp["ffn_conv_w"][0]))
    rv = np.zeros((1, NRV), np.float32)

    def setr(nm, a):
        o, n = RV[nm]; rv[0, o:o + n] = np.asarray(a).reshape(n)
    setr("nwf", inp["norm_final_w"]); setr("dn_alog", inp["dn_a_log"][0]); setr("dn_dtb", inp["dn_dt_bias"][0])
    setr("m2_alog", inp["m2_a_log"][0]); setr("m2_dtb", inp["m2_dt_bias"][0]); setr("m2_d", inp["m2_d"][0])
    return dict(wbig=wbig, wsm=wsm, pp=pp, rv=rv, meta=np.asarray(inp["meta_tokens"], np.float32))


import math

C_RAW, C_LNB, C_BETA, C_G, C_AM, C_GC, C_ACS, C_GLAST, C_ALAST = 0, 64, 80, 96, 112, 144, 160, 192, 208
C_EGC, C_EACS, C_EGLAST, C_EALAST, C_ED, C_ED2, C_BEGE, C_DT, C_TMP = 240, 256, 288, 304, 336, 352, 384, 400, 432
C_STA, C_STB = 432, 512
NS = 576


def build(seq=SEQ, T=384, do_gdn=True, do_m2=True, NWB=2, do_ffn=True):
    ntok = NMETA + seq
    nblk = (ntok + T - 1) // T
    blocks = []
    t0 = 0
    for b in range(nblk):
        tb = min(T, ((ntok - t0 + 127) // 128) * 128)
        blocks.append((t0, tb)); t0 += tb
    nc = bass.Bass("TRN2", target_bir_lowering=False)
    x = nc.dram_tensor("x", [seq, D], F32, kind="ExternalInput").ap()
    meta = nc.dram_tensor("meta", [NMETA, D], F32, kind="ExternalInput").ap()
    wbig = nc.dram_tensor("wbig", [NG, 128, GW], F32, kind="ExternalInput").ap()
    wsm_d = nc.dram_tensor("wsm", [128, 16 * 64], F32, kind="ExternalInput").ap()
    pp_d = nc.dram_tensor("pp", [128, NPP], F32, kind="ExternalInput").ap()
    rv_d = nc.dram_tensor("rv", [1, NRV], F32, kind="ExternalInput").ap()
    out = nc.dram_tensor("out", [seq, D], F32, kind="ExternalOutput").ap()
    NT = T // 128
    from contextlib import ExitStack
    with ExitStack() as es:
        def sb(name, shape, dt=F32):
            return es.enter_context(nc.sbuf_tensor(name, shape, dt))

        def psb(name, shape, dt=F32):
            return es.enter_context(nc.psum_tensor(name, shape, dt))
        S = Sched(nc)
        ident_f = sb("ident_f", [128, 128]); ident_b = sb("ident_b", [128, 128], BF16)
        ones_f = sb("ones_f", [128, 128]); ones_b = sb("ones_b", [128, 128], BF16)
        nones2 = sb("nones2", [128, 256])
        UT_f = sb("UT_f", [128, 128])
        masks = sb("masks", [128, 256])
        pp = sb("pp_sb", [128, NPP]); rv = sb("rv_sb", [128, NRV])
        nA = sb("nA", [128, 48])
        wsm_b = sb("wsm_b", [128, 16 * 64], BF16)
        h = sb("h", [128, NT, D])
        hnT = sb("hnT", [128, KC, T], BF16)
        bigT = sb("bigT", [128, 44, T], BF16)
        wb = [sb(f"wb{i}", [128, GW], BF16) for i in range(NWB)]
        hn_tm = sb("hn_tm", [128, D], BF16)
        stat = sb("stat", [128, 32])
        tails_ff = sb("tails_ff", [128, 88, 3])
        tails_dn = sb("tails_dn", [128, 48, 3])
        tails_m2 = sb("tails_m2", [128, 24, 3])
        pre = [sb(f"pre{i}", [128, 3 + T]) for i in range(4)]
        cv = [sb(f"cv{i}", [128, T]) for i in range(4)]
        ft = [sb(f"ft{i}", [128, 512]) for i in range(6)]
        bt = [None] * 4 + [sb(f"bt{i}", [128, 512], BF16) for i in range(4, 9)]
        negmask2 = sb("negmask2", [128, 256])
        Sst = sb("Sst", [128, 16, 128]); S_b = sb("S_b", [128, 16, 128], BF16)
        stT = sb("stT", [128, 4, 512]); stT_b = sb("stT_b", [128, 4, 512], BF16)
        tokS = [sb(f"tokS{i}", [128, NS]) for i in range(NT)]
        glT = sb("glT", [32, T]); nglT = sb("nglT", [16, T]); egcT = sb("egcT", [16, T])
        acsT = sb("acsT", [32, T]); nacsT = sb("nacsT", [32, T])
        bcT = sb("bcT", [128, 8, T], BF16)
        btm = sb("btm", [128, NT, 512], BF16)
        xs_tm = sb("xs_tm", [128, NT, 512], BF16)
        zs_tm = sb("zs_tm", [128, NT, 512], BF16)
        ps = [psb(f"ps{i}", [128, 512]) for i in range(8)]
        print("sbuf remaining", nc.sbuf_bytes_remaining)

        S.memset(POOL, ones_f[:], 1.0)
        S.memset(POOL, nones2[:], -1.0)
        S.memset(POOL, ones_b[:], 1.0)

        def asel(out_ap, in_ap, cmp, base):
            S.add(POOL, lambda e: e.affine_select(out_ap, in_ap, [[1, 128]], cmp, 0.0, base=base,
                                                  channel_multiplier=-1), [in_ap], [out_ap])
        asel(ident_f[:], ones_f[:], ALU.is_equal, 0)
        asel(UT_f[:], ones_f[:], ALU.is_ge, 0)
        S.copy(POOL, masks[:, 0:128], UT_f[:])
        asel(masks[:, 128:256], nones2[:, 0:128], ALU.is_ge, -1)
        S.copy(POOL, ident_b[:], ident_f[:])
        S.ts(DVE, negmask2[:, 0:128], masks[:, 0:128], -1.0, 30000.0, ALU.add, ALU.mult)
        S.ts(DVE, negmask2[:, 128:256], masks[:, 128:256], 1.0, -30000.0, ALU.add, ALU.mult)
        S.dma(SP, pp[:], pp_d[:, :], "pp")
        S.dma(SP, rv[:], rv_d[0:1, :].to_broadcast([128, NRV]), "rv")
        S.dma(POOL, wsm_b[:], wsm_d[:, :], "wsm")
        for t_ in (tails_ff, tails_dn, tails_m2, Sst, S_b, stT, stT_b):
            S.memset(POOL, t_[:], 0.0)

        def ppc(nm, i=0, n=1):
            o, _ = PP[nm]
            return pp[:, o + i:o + i + n]

        def rvc(nm):
            o, n = RV[nm]
            return rv[:, o:o + n]
        ncb = sb("ncb", [128, 24])
        S.ts(DVE, ncb[:], ppc("cb_m2", 0, 24), -1.0, None, ALU.mult)
        S.act(nA[:, 0:16], rvc("dn_alog"), AF.Exp)
        S.act(nA[:, 16:48], rvc("m2_alog"), AF.Exp)
        S.ts(DVE, nA[:], nA[:], -1.0, None, ALU.mult)

        wstate = dict(next=0)
        total_groups = []

        def plan_groups(bi):
            return [g for g in GL if (g[0] in ("up", "dn") and do_ffn) or (g[0] == "wo" and (do_gdn or do_m2))
                    or (g[0] == "gdn" and do_gdn) or (g[0] in ("mbc", "mx", "mz") and do_m2)]
        for bi in range(nblk):
            total_groups += plan_groups(bi)

        def issue_w(n):
            if n >= len(total_groups):
                return
            g = total_groups[n]
            nk = 12 if (g[0] == "dn" and g[2] == 2) else 16
            S.dma(POOL, wb[n % NWB][:, 0:nk * 512], wbig[GIDX[g], :, 0:nk * 512], f"w{n % NWB}")

        for n in range(NWB - 1):
            issue_w(n)

        def next_w(expect):
            n = wstate["next"]
            assert total_groups[n][0] == expect, (total_groups[n], expect)
            issue_w(n + NWB - 1)
            wstate["next"] = n + 1
            return wb[n % NWB][:].rearrange("p (k c) -> p k c", k=16)

        def rms_stats(src, tt, scale):
            c0 = 3 * tt
            S.act(hn_tm[:, 0:src.shape[1]], src, AF.Square, accum_out=stat[:, c0:c0 + 1])
            S.act(stat[:, c0 + 1:c0 + 2], stat[:, c0:c0 + 1], AF.Ln, bias=EPS, scale=scale)
            S.act(stat[:, c0 + 2:c0 + 3], stat[:, c0 + 1:c0 + 2], AF.Exp, scale=-0.5)
            return stat[:, c0 + 2:c0 + 3]

        def rmsnorm_to_T(nw_name, tb):
            ntl = tb // 128
            for tt in range(ntl):
                r = rms_stats(h[:, tt, :], tt, 1.0 / D)
                S.ts(DVE, hn_tm[:], h[:, tt, :], r, None, ALU.mult)
                for kq in range(4):
                    pv = ps[6 + (kq % 2)][:].bitcast(BF16)
                    for j in range(4):
                        kc = kq * 4 + j
                        S.tr(pv[:, j * 128:(j + 1) * 128], hn_tm[:, kc * 128:(kc + 1) * 128], ident_b[:])
                    for j in range(4):
                        kc = kq * 4 + j
                        dst = hnT[:, kc, tt * 128:(tt + 1) * 128]
                        if j % 2 == 0:
                            S.ts(DVE, dst, pv[:, j * 128:(j + 1) * 128], ppc(nw_name, kc), None, ALU.mult)
                        else:
                            S.act(dst, pv[:, j * 128:(j + 1) * 128], AF.Copy, scale=ppc(nw_name, kc))

        def silu_exp(out, src, tmp, bias=None, nbias=None):
            if bias is None:
                S.act(tmp, src, AF.Exp, scale=-1.0)
            else:
                S.act(tmp, src, AF.Exp, scale=-1.0, bias=nbias)
            S.act(tmp, tmp, AF.Ln, bias=1.0)
            S.act(tmp, tmp, AF.Exp, scale=-1.0)
            if bias is None:
                S.tt(DVE, out, src, tmp, ALU.mult)
            else:
                S.stt(out, src, bias, tmp, ALU.add, ALU.mult)

        def conv_chunk(psrc, tb, pr, c, tails, ch, cwname, K):
            S.copy(POOL, pr[:, 0:3], tails[:, ch, :])
            S.act(pr[:, 3:3 + tb], psrc, AF.Copy)
            S.copy(POOL, tails[:, ch, :], pr[:, tb:tb + 3])
            o, _ = PP[cwname]
            b0 = 4 - K
            S.ts(DVE, c[:, 0:tb], pr[:, b0:b0 + tb], pp[:, o + ch * K:o + ch * K + 1], None, ALU.mult)
            for j in range(1, K):
                S.stt(c[:, 0:tb], pr[:, b0 + j:b0 + j + tb], pp[:, o + ch * K + j:o + ch * K + j + 1], c[:, 0:tb],
                      ALU.mult, ALU.add)

        def small_proj(tb):
            ntl = tb // 128
            for tt in range(ntl):
                tk = tokS[tt]
                cs = slice(tt * 128, (tt + 1) * 128)
                for kc in range(16):
                    S.mm(ps[7][:, 0:64], hnT[:, kc, cs], wsm_b[:, kc * 64:(kc + 1) * 64], start=(kc == 0), stop=(kc == 15))
                S.copy(ACT, tk[:, 0:64], ps[7][:, 0:64])
                tmp = tk[:, C_TMP:C_TMP + 96]
                S.act(tmp[:, 0:16], tk[:, 0:16], AF.Exp, scale=-1.0)
                S.tt(DVE, tmp[:, 16:32], tk[:, 16:32], rvc("dn_dtb"), ALU.add)
                S.tt(DVE, tmp[:, 32:64], tk[:, 32:64], rvc("m2_dtb"), ALU.add)
                S.act(tmp[:, 16:64], tmp[:, 16:64], AF.Exp)
                S.act(tmp[:, 0:64], tmp[:, 0:64], AF.Ln, bias=1.0)
                S.ts(DVE, tk[:, C_LNB:C_LNB + 16], tmp[:, 0:16], -1.0, None, ALU.mult)
                S.act(tk[:, C_BETA:C_BETA + 16], tk[:, C_LNB:C_LNB + 16], AF.Exp)
                S.tt(DVE, tk[:, C_G:C_G + 16], tmp[:, 16:32], nA[:, 0:16], ALU.mult)
                S.copy(DVE, tk[:, C_DT:C_DT + 32], tmp[:, 32:64])
                S.tt(DVE, tk[:, C_AM:C_AM + 32], tmp[:, 32:64], nA[:, 16:48], ALU.mult)
                S.mm(ps[7][:, 64:112], UT_f[:], tk[:, C_G:C_G + 48])
                S.mm(ps[7][:, 112:160], ones_f[:], tk[:, C_G:C_G + 48])
                S.copy(ACT, tk[:, C_GC:C_GC + 96], ps[7][:, 64:160])
                S.act(tk[:, C_EGC:C_EGC + 96], tk[:, C_GC:C_GC + 96], AF.Exp)
                S.tt(DVE, tmp[:, 0:48], tk[:, C_GLAST:C_GLAST + 48], tk[:, C_GC:C_GC + 48], ALU.subtract)
                S.act(tk[:, C_ED:C_ED + 48], tmp[:, 0:48], AF.Exp)
                S.tt(DVE, tk[:, C_BEGE:C_BEGE + 16], tk[:, C_BETA:C_BETA + 16], tk[:, C_EGC:C_EGC + 16], ALU.mult)
                stA = tk[:, C_STA:C_STA + 80]; stB = tk[:, C_STB:C_STB + 64]
                S.copy(DVE, stA[:, 0:16], tk[:, C_GC:C_GC + 16])
                S.tt(DVE, stA[:, 16:32], tk[:, C_GC:C_GC + 16], tk[:, C_LNB:C_LNB + 16], ALU.add)
                S.ts(DVE, stA[:, 32:48], tk[:, C_GC:C_GC + 16], -1.0, None, ALU.mult)
                S.memset(DVE, stA[:, 48:64], 0.0)
                S.copy(DVE, stA[:, 64:80], tk[:, C_EGC:C_EGC + 16])
                S.copy(DVE, stB[:, 0:32], tk[:, C_ACS:C_ACS + 32])
                S.ts(DVE, stB[:, 32:64], tk[:, C_ACS:C_ACS + 32], -1.0, None, ALU.mult)
                for (dst, src, n) in [(glT, stA[:, 0:32], 32), (nglT, stA[:, 32:48], 16), (egcT, stA[:, 64:80], 16),
                                      (acsT, stB[:, 0:32], 32), (nacsT, stB[:, 32:64], 32)]:
                    S.tr(ps[7][0:n, 0:128], src, ident_f[:])
                    S.copy(ACT, dst[0:n, cs], ps[7][0:n, 0:128])

        def run_rr(gens):
            gens = list(gens)
            while gens:
                nxt = []
                for g_ in gens:
                    try:
                        next(g_); nxt.append(g_)
                    except StopIteration:
                        pass
                gens = nxt

        def slot128(i):
            if i < 12:
                return btm[:, i // 4, (i % 4) * 128:(i % 4 + 1) * 128]
            i -= 12
            return zs_tm[:, i // 4, (i % 4) * 128:(i % 4 + 1) * 128]

        def xp_tile(c, i):
            k = 3 * c + i
            return bcT[:, k, 0:384] if k < 8 else xs_tm[:, 0, 0:384]

        def gdn_prep_block():
            for c in range(3):
                S.copy(POOL, xp_tile(c, 0)[:, 0:128], ident_b[:])

        def gset(sidx):
            b0 = 32 + 4 * sidx
            return dict(qn=bigT[:, b0, :], kn=bigT[:, b0 + 1, :], v=bigT[:, b0 + 2, :], Qg=bigT[:, b0 + 3, :],
                        zsil=bt[6 + sidx])

        def cslot(i):
            return bigT[:, 16 + i // 3, (i % 3) * 128:(i % 3 + 1) * 128]

        def cbufs(c, pset):
            b0 = 24 * pset + 8 * c
            return dict(Kb=cslot(b0), Kd=cslot(b0 + 1), Vb=cslot(b0 + 2), QKm=cslot(b0 + 3),
                        nWT=cslot(b0 + 4), TT=cslot(b0 + 5), Em0=cslot(b0 + 6), Em1=cslot(b0 + 7))

        def gdn_stage1(hd, tb):
            G = gset(hd % 3)
            wv = next_w("gdn")
            PA, PBk = ps[6], ps[7]
            sq_b = bt[4]
            tmpf = ft[0]

            def proj(i, bank):
                for kc in range(16):
                    S.mm(bank[:, 0:tb], wv[:, kc, i * 128:(i + 1) * 128], hnT[:, kc, 0:tb], start=(kc == 0), stop=(kc == 15))
                    if kc % 8 == 7:
                        yield

            def convsilu(i, bank):
                conv_chunk(bank[:, 0:tb], tb, pre[i], cv[i], tails_dn, i * 16 + hd, "cw_dn", 4)
                yield
                silu_exp(cv[i][:, 0:tb], cv[i][:, 0:tb], tmpf[:, 0:tb])
                yield
            yield from proj(0, PA)
            yield from proj(1, PBk)
            yield from convsilu(0, PA)
            yield from proj(2, PA)
            yield from convsilu(1, PBk)
            yield from proj(3, PBk)
            yield from convsilu(2, PA)
            S.copy(ACT, ft[4][:, 0:tb], PBk[:, 0:tb]) if False else None
            silu_exp(G["zsil"][:, 0:tb], PBk[:, 0:tb], tmpf[:, 0:tb])
            S.mm(PA[:, 0:tb], ident_f[0:16, hd:hd + 1].to_broadcast([16, 128]), egcT[0:16, 0:tb])
            yield
            for i, dstb in enumerate([G["qn"], G["kn"]]):
                S.act(sq_b[:, 0:tb], cv[i][:, 0:tb], AF.Square)
                S.mm(PBk[:, 0:tb], ones_b[:], sq_b[:, 0:tb])
                yield
                S.act(tmpf[:, 0:tb], PBk[:, 0:tb], AF.Ln, bias=EPS)
                S.act(tmpf[:, 0:tb], tmpf[:, 0:tb], AF.Exp, scale=-0.5, bias=(math.log(128 ** -0.5) if i == 0 else 0.0))
                yield
                S.tt(DVE, dstb[:, 0:tb], cv[i][:, 0:tb], tmpf[:, 0:tb], ALU.mult)
                yield
            S.tt(DVE, G["Qg"][:, 0:tb], G["qn"][:, 0:tb], PA[:, 0:tb], ALU.mult)
            S.copy(ACT, G["v"][:, 0:tb], cv[2][:, 0:tb])
            yield

        def gdn_stage2(hd, tb):
            ntl = tb // 128
            G = gset(hd % 3)
            qn_b, kn_b, v_b = G["qn"], G["kn"], G["v"]
            Ecls = [ft[2][:, 0:256], ft[2][:, 256:512], ft[3][:, 0:256]]

            def bufs(c):
                return cbufs(c, hd % 2)

            def chainA(c):
                tt = c
                cs = slice(tt * 128, (tt + 1) * 128)
                tk = tokS[tt]
                B = bufs(c)
                bank = ps[c]
                trb = bank[:].bitcast(BF16)
                ev = ACT if (c % 2 == 0) else DVE

                def col(o):
                    return tk[:, o + hd:o + hd + 1]
                S.tr(trb[:, 768:896], kn_b[:, cs], ident_b[:])
                S.tr(trb[:, 896:1024], v_b[:, cs], ident_b[:])
                S.ts(DVE, B["Kb"], trb[:, 768:896], col(C_BEGE), None, ALU.mult)
                S.act(B["Kd"], trb[:, 768:896], AF.Copy, scale=col(C_ED))
                S.ts(DVE, B["Vb"], trb[:, 896:1024], col(C_BETA), None, ALU.mult)
                yield
                S.mm(bank[:, 0:128], kn_b[:, cs], qn_b[:, cs])
                S.mm(bank[:, 128:256], kn_b[:, cs], kn_b[:, cs])
                S.mm(bank[:, 256:512], nglT[0:16, cs], ident_f[0:16, hd:hd + 1].to_broadcast([16, 256]), start=True, stop=False)
                S.mm(bank[:, 256:384], ident_f[0:32, hd:hd + 1].to_broadcast([32, 128]), glT[0:32, cs], start=False, stop=False)
                S.mm(bank[:, 384:512], ident_f[0:32, 16 + hd:17 + hd].to_broadcast([32, 128]), glT[0:32, cs], start=False, stop=True)
                yield
                S.tt(DVE, Ecls[c], bank[:, 256:512], negmask2[:], ALU.min)
                yield
                S.act(B["Em0"], Ecls[c][:, 0:128], AF.Exp)
                S.act(B["Em1"], Ecls[c][:, 128:256], AF.Exp)
                yield
                XP = xp_tile(c, 0)
                S.tt(DVE, B["QKm"], bank[:, 0:128], B["Em0"], ALU.mult)
                S.stt(XP[:, 128:256], bank[:, 128:256], -1.0, B["Em1"], ALU.mult, ALU.mult)
                yield
                S.tr(trb[:, 0:128], XP[:, 128:256], ident_b[:])
                S.copy(ev, XP[:, 256:384], trb[:, 0:128])
                yield
                for j in range(7):
                    last = (j == 6)
                    if not last:
                        XPn = xp_tile(c, 1 + (j % 2))
                        S.mm(bank[:, 0:256], XP[:, 256:384], XP[:, 0:256], start=True, stop=False)
                        S.mm(bank[:, 0:128], ident_b[:], XP[:, 0:128], start=False, stop=True)
                        S.mm(bank[:, 256:384], XP[:, 128:256], XP[:, 256:384], start=True, stop=True)
                        S.copy(ev, XPn, bank[:, 0:384])
                        XP = XPn
                    else:
                        S.mm(bank[:, 0:128], XP[:, 256:384], XP[:, 0:128], start=True, stop=False)
                        S.mm(bank[:, 0:128], ident_b[:], XP[:, 0:128], start=False, stop=True)
                        S.copy(ev, B["TT"], bank[:, 0:128])
                    yield
                S.mm(bank[:, 0:128], B["Kb"], B["TT"])
                S.act(B["nWT"], bank[:, 0:128], AF.Copy, scale=-1.0)
                yield

            gens = [chainA(c) for c in range(ntl)]
            while gens:
                nxt = []
                for g_ in gens:
                    try:
                        next(g_); nxt.append(g_)
                    except StopIteration:
                        pass
                gens = nxt
                yield

        def gdn_stage3(hd, tb):
            ntl = tb // 128
            G = gset(hd % 3)
            Qg_b, zsil = G["Qg"], G["zsil"]
            oT = ps[3]
            vnew_b = btm[:, 0, 0:128]
            for tt in range(ntl):
                cs = slice(tt * 128, (tt + 1) * 128)
                tk = tokS[tt]
                B = cbufs(tt, hd % 2)
                S.mm(ps[4][:, 0:128], B["TT"], B["Vb"], start=True, stop=False)
                S.mm(ps[4][:, 0:128], B["nWT"], S_b[:, hd, :], start=False, stop=True)
                S.copy(ACT, vnew_b, ps[4][:, 0:128])
                yield
                S.mm(oT[:, cs], S_b[:, hd, :], Qg_b[:, cs], start=True, stop=False)
                S.mm(oT[:, cs], vnew_b, B["QKm"], start=False, stop=True)
                S.mm(ps[5][:, 0:128], B["Kd"], vnew_b)
                yield
                S.stt(Sst[:, hd, :], Sst[:, hd, :], tk[:, C_EGLAST + hd:C_EGLAST + hd + 1], ps[5][:, 0:128], ALU.mult, ALU.add)
                S.copy(ACT, S_b[:, hd, :], Sst[:, hd, :])
                yield
            sq3 = bt[5]
            tmpf3, tmpf2 = ft[4], ft[5]
            S.act(sq3[:, 0:tb], oT[:, 0:tb], AF.Square)
            S.mm(ps[4][:, 0:tb], ones_b[:], sq3[:, 0:tb])
            yield
            S.act(tmpf3[:, 0:tb], ps[4][:, 0:tb], AF.Ln, bias=EPS, scale=1.0 / 128)
            S.act(tmpf3[:, 0:tb], tmpf3[:, 0:tb], AF.Exp, scale=-0.5)
            yield
            S.tt(DVE, tmpf2[:, 0:tb], oT[:, 0:tb], tmpf3[:, 0:tb], ALU.mult)
            S.stt(bigT[:, hd, 0:tb], tmpf2[:, 0:tb], ppc("dnnw"), zsil[:, 0:tb], ALU.mult, ALU.mult)
            yield

        def gdn_all(tb):
            gdn_prep_block()
            for t in range(16 + 2):
                gens = []
                if 0 <= t - 2 < 16:
                    gens.append(gdn_stage3(t - 2, tb))
                if 0 <= t - 1 < 16:
                    gens.append(gdn_stage2(t - 1, tb))
                if t < 16:
                    gens.append(gdn_stage1(t, tb))
                run_rr(gens)

        def mamba(tb):
            ntl = tb // 128
            for p in range(2):
                wv = next_w("mbc")
                for i in range(4):
                    for kc in range(16):
                        S.mm(ps[i][:, 0:tb], wv[:, kc, i * 128:(i + 1) * 128], hnT[:, kc, 0:tb], start=(kc == 0), stop=(kc == 15))
                for i in range(4):
                    g = 2 * p + i // 2
                    isC = i % 2
                    ch = (20 if isC else 16) + g
                    conv_chunk(ps[i][:, 0:tb], tb, pre[i], cv[i], tails_m2, ch, "cw_m2", 4)
                    silu_exp(bcT[:, 2 * g + isC, 0:tb], cv[i][:, 0:tb], ft[0][:, 0:tb], bias=ppc("cb_m2", ch), nbias=ncb[:, ch:ch + 1])
            for tt in range(ntl):
                cs = slice(tt * 128, (tt + 1) * 128)
                trb = ps[4][:].bitcast(BF16)
                for g in range(4):
                    S.tr(trb[:, g * 128:(g + 1) * 128], bcT[:, 2 * g, cs], ident_b[:])
                S.copy(ACT, btm[:, tt, :], trb[:, 0:512])
            Ecl4, y1, y2 = ft[0], ft[1], ft[2]
            cbs = ft[3][:, 0:128]
            xdt_b, xdtd_b, MT4_b, y_b, Eex4 = bt[4], bt[5], bt[6], bt[7], bt[8]
            for g in range(4):
                wv = next_w("mx")
                for i in range(4):
                    for kc in range(16):
                        S.mm(ps[i][:, 0:tb], wv[:, kc, i * 128:(i + 1) * 128], hnT[:, kc, 0:tb], start=(kc == 0), stop=(kc == 15))
                for i in range(4):
                    ch = 4 * g + i
                    conv_chunk(ps[i][:, 0:tb], tb, pre[i], cv[i], tails_m2, ch, "cw_m2", 4)
                    silu_exp(bigT[:, 32 + i, 0:tb], cv[i][:, 0:tb], ft[0][:, 0:tb], bias=ppc("cb_m2", ch), nbias=ncb[:, ch:ch + 1])
                for tt in range(ntl):
                    cs = slice(tt * 128, (tt + 1) * 128)
                    trb = ps[4][:].bitcast(BF16)
                    for i in range(4):
                        S.tr(trb[:, i * 128:(i + 1) * 128], bigT[:, 32 + i, cs], ident_b[:])
                    S.copy(ACT, xs_tm[:, tt, :], trb[:, 0:512])
                wv = next_w("mz")
                for tt in range(ntl):
                    cs = slice(tt * 128, (tt + 1) * 128)
                    for kc in range(16):
                        S.mm(ps[5][:, :], hnT[:, kc, cs], wv[:, kc, :], start=(kc == 0), stop=(kc == 15))
                    silu_exp(zs_tm[:, tt, :], ps[5][:, :], ft[3][:, :])
                for tt in range(ntl):
                    cs = slice(tt * 128, (tt + 1) * 128)
                    tk = tokS[tt]

                    def hb(o):
                        return tk[:, o + 8 * g:o + 8 * g + 8].unsqueeze(2).to_broadcast([128, 8, 64])

                    def v3(ap):
                        return ap.rearrange("p (h c) -> p h c", h=8)
                    xv = v3(xs_tm[:, tt, :])
                    S.tt(POOL, v3(xdt_b[:]), xv, hb(C_DT), ALU.mult)
                    S.tt(POOL, v3(xdtd_b[:]), v3(xdt_b[:]), hb(C_ED2), ALU.mult)
                    S.mm(ps[6][:, 0:128], bcT[:, 2 * g, cs], bcT[:, 2 * g + 1, cs])
                    S.copy(ACT, cbs, ps[6][:, 0:128])
                    for hq in range(2):
                        h0 = 8 * g + 4 * hq
                        S.mm(ps[7][:, :], nacsT[0:32, cs],
                             ident_f[0:32, h0:h0 + 4].unsqueeze(2).to_broadcast([32, 4, 128]), start=True, stop=False)
                        for i in range(4):
                            S.mm(ps[7][:, i * 128:(i + 1) * 128], ident_f[0:32, h0 + i:h0 + i + 1].to_broadcast([32, 128]),
                                 acsT[0:32, cs], start=False, stop=(i == 3))
                        S.tt(DVE, Ecl4[:].rearrange("p (i l) -> p i l", i=4), ps[7][:, :].rearrange("p (i l) -> p i l", i=4),
                             negmask2[:, 0:128].unsqueeze(1).to_broadcast([128, 4, 128]), ALU.min)
                        S.act(Eex4[:], Ecl4[:], AF.Exp)
                        M4 = MT4_b[:].rearrange("p (i l) -> p i l", i=4)
                        S.tt(DVE, M4, Eex4[:].rearrange("p (i l) -> p i l", i=4),
                             cbs.unsqueeze(1).to_broadcast([128, 4, 128]), ALU.mult)
                        for i in range(4):
                            h8 = 4 * hq + i
                            S.mm(ps[0][:, h8 * 64:(h8 + 1) * 64], M4[:, i, :], xdt_b[:, h8 * 64:(h8 + 1) * 64], start=True, stop=True)
                    S.mm(ps[1][:, :], bcT[:, 2 * g + 1, cs], stT_b[:, g, :])
                    S.tt(DVE, v3(y1[:]), v3(ps[1][:, :]), hb(C_EACS), ALU.mult)
                    S.tt(DVE, y1[:], y1[:], ps[0][:, :], ALU.add)
                    o_d, _ = RV["m2_d"]
                    dbc = rv[:, o_d + 8 * g:o_d + 8 * g + 8].unsqueeze(2).to_broadcast([128, 8, 64])
                    S.tt(POOL, v3(y2[:]), xv, dbc, ALU.mult)
                    S.tt(POOL, y2[:], y2[:], y1[:], ALU.add)
                    S.tt(DVE, y2[:], y2[:], zs_tm[:, tt, :], ALU.mult)
                    r = rms_stats(y2[:], 4, 1.0 / 512)
                    S.act(y_b[:], y2[:], AF.Copy, scale=r)
                    trb = ps[2][:].bitcast(BF16)
                    for i in range(4):
                        S.tr(trb[:, i * 128:(i + 1) * 128], y_b[:, i * 128:(i + 1) * 128], ident_b[:])
                    for i in range(4):
                        dst = bigT[:, 16 + 4 * g + i, cs]
                        if i % 2 == 0:
                            S.ts(DVE, dst, trb[:, i * 128:(i + 1) * 128], ppc("m2nw", 4 * g + i), None, ALU.mult)
                        else:
                            S.act(dst, trb[:, i * 128:(i + 1) * 128], AF.Copy, scale=ppc("m2nw", 4 * g + i))
                    S.mm(ps[3][:, :], btm[:, tt, g * 128:(g + 1) * 128], xdtd_b[:])
                    S.tt(DVE, v3(stT[:, g, :]), v3(stT[:, g, :]), hb(C_EALAST), ALU.mult)
                    S.tt(DVE, stT[:, g, :], stT[:, g, :], ps[3][:, :], ALU.add)
                    S.copy(ACT, stT_b[:, g, :], stT[:, g, :])

        store_streams = []
        for bi, (tok0, tb) in enumerate(blocks):
            ntl = tb // 128
            for tt in range(ntl):
                a0 = tok0 + tt * 128
                r0 = a0 - NMETA
                lo, hi = max(r0, 0), min(r0 + 128, seq)
                if a0 == 0:
                    S.dma(SP, h[0:NMETA, tt, :], meta[:, :], f"h{tt}")
                    S.dma(SP, h[NMETA:128, tt, :], x[0:128 - NMETA, :], f"h{tt}")
                else:
                    if hi - lo < 128:
                        S.memset(POOL, h[:, tt, :], 0.0)
                    if hi > lo:
                        S.dma(SP, h[0:hi - lo, tt, :], x[lo:hi, :], f"h{tt}")
            if do_gdn or do_m2:
                rmsnorm_to_T("nw_mix", tb)
                small_proj(tb)
                if do_gdn:
                    gdn_all(tb)
                else:
                    S.memset(POOL, bigT[:, 0:16, :], 0.0)
                if do_m2:
                    mamba(tb)
                else:
                    S.memset(POOL, bigT[:, 16:32, :], 0.0)
                for cb in range(4):
                    for kg in range(2):
                        wv = next_w("wo")
                        for tt in range(ntl):
                            for k in range(16):
                                kc = kg * 16 + k
                                S.mm(ps[tt][:, :], bigT[:, kc, tt * 128:(tt + 1) * 128], wv[:, k, :], start=(kc == 0), stop=(kc == 31))
                    for tt in range(ntl):
                        S.tt(DVE, h[:, tt, cb * 512:(cb + 1) * 512], h[:, tt, cb * 512:(cb + 1) * 512], ps[tt][:, :], ALU.add)
            if do_ffn:
                rmsnorm_to_T("nw_ffn", tb)
                for u in range(22):
                    wv = next_w("up")
                    for half in range(2):
                        f_g = 2 * u + half
                        for which in range(2):
                            idx = half * 2 + which
                            pst = ps[idx]
                            co = which * 256 + half * 128
                            for kc in range(16):
                                S.mm(pst[:, 0:tb], wv[:, kc, co:co + 128], hnT[:, kc, 0:tb], start=(kc == 0), stop=(kc == 15))
                            conv_chunk(pst[:, 0:tb], tb, pre[idx], cv[idx], tails_ff, f_g + which * 44, "cw_ff", 3)
                        cg, cvv = cv[half * 2], cv[half * 2 + 1]
                        S.act(cg[:, 0:tb], cg[:, 0:tb], AF.Silu)
                        S.tt(DVE, bigT[:, f_g, 0:tb], cg[:, 0:tb], cvv[:, 0:tb], ALU.mult)
                for cb in range(4):
                    for kg in range(3):
                        wv = next_w("dn")
                        nk = 16 if kg < 2 else 12
                        for tt in range(ntl):
                            for k in range(nk):
                                kc = kg * 16 + k
                                S.mm(ps[tt][:, :], bigT[:, kc, tt * 128:(tt + 1) * 128], wv[:, k, :], start=(kc == 0), stop=(kc == 43))
                    for tt in range(ntl):
                        S.tt(DVE, h[:, tt, cb * 512:(cb + 1) * 512], h[:, tt, cb * 512:(cb + 1) * 512], ps[tt][:, :], ALU.add)
            for tt in range(ntl):
                r = rms_stats(h[:, tt, :], tt, 1.0 / D)
                S.stt(h[:, tt, :], h[:, tt, :], r, rvc("nwf"), ALU.mult, ALU.mult)
                a0 = tok0 + tt * 128
                r0 = a0 - NMETA
                lo, hi = max(r0, 0), min(r0 + 128, seq)
                if hi > lo:
                    S.dma(SP, out[lo:hi, :], h[lo - r0:hi - r0, tt, :], f"o{tt}")
                    if f"o{tt}" not in store_streams:
                        store_streams.append(f"o{tt}")
        S.emit(store_streams)
        print("ops", S.stats)
    return nc


_NC_CACHE = {}


def kernel(**inputs):
    inp = {k: np.asarray(v) for k, v in inputs.items()}
    x = inp["x"]
    B = x.shape[0]
    W = prep_weights(inp)
    if "nc" not in _NC_CACHE:
        _NC_CACHE["nc"] = build(seq=SEQ)
    nc = _NC_CACHE["nc"]
    in_maps = []
    for b in range(B):
        in_maps.append(dict(x=np.ascontiguousarray(x[b], dtype=np.float32), meta=W["meta"], wbig=W["wbig"],
                            wsm=W["wsm"], pp=W["pp"], rv=W["rv"]))
    res = run_bass_kernel_spmd(nc, in_maps, core_ids=list(range(B)))
    return np.stack([np.asarray(r["out"], dtype=np.float32) for r in res.results], axis=0)
```

```python
import numpy as np
import concourse.bass as bass
import concourse.mybir as mybir

F32 = mybir.dt.float32
BF16 = mybir.dt.bfloat16
AF = mybir.ActivationFunctionType
ALU = mybir.AluOpType

PE, ACT, DVE, POOL, SP = "pe", "act", "dve", "pool", "sp"
ENGS = [PE, ACT, DVE, POOL, SP]


class Acc:
    __slots__ = ("op", "eng", "w", "p0", "p1", "f0", "f1")

    def __init__(self, op, eng, w, p0, p1, f0, f1):
        self.op = op; self.eng = eng; self.w = w
        self.p0 = p0; self.p1 = p1; self.f0 = f0; self.f1 = f1


def region(ap):
    sp = str(ap.space)
    if "DRAM" in sp.upper():
        return None
    t = ap.tensor
    a = ap.ap
    ps = a[0][0]
    off = ap.offset
    if ps > 0:
        p0 = off // ps
        f0 = off % ps
    else:
        p0 = 0
        f0 = off
    p1 = p0 + a[0][1]
    ext = 0
    for st, cnt in a[1:]:
        ext += abs(st) * (cnt - 1)
    f1 = f0 + ext + 1
    return (t.name, "PSUM" in sp.upper() or sp.upper().startswith("PS"), p0, p1, f0, f1)


def _fsize(ap):
    n = 1
    for st, cnt in ap.ap[1:]:
        n *= cnt
    return n


def _vcost(eng, out):
    n = _fsize(out)
    if eng == POOL:
        return 250.0 + 1.1 * n
    return 110.0 + n / 0.96


class Sched:
    def __init__(self, nc):
        self.nc = nc
        self.ops = []
        self.hist = {}
        self.dma_count = {}
        self.cap = None
        self.eng_free = {}

    def _regions(self, reads, writes):
        out = []
        for ap in reads:
            if ap is not None:
                rg = region(ap)
                if rg is not None:
                    out.append((rg, False))
        for ap in writes:
            if ap is not None:
                rg = region(ap)
                if rg is not None:
                    out.append((rg, True))
        return out

    def _analyze(self, eng, regs, oid, commit):
        deps = set()
        for (name, is_ps, p0, p1, f0, f1), is_w in regs:
            lst = self.hist.get(name)
            if lst is None:
                lst = []
                if commit:
                    self.hist[name] = lst
            keep = []
            for a in lst:
                if a.op == oid:
                    keep.append(a)
                    continue
                overlap = not (a.p1 <= p0 or p1 <= a.p0 or a.f1 <= f0 or f1 <= a.f0)
                if is_ps and a.eng != eng:
                    conflict = True
                else:
                    conflict = overlap and (is_w or a.w)
                if conflict and not (a.eng == PE and eng == PE):
                    deps.add(a.op)
                covered = is_w and p0 <= a.p0 and a.p1 <= p1 and f0 <= a.f0 and a.f1 <= f1
                if covered or (is_ps and a.eng != eng):
                    continue
                keep.append(a)
            if commit:
                keep.append(Acc(oid, eng, is_w, p0, p1, f0, f1))
                self.hist[name] = keep
        best = {}
        nd = set()
        for d in deps:
            dop = self.ops[d]
            if dop["stream"] is not None:
                nd.add(d)
            else:
                e2 = dop["eng"]
                if best.get(e2, -1) < d:
                    best[e2] = d
        nd.update(best.values())
        return nd

    def _est_start(self, eng, deps):
        t = self.eng_free.get(eng, 0.0)
        for d in deps:
            dop = self.ops[d]
            lat = 60.0 if dop["eng"] == eng else 180.0
            if dop["fin"] + lat > t:
                t = dop["fin"] + lat
        return t

    def add(self, eng, fn, reads, writes, stream=None, cost=300.0):
        if self.cap is not None:
            self.cap.append((eng, fn, self._regions(reads, writes), stream, cost))
            return None
        return self._commit(eng, fn, self._regions(reads, writes), stream, cost)

    def _commit(self, eng, fn, regs, stream, cost, start=None):
        oid = len(self.ops)
        deps = self._analyze(eng, regs, oid, True)
        if start is None:
            start = self._est_start(eng, deps)
        if stream is not None:
            self.eng_free[eng] = start + 500.0
        else:
            self.eng_free[eng] = start + cost
        op = dict(id=oid, eng=eng, fn=fn, deps=deps, stream=stream, sig=False, dn=None, fin=start + cost)
        if stream is not None:
            n = self.dma_count.get(stream, 0) + 1
            self.dma_count[stream] = n
            op["dn"] = n
        self.ops.append(op)
        return oid

    def capture(self, f):
        assert self.cap is None
        self.cap = []
        try:
            f()
            out = self.cap
        finally:
            self.cap = None
        return out

    def merge(self, streams):
        idx = [0] * len(streams)
        live = [i for i in range(len(streams)) if streams[i]]
        while live:
            best = None
            for i in live:
                eng, fn, regs, stream, cost = streams[i][idx[i]]
                deps = self._analyze(eng, regs, -1, False)
                st = self._est_start(eng, deps)
                if best is None or st < best[0]:
                    best = (st, i)
            st, i = best
            eng, fn, regs, stream, cost = streams[i][idx[i]]
            self._commit(eng, fn, regs, stream, cost, start=st)
            idx[i] += 1
            if idx[i] >= len(streams[i]):
                live.remove(i)

    def emit(self, final_wait_streams):
        nc = self.nc
        ops = self.ops
        for op in ops:
            for d in op["deps"]:
                ops[d]["sig"] = True
        cnt = {e: 0 for e in ENGS}
        for op in ops:
            if op["stream"] is None and op["sig"]:
                cnt[op["eng"]] += 1
                op["sv"] = cnt[op["eng"]]
        from contextlib import ExitStack
        with ExitStack() as es:
            sems = {e: es.enter_context(nc.semaphore("sem_" + e)) for e in ENGS}
            dsems = {s: es.enter_context(nc.semaphore("dsem_" + s)) for s in self.dma_count}
            block = es.enter_context(nc.Block())
            per_eng = {e: [op for op in ops if op["eng"] == e] for e in ENGS}

            def run(eng_name, eng):
                waited = {}
                for op in per_eng[eng_name]:
                    need = {}
                    for d in op["deps"]:
                        dop = ops[d]
                        if dop["stream"] is not None:
                            key = ("d", dop["stream"]); val = 16 * dop["dn"]
                        else:
                            key = ("e", dop["eng"]); val = dop["sv"]
                        if need.get(key, 0) < val:
                            need[key] = val
                    for key, val in need.items():
                        if waited.get(key, 0) >= val:
                            continue
                        waited[key] = val
                        sem = dsems[key[1]] if key[0] == "d" else sems[key[1]]
                        eng.wait_ge(sem, val)
                    ins = op["fn"](eng)
                    if op["stream"] is not None:
                        ins.then_inc(dsems[op["stream"]], 16)
                    elif op["sig"]:
                        ins.then_inc(sems[eng_name], 1)
                if eng_name == SP:
                    for s in final_wait_streams:
                        eng.wait_ge(dsems[s], 16 * self.dma_count[s])

            block.tensor(lambda e: run(PE, e))
            block.scalar(lambda e: run(ACT, e))
            block.vector(lambda e: run(DVE, e))
            block.gpsimd(lambda e: run(POOL, e))
            block.sync(lambda e: run(SP, e))
        self.stats = {e: len(per_eng[e]) for e in ENGS}
        self.stats["sig"] = dict(cnt)

    def mm(self, out, lhsT, rhs, start=True, stop=True):
        n = _fsize(out)
        c = max(64, n) * (4 if lhsT.dtype == F32 else 1) / 1.95 + 12
        return self.add(PE, lambda e: e.matmul(out, lhsT=lhsT, rhs=rhs, start=start, stop=stop,
                                               skip_group_check=True),
                        [lhsT, rhs], [out], cost=c)

    def tr(self, out, in_, ident):
        return self.add(PE, lambda e: e.transpose(out, in_, ident), [in_, ident], [out], cost=70.0)

    def act(self, out, in_, func, bias=None, scale=None, accum_out=None, eng=ACT):
        kw = {}
        rd = [in_]
        if bias is not None:
            kw["bias"] = bias
            if not isinstance(bias, (int, float)):
                rd.append(bias)
        if scale is not None:
            kw["scale"] = scale
            if not isinstance(scale, (int, float)):
                rd.append(scale)
        wr = [out]
        if accum_out is not None:
            kw["accum_out"] = accum_out
            wr.append(accum_out)
        return self.add(ACT, lambda e: e.activation(out, in_, func, **kw), rd, wr, cost=200 + 0.83 * _fsize(out))

    def ts(self, eng, out, in0, s1, s2, op0, op1=None, accum_out=None):
        rd = [in0]
        if not isinstance(s1, (int, float)):
            rd.append(s1)
        if s2 is not None and not isinstance(s2, (int, float)):
            rd.append(s2)
        kw = {}
        wr = [out]
        if op1 is not None:
            kw["op1"] = op1
        if accum_out is not None:
            kw["accum_out"] = accum_out
            wr.append(accum_out)
        return self.add(eng, lambda e: e.tensor_scalar(out, in0, s1, s2, op0, **kw), rd, wr, cost=_vcost(eng, out))

    def stt(self, out, in0, scalar, in1, op0, op1, eng=DVE):
        rd = [in0, in1]
        if not isinstance(scalar, (int, float)):
            rd.append(scalar)
        return self.add(eng, lambda e: e.scalar_tensor_tensor(out, in0, scalar, in1, op0, op1), rd, [out], cost=_vcost(eng, out))

    def tt(self, eng, out, in0, in1, op):
        return self.add(eng, lambda e: e.tensor_tensor(out, in0, in1, op), [in0, in1], [out], cost=_vcost(eng, out))

    def copy(self, eng, out, in_):
        if eng == ACT:
            return self.add(ACT, lambda e: e.copy(out, in_), [in_], [out], cost=200 + 0.83 * _fsize(out))
        return self.add(eng, lambda e: e.tensor_copy(out, in_), [in_], [out], cost=_vcost(eng, out))

    def memset(self, eng, out, val):
        return self.add(eng, lambda e: e.memset(out, val), [], [out], cost=_vcost(eng, out))

    def dma(self, queue, out, in_, stream):
        return self.add(queue, lambda e: e.dma_start(out=out, in_=in_), [in_], [out], stream=stream,
                        cost=2500.0 + _fsize(out) * out.ap[0][1] * 4 / 150.0)


import numpy as np
import concourse.bass as bass
import concourse.mybir as mybir
from concourse.bass_utils import run_bass_kernel_spmd

D = 2048
KC = 16
SEQ = 4096
NMETA = 16
DFF = 5632
EPS = 1e-6
GW = 8192

OQ, OK_, OV, OZ, OB, OA, OMZ, OXS, OBM, OCM, ODT = 0, 2048, 4096, 6144, 8192, 8208, 8224, 10272, 12320, 12832, 13344


def group_list():
    gl = []
    for h in range(16):
        gl.append(("gdn", h))
    gl.append(("mbc", 0)); gl.append(("mbc", 1))
    for g in range(4):
        gl.append(("mx", g)); gl.append(("mz", g))
    for cb in range(4):
        for kg in range(2):
            gl.append(("wo", cb, kg))
    for u in range(22):
        gl.append(("up", u))
    for cb in range(4):
        for kg in range(3):
            gl.append(("dn", cb, kg))
    return gl


GL = group_list()
GIDX = {g: i for i, g in enumerate(GL)}
NG = len(GL)

PP = {}
_o = 0
for nm, n in [("nw_mix", 16), ("nw_ffn", 16), ("m2nw", 16), ("dnnw", 1), ("cw_dn", 48 * 4), ("cw_m2", 24 * 4),
              ("cb_m2", 24), ("cw_ff", 88 * 3)]:
    PP[nm] = (_o, n); _o += n
NPP = _o
RV = {}
_o = 0
for nm, n in [("nwf", 2048), ("dn_alog", 16), ("dn_dtb", 16), ("m2_alog", 32), ("m2_dtb", 32), ("m2_d", 32)]:
    RV[nm] = (_o, n); _o += n
NRV = _o


def prep_weights(inp):
    w_in = np.asarray(inp["w_in"][0]); w_out = np.asarray(inp["w_out"][0])
    up = np.asarray(inp["ffn_up"][0]); dn = np.asarray(inp["ffn_down"][0])
    wbig = np.zeros((NG, 128, GW), np.float32)

    def put(gi, W, rows0, nk, cols):
        blk = W[rows0:rows0 + nk * 128][:, cols]
        blk = blk.reshape(nk, 128, len(cols)).transpose(1, 0, 2)
        wbig[gi, :, :nk * 512] = blk.reshape(128, nk * 512)

    ar = np.arange
    for gi, g in enumerate(GL):
        if g[0] == "gdn":
            h = g[1]
            cols = np.concatenate([OQ + h * 128 + ar(128), OK_ + h * 128 + ar(128), OV + h * 128 + ar(128), OZ + h * 128 + ar(128)])
            put(gi, w_in, 0, 16, cols)
        elif g[0] == "mbc":
            p = g[1]
            cols = np.concatenate([OBM + (2 * p) * 128 + ar(128), OCM + (2 * p) * 128 + ar(128),
                                   OBM + (2 * p + 1) * 128 + ar(128), OCM + (2 * p + 1) * 128 + ar(128)])
            put(gi, w_in, 0, 16, cols)
        elif g[0] == "mx":
            put(gi, w_in, 0, 16, OXS + g[1] * 512 + ar(512))
        elif g[0] == "mz":
            put(gi, w_in, 0, 16, OMZ + g[1] * 512 + ar(512))
        elif g[0] == "wo":
            put(gi, w_out, g[2] * 2048, 16, g[1] * 512 + ar(512))
        elif g[0] == "up":
            u = g[1]
            cols = np.concatenate([u * 256 + ar(256), DFF + u * 256 + ar(256)])
            put(gi, up, 0, 16, cols)
        elif g[0] == "dn":
            kg = g[2]
            nk = 16 if kg < 2 else 12
            put(gi, dn, kg * 2048, nk, g[1] * 512 + ar(512))
    cols = np.concatenate([OB + ar(16), OA + ar(16), ODT + ar(32)])
    wsm = w_in[:, cols].reshape(16, 128, 64).transpose(1, 0, 2).reshape(128, 16 * 64).copy()
    pp = np.zeros((128, NPP), np.float32)

    def fm(v):
        return np.asarray(v).reshape(-1, 128).T

    def setp(nm, a):
        o, n = PP[nm]; pp[:, o:o + n] = a.reshape(128, n)
    setp("nw_mix", fm(inp["norm_mix_w"][0])); setp("nw_ffn", fm(inp["norm_ffn_w"][0]))
    setp("m2nw", fm(inp["m2_norm_w"][0])); setp("dnnw", np.asarray(inp["dn_norm_w"][0]).reshape(128, 1))

    def cw(w):
        w = np.asarray(w); K, C = w.shape
        return w.reshape(K, C // 128, 128).transpose(2, 1, 0).reshape(128, -1)
    setp("cw_dn", cw(inp["dn_conv_w"][0])); setp("cw_m2", cw(inp["m2_conv_w"][0]))
    setp("cb_m2", fm(inp["m2_conv_b"][0])); setp("cw_ff", cw(inp["ffn_conv_w"][0]))
    rv = np.zeros((1, NRV), np.float32)

    def setr(nm, a):
        o, n = RV[nm]; rv[0, o:o + n] = np.asarray(a).reshape(n)
    setr("nwf", inp["norm_final_w"]); setr("dn_alog", inp["dn_a_log"][0]); setr("dn_dtb", inp["dn_dt_bias"][0])
    setr("m2_alog", inp["m2_a_log"][0]); setr("m2_dtb", inp["m2_dt_bias"][0]); setr("m2_d", inp["m2_d"][0])
    return dict(wbig=wbig, wsm=wsm, pp=pp, rv=rv, meta=np.asarray(inp["meta_tokens"], np.float32))


import math

C_RAW, C_LNB, C_BETA, C_G, C_AM, C_GC, C_ACS, C_GLAST, C_ALAST = 0, 64, 80, 96, 112, 144, 160, 192, 208
C_EGC, C_EACS, C_EGLAST, C_EALAST, C_ED, C_ED2, C_BEGE, C_DT, C_TMP = 240, 256, 288, 304, 336, 352, 384, 400, 432
C_STA, C_STB = 432, 512
NS = 576


def build(seq=SEQ, T=384, do_gdn=True, do_m2=True, NWB=2, do_ffn=True):
    ntok = NMETA + seq
    nblk = (ntok + T - 1) // T
    blocks = []
    t0 = 0
    for b in range(nblk):
        tb = min(T, ((ntok - t0 + 127) // 128) * 128)
        blocks.append((t0, tb)); t0 += tb
    nc = bass.Bass("TRN2", target_bir_lowering=False)
    x = nc.dram_tensor("x", [seq, D], F32, kind="ExternalInput").ap()
    meta = nc.dram_tensor("meta", [NMETA, D], F32, kind="ExternalInput").ap()
    wbig = nc.dram_tensor("wbig", [NG, 128, GW], F32, kind="ExternalInput").ap()
    wsm_d = nc.dram_tensor("wsm", [128, 16 * 64], F32, kind="ExternalInput").ap()
    pp_d = nc.dram_tensor("pp", [128, NPP], F32, kind="ExternalInput").ap()
    rv_d = nc.dram_tensor("rv", [1, NRV], F32, kind="ExternalInput").ap()
    out = nc.dram_tensor("out", [seq, D], F32, kind="ExternalOutput").ap()
    NT = T // 128
    from contextlib import ExitStack
    with ExitStack() as es:
        def sb(name, shape, dt=F32):
            return es.enter_context(nc.sbuf_tensor(name, shape, dt))

        def psb(name, shape, dt=F32):
            return es.enter_context(nc.psum_tensor(name, shape, dt))
        S = Sched(nc)
        ident_f = sb("ident_f", [128, 128]); ident_b = sb("ident_b", [128, 128], BF16)
        ones_f = sb("ones_f", [128, 128]); ones_b = sb("ones_b", [128, 128], BF16)
        nones2 = sb("nones2", [128, 256])
        UT_f = sb("UT_f", [128, 128])
        masks = sb("masks", [128, 256])
        pp = sb("pp_sb", [128, NPP]); rv = sb("rv_sb", [128, NRV])
        nA = sb("nA", [128, 48])
        wsm_b = sb("wsm_b", [128, 16 * 64], BF16)
        h = sb("h", [128, NT, D])
        hnT = sb("hnT", [128, KC, T], BF16)
        bigT = sb("bigT", [128, 44, T], BF16)
        wb = [sb(f"wb{i}", [128, GW], BF16) for i in range(NWB)]
        hn_tm = sb("hn_tm", [128, D], BF16)
        stat = sb("stat", [128, 32])
        tails_ff = sb("tails_ff", [128, 88, 3])
        tails_dn = sb("tails_dn", [128, 48, 3])
        tails_m2 = sb("tails_m2", [128, 24, 3])
        pre = [sb(f"pre{i}", [128, 3 + T]) for i in range(4)]
        cv = [sb(f"cv{i}", [128, T]) for i in range(4)]
        ft = [sb(f"ft{i}", [128, 512]) for i in range(6)]
        bt = [None] * 4 + [sb(f"bt{i}", [128, 512], BF16) for i in range(4, 9)]
        negmask2 = sb("negmask2", [128, 256])
        Sst = sb("Sst", [128, 16, 128]); S_b = sb("S_b", [128, 16, 128], BF16)
        stT = sb("stT", [128, 4, 512]); stT_b = sb("stT_b", [128, 4, 512], BF16)
        tokS = [sb(f"tokS{i}", [128, NS]) for i in range(NT)]
        glT = sb("glT", [32, T]); nglT = sb("nglT", [16, T]); egcT = sb("egcT", [16, T])
        acsT = sb("acsT", [32, T]); nacsT = sb("nacsT", [32, T])
        bcT = sb("bcT", [128, 8, T], BF16)
        btm = sb("btm", [128, NT, 512], BF16)
        xs_tm = sb("xs_tm", [128, NT, 512], BF16)
        zs_tm = sb("zs_tm", [128, NT, 512], BF16)
        ps = [psb(f"ps{i}", [128, 512]) for i in range(8)]
        print("sbuf remaining", nc.sbuf_bytes_remaining)

        S.memset(POOL, ones_f[:], 1.0)
        S.memset(POOL, nones2[:], -1.0)
        S.memset(POOL, ones_b[:], 1.0)

        def asel(out_ap, in_ap, cmp, base):
            S.add(POOL, lambda e: e.affine_select(out_ap, in_ap, [[1, 128]], cmp, 0.0, base=base,
                                                  channel_multiplier=-1), [in_ap], [out_ap])
        asel(ident_f[:], ones_f[:], ALU.is_equal, 0)
        asel(UT_f[:], ones_f[:], ALU.is_ge, 0)
        S.copy(POOL, masks[:, 0:128], UT_f[:])
        asel(masks[:, 128:256], nones2[:, 0:128], ALU.is_ge, -1)
        S.copy(POOL, ident_b[:], ident_f[:])
        S.ts(DVE, negmask2[:, 0:128], masks[:, 0:128], -1.0, 30000.0, ALU.add, ALU.mult)
        S.ts(DVE, negmask2[:, 128:256], masks[:, 128:256], 1.0, -30000.0, ALU.add, ALU.mult)
        S.dma(SP, pp[:], pp_d[:, :], "pp")
        S.dma(SP, rv[:], rv_d[0:1, :].to_broadcast([128, NRV]), "rv")
        S.dma(POOL, wsm_b[:], wsm_d[:, :], "wsm")
        for t_ in (tails_ff, tails_dn, tails_m2, Sst, S_b, stT, stT_b):
            S.memset(POOL, t_[:], 0.0)

        def ppc(nm, i=0, n=1):
            o, _ = PP[nm]
            return pp[:, o + i:o + i + n]

        def rvc(nm):
            o, n = RV[nm]
            return rv[:, o:o + n]
        ncb = sb("ncb", [128, 24])
        S.ts(DVE, ncb[:], ppc("cb_m2", 0, 24), -1.0, None, ALU.mult)
        S.act(nA[:, 0:16], rvc("dn_alog"), AF.Exp)
        S.act(nA[:, 16:48], rvc("m2_alog"), AF.Exp)
        S.ts(DVE, nA[:], nA[:], -1.0, None, ALU.mult)

        wstate = dict(next=0)
        total_groups = []

        def plan_groups(bi):
            return [g for g in GL if (g[0] in ("up", "dn") and do_ffn) or (g[0] == "wo" and (do_gdn or do_m2))
                    or (g[0] == "gdn" and do_gdn) or (g[0] in ("mbc", "mx", "mz") and do_m2)]
        for bi in range(nblk):
            total_groups += plan_groups(bi)

        def issue_w(n):
            if n >= len(total_groups):
                return
            g = total_groups[n]
            nk = 12 if (g[0] == "dn" and g[2] == 2) else 16
            S.dma(POOL, wb[n % NWB][:, 0:nk * 512], wbig[GIDX[g], :, 0:nk * 512], f"w{n % NWB}")

        for n in range(NWB - 1):
            issue_w(n)

        def next_w(expect):
            n = wstate["next"]
            assert total_groups[n][0] == expect, (total_groups[n], expect)
            issue_w(n + NWB - 1)
            wstate["next"] = n + 1
            return wb[n % NWB][:].rearrange("p (k c) -> p k c", k=16)

        def rms_stats(src, tt, scale):
            c0 = 3 * tt
            S.act(hn_tm[:, 0:src.shape[1]], src, AF.Square, accum_out=stat[:, c0:c0 + 1])
            S.act(stat[:, c0 + 1:c0 + 2], stat[:, c0:c0 + 1], AF.Ln, bias=EPS, scale=scale)
            S.act(stat[:, c0 + 2:c0 + 3], stat[:, c0 + 1:c0 + 2], AF.Exp, scale=-0.5)
            return stat[:, c0 + 2:c0 + 3]

        def rmsnorm_to_T(nw_name, tb):
            ntl = tb // 128
            for tt in range(ntl):
                r = rms_stats(h[:, tt, :], tt, 1.0 / D)
                S.ts(DVE, hn_tm[:], h[:, tt, :], r, None, ALU.mult)
                for kq in range(4):
                    pv = ps[6 + (kq % 2)][:].bitcast(BF16)
                    for j in range(4):
                        kc = kq * 4 + j
                        S.tr(pv[:, j * 128:(j + 1) * 128], hn_tm[:, kc * 128:(kc + 1) * 128], ident_b[:])
                    for j in range(4):
                        kc = kq * 4 + j
                        dst = hnT[:, kc, tt * 128:(tt + 1) * 128]
                        if j % 2 == 0:
                            S.ts(DVE, dst, pv[:, j * 128:(j + 1) * 128], ppc(nw_name, kc), None, ALU.mult)
                        else:
                            S.act(dst, pv[:, j * 128:(j + 1) * 128], AF.Copy, scale=ppc(nw_name, kc))

        def silu_exp(out, src, tmp, bias=None, nbias=None):
            if bias is None:
                S.act(tmp, src, AF.Exp, scale=-1.0)
            else:
                S.act(tmp, src, AF.Exp, scale=-1.0, bias=nbias)
            S.act(tmp, tmp, AF.Ln, bias=1.0)
            S.act(tmp, tmp, AF.Exp, scale=-1.0)
            if bias is None:
                S.tt(DVE, out, src, tmp, ALU.mult)
            else:
                S.stt(out, src, bias, tmp, ALU.add, ALU.mult)

        def conv_chunk(psrc, tb, pr, c, tails, ch, cwname, K):
            S.copy(POOL, pr[:, 0:3], tails[:, ch, :])
            S.act(pr[:, 3:3 + tb], psrc, AF.Copy)
            S.copy(POOL, tails[:, ch, :], pr[:, tb:tb + 3])
            o, _ = PP[cwname]
            b0 = 4 - K
            S.ts(DVE, c[:, 0:tb], pr[:, b0:b0 + tb], pp[:, o + ch * K:o + ch * K + 1], None, ALU.mult)
            for j in range(1, K):
                S.stt(c[:, 0:tb], pr[:, b0 + j:b0 + j + tb], pp[:, o + ch * K + j:o + ch * K + j + 1], c[:, 0:tb],
                      ALU.mult, ALU.add)

        def small_proj(tb):
            ntl = tb // 128
            for tt in range(ntl):
                tk = tokS[tt]
                cs = slice(tt * 128, (tt + 1) * 128)
                for kc in range(16):
                    S.mm(ps[7][:, 0:64], hnT[:, kc, cs], wsm_b[:, kc * 64:(kc + 1) * 64], start=(kc == 0), stop=(kc == 15))
                S.copy(ACT, tk[:, 0:64], ps[7][:, 0:64])
                tmp = tk[:, C_TMP:C_TMP + 96]
                S.act(tmp[:, 0:16], tk[:, 0:16], AF.Exp, scale=-1.0)
                S.tt(DVE, tmp[:, 16:32], tk[:, 16:32], rvc("dn_dtb"), ALU.add)
                S.tt(DVE, tmp[:, 32:64], tk[:, 32:64], rvc("m2_dtb"), ALU.add)
                S.act(tmp[:, 16:64], tmp[:, 16:64], AF.Exp)
                S.act(tmp[:, 0:64], tmp[:, 0:64], AF.Ln, bias=1.0)
                S.ts(DVE, tk[:, C_LNB:C_LNB + 16], tmp[:, 0:16], -1.0, None, ALU.mult)
                S.act(tk[:, C_BETA:C_BETA + 16], tk[:, C_LNB:C_LNB + 16], AF.Exp)
                S.tt(DVE, tk[:, C_G:C_G + 16], tmp[:, 16:32], nA[:, 0:16], ALU.mult)
                S.copy(DVE, tk[:, C_DT:C_DT + 32], tmp[:, 32:64])
                S.tt(DVE, tk[:, C_AM:C_AM + 32], tmp[:, 32:64], nA[:, 16:48], ALU.mult)
                S.mm(ps[7][:, 64:112], UT_f[:], tk[:, C_G:C_G + 48])
                S.mm(ps[7][:, 112:160], ones_f[:], tk[:, C_G:C_G + 48])
                S.copy(ACT, tk[:, C_GC:C_GC + 96], ps[7][:, 64:160])
                S.act(tk[:, C_EGC:C_EGC + 96], tk[:, C_GC:C_GC + 96], AF.Exp)
                S.tt(DVE, tmp[:, 0:48], tk[:, C_GLAST:C_GLAST + 48], tk[:, C_GC:C_GC + 48], ALU.subtract)
                S.act(tk[:, C_ED:C_ED + 48], tmp[:, 0:48], AF.Exp)
                S.tt(DVE, tk[:, C_BEGE:C_BEGE + 16], tk[:, C_BETA:C_BETA + 16], tk[:, C_EGC:C_EGC + 16], ALU.mult)
                stA = tk[:, C_STA:C_STA + 80]; stB = tk[:, C_STB:C_STB + 64]
                S.copy(DVE, stA[:, 0:16], tk[:, C_GC:C_GC + 16])
                S.tt(DVE, stA[:, 16:32], tk[:, C_GC:C_GC + 16], tk[:, C_LNB:C_LNB + 16], ALU.add)
                S.ts(DVE, stA[:, 32:48], tk[:, C_GC:C_GC + 16], -1.0, None, ALU.mult)
                S.memset(DVE, stA[:, 48:64], 0.0)
                S.copy(DVE, stA[:, 64:80], tk[:, C_EGC:C_EGC + 16])
                S.copy(DVE, stB[:, 0:32], tk[:, C_ACS:C_ACS + 32])
                S.ts(DVE, stB[:, 32:64], tk[:, C_ACS:C_ACS + 32], -1.0, None, ALU.mult)
                for (dst, src, n) in [(glT, stA[:, 0:32], 32), (nglT, stA[:, 32:48], 16), (egcT, stA[:, 64:80], 16),
                                      (acsT, stB[:, 0:32], 32), (nacsT, stB[:, 32:64], 32)]:
                    S.tr(ps[7][0:n, 0:128], src, ident_f[:])
                    S.copy(ACT, dst[0:n, cs], ps[7][0:n, 0:128])

        def run_rr(gens):
            gens = list(gens)
            while gens:
                nxt = []
                for g_ in gens:
                    try:
                        next(g_); nxt.append(g_)
                    except StopIteration:
                        pass
                gens = nxt

        def slot128(i):
            if i < 12:
                return btm[:, i // 4, (i % 4) * 128:(i % 4 + 1) * 128]
            i -= 12
            return zs_tm[:, i // 4, (i % 4) * 128:(i % 4 + 1) * 128]

        def xp_tile(c, i):
            k = 3 * c + i
            return bcT[:, k, 0:384] if k < 8 else xs_tm[:, 0, 0:384]

        def gdn_prep_block():
            for c in range(3):
                S.copy(POOL, xp_tile(c, 0)[:, 0:128], ident_b[:])

        def gset(sidx):
            b0 = 32 + 4 * sidx
            return dict(qn=bigT[:, b0, :], kn=bigT[:, b0 + 1, :], v=bigT[:, b0 + 2, :], Qg=bigT[:, b0 + 3, :],
                        zsil=bt[6 + sidx])

        def cslot(i):
            return bigT[:, 16 + i // 3, (i % 3) * 128:(i % 3 + 1) * 128]

        def cbufs(c, pset):
            b0 = 24 * pset + 8 * c
            return dict(Kb=cslot(b0), Kd=cslot(b0 + 1), Vb=cslot(b0 + 2), QKm=cslot(b0 + 3),
                        nWT=cslot(b0 + 4), TT=cslot(b0 + 5), Em0=cslot(b0 + 6), Em1=cslot(b0 + 7))

        def gdn_stage1(hd, tb):
            G = gset(hd % 3)
            wv = next_w("gdn")
            PA, PBk = ps[6], ps[7]
            sq_b = bt[4]
            tmpf = ft[0]

            def proj(i, bank):
                for kc in range(16):
                    S.mm(bank[:, 0:tb], wv[:, kc, i * 128:(i + 1) * 128], hnT[:, kc, 0:tb], start=(kc == 0), stop=(kc == 15))
                    if kc % 8 == 7:
                        yield

            def convsilu(i, bank):
                conv_chunk(bank[:, 0:tb], tb, pre[i], cv[i], tails_dn, i * 16 + hd, "cw_dn", 4)
                yield
                silu_exp(cv[i][:, 0:tb], cv[i][:, 0:tb], tmpf[:, 0:tb])
                yield
            yield from proj(0, PA)
            yield from proj(1, PBk)
            yield from convsilu(0, PA)
            yield from proj(2, PA)
            yield from convsilu(1, PBk)
            yield from proj(3, PBk)
            yield from convsilu(2, PA)
            S.copy(ACT, ft[4][:, 0:tb], PBk[:, 0:tb]) if False else None
            silu_exp(G["zsil"][:, 0:tb], PBk[:, 0:tb], tmpf[:, 0:tb])
            S.mm(PA[:, 0:tb], ident_f[0:16, hd:hd + 1].to_broadcast([16, 128]), egcT[0:16, 0:tb])
            yield
            for i, dstb in enumerate([G["qn"], G["kn"]]):
                S.act(sq_b[:, 0:tb], cv[i][:, 0:tb], AF.Square)
                S.mm(PBk[:, 0:tb], ones_b[:], sq_b[:, 0:tb])
                yield
                S.act(tmpf[:, 0:tb], PBk[:, 0:tb], AF.Ln, bias=EPS)
                S.act(tmpf[:, 0:tb], tmpf[:, 0:tb], AF.Exp, scale=-0.5, bias=(math.log(128 ** -0.5) if i == 0 else 0.0))
                yield
                S.tt(DVE, dstb[:, 0:tb], cv[i][:, 0:tb], tmpf[:, 0:tb], ALU.mult)
                yield
            S.tt(DVE, G["Qg"][:, 0:tb], G["qn"][:, 0:tb], PA[:, 0:tb], ALU.mult)
            S.copy(ACT, G["v"][:, 0:tb], cv[2][:, 0:tb])
            yield

        def gdn_stage2(hd, tb):
            ntl = tb // 128
            G = gset(hd % 3)
            qn_b, kn_b, v_b = G["qn"], G["kn"], G["v"]
            Ecls = [ft[2][:, 0:256], ft[2][:, 256:512], ft[3][:, 0:256]]

            def bufs(c):
                return cbufs(c, hd % 2)

            def chainA(c):
                tt = c
                cs = slice(tt * 128, (tt + 1) * 128)
                tk = tokS[tt]
                B = bufs(c)
                bank = ps[c]
                trb = bank[:].bitcast(BF16)
                ev = ACT if (c % 2 == 0) else DVE

                def col(o):
                    return tk[:, o + hd:o + hd + 1]
                S.tr(trb[:, 768:896], kn_b[:, cs], ident_b[:])
                S.tr(trb[:, 896:1024], v_b[:, cs], ident_b[:])
                S.ts(DVE, B["Kb"], trb[:, 768:896], col(C_BEGE), None, ALU.mult)
                S.act(B["Kd"], trb[:, 768:896], AF.Copy, scale=col(C_ED))
                S.ts(DVE, B["Vb"], trb[:, 896:1024], col(C_BETA), None, ALU.mult)
                yield
                S.mm(bank[:, 0:128], kn_b[:, cs], qn_b[:, cs])
                S.mm(bank[:, 128:256], kn_b[:, cs], kn_b[:, cs])
                S.mm(bank[:, 256:512], nglT[0:16, cs], ident_f[0:16, hd:hd + 1].to_broadcast([16, 256]), start=True, stop=False)
                S.mm(bank[:, 256:384], ident_f[0:32, hd:hd + 1].to_broadcast([32, 128]), glT[0:32, cs], start=False, stop=False)
                S.mm(bank[:, 384:512], ident_f[0:32, 16 + hd:17 + hd].to_broadcast([32, 128]), glT[0:32, cs], start=False, stop=True)
                yield
                S.tt(DVE, Ecls[c], bank[:, 256:512], negmask2[:], ALU.min)
                yield
                S.act(B["Em0"], Ecls[c][:, 0:128], AF.Exp)
                S.act(B["Em1"], Ecls[c][:, 128:256], AF.Exp)
                yield
                XP = xp_tile(c, 0)
                S.tt(DVE, B["QKm"], bank[:, 0:128], B["Em0"], ALU.mult)
                S.stt(XP[:, 128:256], bank[:, 128:256], -1.0, B["Em1"], ALU.mult, ALU.mult)
                yield
                S.tr(trb[:, 0:128], XP[:, 128:256], ident_b[:])
                S.copy(ev, XP[:, 256:384], trb[:, 0:128])
                yield
                for j in range(7):
                    last = (j == 6)
                    if not last:
                        XPn = xp_tile(c, 1 + (j % 2))
                        S.mm(bank[:, 0:256], XP[:, 256:384], XP[:, 0:256], start=True, stop=False)
                        S.mm(bank[:, 0:128], ident_b[:], XP[:, 0:128], start=False, stop=True)
                        S.mm(bank[:, 256:384], XP[:, 128:256], XP[:, 256:384], start=True, stop=True)
                        S.copy(ev, XPn, bank[:, 0:384])
                        XP = XPn
                    else:
                        S.mm(bank[:, 0:128], XP[:, 256:384], XP[:, 0:128], start=True, stop=False)
                        S.mm(bank[:, 0:128], ident_b[:], XP[:, 0:128], start=False, stop=True)
                        S.copy(ev, B["TT"], bank[:, 0:128])
                    yield
                S.mm(bank[:, 0:128], B["Kb"], B["TT"])
                S.act(B["nWT"], bank[:, 0:128], AF.Copy, scale=-1.0)
                yield

            return [chainA(c) for c in range(ntl)]

        def gdn_stage3(hd, tb):
            ntl = tb // 128
            G = gset(hd % 3)
            Qg_b, zsil = G["Qg"], G["zsil"]
            oT = ps[3]
            vnew_b = btm[:, 0, 0:128]
            for tt in range(ntl):
                cs = slice(tt * 128, (tt + 1) * 128)
                tk = tokS[tt]
                B = cbufs(tt, hd % 2)
                S.mm(ps[4][:, 0:128], B["TT"], B["Vb"], start=True, stop=False)
                S.mm(ps[4][:, 0:128], B["nWT"], S_b[:, hd, :], start=False, stop=True)
                S.copy(ACT, vnew_b, ps[4][:, 0:128])
                yield
                S.mm(oT[:, cs], S_b[:, hd, :], Qg_b[:, cs], start=True, stop=False)
                S.mm(oT[:, cs], vnew_b, B["QKm"], start=False, stop=True)
                S.mm(ps[5][:, 0:128], B["Kd"], vnew_b)
                yield
                S.stt(Sst[:, hd, :], Sst[:, hd, :], tk[:, C_EGLAST + hd:C_EGLAST + hd + 1], ps[5][:, 0:128], ALU.mult, ALU.add)
                S.copy(ACT, S_b[:, hd, :], Sst[:, hd, :])
                yield
            sq3 = bt[5]
            tmpf3, tmpf2 = ft[4], ft[5]
            S.act(sq3[:, 0:tb], oT[:, 0:tb], AF.Square)
            S.mm(ps[4][:, 0:tb], ones_b[:], sq3[:, 0:tb])
            yield
            S.act(tmpf3[:, 0:tb], ps[4][:, 0:tb], AF.Ln, bias=EPS, scale=1.0 / 128)
            S.act(tmpf3[:, 0:tb], tmpf3[:, 0:tb], AF.Exp, scale=-0.5)
            yield
            S.tt(DVE, tmpf2[:, 0:tb], oT[:, 0:tb], tmpf3[:, 0:tb], ALU.mult)
            S.stt(bigT[:, hd, 0:tb], tmpf2[:, 0:tb], ppc("dnnw"), zsil[:, 0:tb], ALU.mult, ALU.mult)
            yield

        def exhaust(gen):
            for _ in gen:
                pass

        def gdn_all(tb):
            gdn_prep_block()
            for t in range(16 + 2):
                streams = []
                if 0 <= t - 2 < 16:
                    streams.append(S.capture(lambda: exhaust(gdn_stage3(t - 2, tb))))
                if 0 <= t - 1 < 16:
                    for ch in gdn_stage2(t - 1, tb):
                        streams.append(S.capture(lambda: exhaust(ch)))
                if t < 16:
                    streams.append(S.capture(lambda: exhaust(gdn_stage1(t, tb))))
                S.merge(streams)

        def mamba(tb):
            ntl = tb // 128
            for p in range(2):
                wv = next_w("mbc")
                for i in range(4):
                    for kc in range(16):
                        S.mm(ps[i][:, 0:tb], wv[:, kc, i * 128:(i + 1) * 128], hnT[:, kc, 0:tb], start=(kc == 0), stop=(kc == 15))
                for i in range(4):
                    g = 2 * p + i // 2
                    isC = i % 2
                    ch = (20 if isC else 16) + g
                    conv_chunk(ps[i][:, 0:tb], tb, pre[i], cv[i], tails_m2, ch, "cw_m2", 4)
                    silu_exp(bcT[:, 2 * g + isC, 0:tb], cv[i][:, 0:tb], ft[0][:, 0:tb], bias=ppc("cb_m2", ch), nbias=ncb[:, ch:ch + 1])
            for tt in range(ntl):
                cs = slice(tt * 128, (tt + 1) * 128)
                trb = ps[4][:].bitcast(BF16)
                for g in range(4):
                    S.tr(trb[:, g * 128:(g + 1) * 128], bcT[:, 2 * g, cs], ident_b[:])
                S.copy(ACT, btm[:, tt, :], trb[:, 0:512])
            Ecl4, y1, y2 = ft[0], ft[1], ft[2]
            cbs = ft[3][:, 0:128]
            xdt_b, xdtd_b, MT4_b, y_b, Eex4 = bt[4], bt[5], bt[6], bt[7], bt[8]
            for g in range(4):
                wv = next_w("mx")
                for i in range(4):
                    for kc in range(16):
                        S.mm(ps[i][:, 0:tb], wv[:, kc, i * 128:(i + 1) * 128], hnT[:, kc, 0:tb], start=(kc == 0), stop=(kc == 15))
                for i in range(4):
                    ch = 4 * g + i
                    conv_chunk(ps[i][:, 0:tb], tb, pre[i], cv[i], tails_m2, ch, "cw_m2", 4)
                    silu_exp(bigT[:, 32 + i, 0:tb], cv[i][:, 0:tb], ft[0][:, 0:tb], bias=ppc("cb_m2", ch), nbias=ncb[:, ch:ch + 1])
                for tt in range(ntl):
                    cs = slice(tt * 128, (tt + 1) * 128)
                    trb = ps[4][:].bitcast(BF16)
                    for i in range(4):
                        S.tr(trb[:, i * 128:(i + 1) * 128], bigT[:, 32 + i, cs], ident_b[:])
                    S.copy(ACT, xs_tm[:, tt, :], trb[:, 0:512])
                wv = next_w("mz")
                for tt in range(ntl):
                    cs = slice(tt * 128, (tt + 1) * 128)
                    for kc in range(16):
                        S.mm(ps[5][:, :], hnT[:, kc, cs], wv[:, kc, :], start=(kc == 0), stop=(kc == 15))
                    silu_exp(zs_tm[:, tt, :], ps[5][:, :], ft[3][:, :])
                for tt in range(ntl):
                    cs = slice(tt * 128, (tt + 1) * 128)
                    tk = tokS[tt]

                    def hb(o):
                        return tk[:, o + 8 * g:o + 8 * g + 8].unsqueeze(2).to_broadcast([128, 8, 64])

                    def v3(ap):
                        return ap.rearrange("p (h c) -> p h c", h=8)
                    xv = v3(xs_tm[:, tt, :])
                    S.tt(POOL, v3(xdt_b[:]), xv, hb(C_DT), ALU.mult)
                    S.tt(POOL, v3(xdtd_b[:]), v3(xdt_b[:]), hb(C_ED2), ALU.mult)
                    S.mm(ps[6][:, 0:128], bcT[:, 2 * g, cs], bcT[:, 2 * g + 1, cs])
                    S.copy(ACT, cbs, ps[6][:, 0:128])
                    for hq in range(2):
                        h0 = 8 * g + 4 * hq
                        S.mm(ps[7][:, :], nacsT[0:32, cs],
                             ident_f[0:32, h0:h0 + 4].unsqueeze(2).to_broadcast([32, 4, 128]), start=True, stop=False)
                        for i in range(4):
                            S.mm(ps[7][:, i * 128:(i + 1) * 128], ident_f[0:32, h0 + i:h0 + i + 1].to_broadcast([32, 128]),
                                 acsT[0:32, cs], start=False, stop=(i == 3))
                        S.tt(DVE, Ecl4[:].rearrange("p (i l) -> p i l", i=4), ps[7][:, :].rearrange("p (i l) -> p i l", i=4),
                             negmask2[:, 0:128].unsqueeze(1).to_broadcast([128, 4, 128]), ALU.min)
                        S.act(Eex4[:], Ecl4[:], AF.Exp)
                        M4 = MT4_b[:].rearrange("p (i l) -> p i l", i=4)
                        S.tt(DVE, M4, Eex4[:].rearrange("p (i l) -> p i l", i=4),
                             cbs.unsqueeze(1).to_broadcast([128, 4, 128]), ALU.mult)
                        for i in range(4):
                            h8 = 4 * hq + i
                            S.mm(ps[0][:, h8 * 64:(h8 + 1) * 64], M4[:, i, :], xdt_b[:, h8 * 64:(h8 + 1) * 64], start=True, stop=True)
                    S.mm(ps[1][:, :], bcT[:, 2 * g + 1, cs], stT_b[:, g, :])
                    S.tt(DVE, v3(y1[:]), v3(ps[1][:, :]), hb(C_EACS), ALU.mult)
                    S.tt(DVE, y1[:], y1[:], ps[0][:, :], ALU.add)
                    o_d, _ = RV["m2_d"]
                    dbc = rv[:, o_d + 8 * g:o_d + 8 * g + 8].unsqueeze(2).to_broadcast([128, 8, 64])
                    S.tt(POOL, v3(y2[:]), xv, dbc, ALU.mult)
                    S.tt(POOL, y2[:], y2[:], y1[:], ALU.add)
                    S.tt(DVE, y2[:], y2[:], zs_tm[:, tt, :], ALU.mult)
                    r = rms_stats(y2[:], 4, 1.0 / 512)
                    S.act(y_b[:], y2[:], AF.Copy, scale=r)
                    trb = ps[2][:].bitcast(BF16)
                    for i in range(4):
                        S.tr(trb[:, i * 128:(i + 1) * 128], y_b[:, i * 128:(i + 1) * 128], ident_b[:])
                    for i in range(4):
                        dst = bigT[:, 16 + 4 * g + i, cs]
                        if i % 2 == 0:
                            S.ts(DVE, dst, trb[:, i * 128:(i + 1) * 128], ppc("m2nw", 4 * g + i), None, ALU.mult)
                        else:
                            S.act(dst, trb[:, i * 128:(i + 1) * 128], AF.Copy, scale=ppc("m2nw", 4 * g + i))
                    S.mm(ps[3][:, :], btm[:, tt, g * 128:(g + 1) * 128], xdtd_b[:])
                    S.tt(DVE, v3(stT[:, g, :]), v3(stT[:, g, :]), hb(C_EALAST), ALU.mult)
                    S.tt(DVE, stT[:, g, :], stT[:, g, :], ps[3][:, :], ALU.add)
                    S.copy(ACT, stT_b[:, g, :], stT[:, g, :])

        store_streams = []
        for bi, (tok0, tb) in enumerate(blocks):
            ntl = tb // 128
            for tt in range(ntl):
                a0 = tok0 + tt * 128
                r0 = a0 - NMETA
                lo, hi = max(r0, 0), min(r0 + 128, seq)
                if a0 == 0:
                    S.dma(SP, h[0:NMETA, tt, :], meta[:, :], f"h{tt}")
                    S.dma(SP, h[NMETA:128, tt, :], x[0:128 - NMETA, :], f"h{tt}")
                else:
                    if hi - lo < 128:
                        S.memset(POOL, h[:, tt, :], 0.0)
                    if hi > lo:
                        S.dma(SP, h[0:hi - lo, tt, :], x[lo:hi, :], f"h{tt}")
            if do_gdn or do_m2:
                rmsnorm_to_T("nw_mix", tb)
                small_proj(tb)
                if do_gdn:
                    gdn_all(tb)
                else:
                    S.memset(POOL, bigT[:, 0:16, :], 0.0)
                if do_m2:
                    mamba(tb)
                else:
                    S.memset(POOL, bigT[:, 16:32, :], 0.0)
                for cb in range(4):
                    for kg in range(2):
                        wv = next_w("wo")
                        for tt in range(ntl):
                            for k in range(16):
                                kc = kg * 16 + k
                                S.mm(ps[tt][:, :], bigT[:, kc, tt * 128:(tt + 1) * 128], wv[:, k, :], start=(kc == 0), stop=(kc == 31))
                    for tt in range(ntl):
                        S.tt(DVE, h[:, tt, cb * 512:(cb + 1) * 512], h[:, tt, cb * 512:(cb + 1) * 512], ps[tt][:, :], ALU.add)
            if do_ffn:
                rmsnorm_to_T("nw_ffn", tb)
                for u in range(22):
                    wv = next_w("up")
                    for half in range(2):
                        f_g = 2 * u + half
                        for which in range(2):
                            idx = half * 2 + which
                            pst = ps[idx]
                            co = which * 256 + half * 128
                            for kc in range(16):
                                S.mm(pst[:, 0:tb], wv[:, kc, co:co + 128], hnT[:, kc, 0:tb], start=(kc == 0), stop=(kc == 15))
                            conv_chunk(pst[:, 0:tb], tb, pre[idx], cv[idx], tails_ff, f_g + which * 44, "cw_ff", 3)
                        cg, cvv = cv[half * 2], cv[half * 2 + 1]
                        S.act(cg[:, 0:tb], cg[:, 0:tb], AF.Silu)
                        S.tt(DVE, bigT[:, f_g, 0:tb], cg[:, 0:tb], cvv[:, 0:tb], ALU.mult)
                for cb in range(4):
                    for kg in range(3):
                        wv = next_w("dn")
                        nk = 16 if kg < 2 else 12
                        for tt in range(ntl):
                            for k in range(nk):
                                kc = kg * 16 + k
                                S.mm(ps[tt][:, :], bigT[:, kc, tt * 128:(tt + 1) * 128], wv[:, k, :], start=(kc == 0), stop=(kc == 43))
                    for tt in range(ntl):
                        S.tt(DVE, h[:, tt, cb * 512:(cb + 1) * 512], h[:, tt, cb * 512:(cb + 1) * 512], ps[tt][:, :], ALU.add)
            for tt in range(ntl):
                r = rms_stats(h[:, tt, :], tt, 1.0 / D)
                S.stt(h[:, tt, :], h[:, tt, :], r, rvc("nwf"), ALU.mult, ALU.mult)
                a0 = tok0 + tt * 128
                r0 = a0 - NMETA
                lo, hi = max(r0, 0), min(r0 + 128, seq)
                if hi > lo:
                    S.dma(SP, out[lo:hi, :], h[lo - r0:hi - r0, tt, :], f"o{tt}")
                    if f"o{tt}" not in store_streams:
                        store_streams.append(f"o{tt}")
        S.emit(store_streams)
        print("ops", S.stats)
    return nc


_NC_CACHE = {}


def kernel(**inputs):
    inp = {k: np.asarray(v) for k, v in inputs.items()}
    x = inp["x"]
    B = x.shape[0]
    W = prep_weights(inp)
    if "nc" not in _NC_CACHE:
        _NC_CACHE["nc"] = build(seq=SEQ)
    nc = _NC_CACHE["nc"]
    in_maps = []
    for b in range(B):
        in_maps.append(dict(x=np.ascontiguousarray(x[b], dtype=np.float32), meta=W["meta"], wbig=W["wbig"],
                            wsm=W["wsm"], pp=W["pp"], rv=W["rv"]))
    res = run_bass_kernel_spmd(nc, in_maps, core_ids=list(range(B)))
    return np.stack([np.asarray(r["out"], dtype=np.float32) for r in res.results], axis=0)
```

```python
import numpy as np
import concourse.bass as bass
import concourse.mybir as mybir

F32 = mybir.dt.float32
BF16 = mybir.dt.bfloat16
AF = mybir.ActivationFunctionType
ALU = mybir.AluOpType

PE, ACT, DVE, POOL, SP = "pe", "act", "dve", "pool", "sp"
ENGS = [PE, ACT, DVE, POOL, SP]


class Acc:
    __slots__ = ("op", "eng", "w", "p0", "p1", "f0", "f1")

    def __init__(self, op, eng, w, p0, p1, f0, f1):
        self.op = op; self.eng = eng; self.w = w
        self.p0 = p0; self.p1 = p1; self.f0 = f0; self.f1 = f1


def region(ap):
    sp = str(ap.space)
    if "DRAM" in sp.upper():
        return None
    t = ap.tensor
    a = ap.ap
    ps = a[0][0]
    off = ap.offset
    if ps > 0:
        p0 = off // ps
        f0 = off % ps
    else:
        p0 = 0
        f0 = off
    p1 = p0 + a[0][1]
    ext = 0
    for st, cnt in a[1:]:
        ext += abs(st) * (cnt - 1)
    f1 = f0 + ext + 1
    return (t.name, "PSUM" in sp.upper() or sp.upper().startswith("PS"), p0, p1, f0, f1)


def _fsize(ap):
    n = 1
    for st, cnt in ap.ap[1:]:
        n *= cnt
    return n


def _vcost(eng, out):
    n = _fsize(out)
    if eng == POOL:
        return 250.0 + 1.1 * n
    return 110.0 + n / 0.96


class Sched:
    def __init__(self, nc):
        self.nc = nc
        self.ops = []
        self.hist = {}
        self.dma_count = {}
        self.cap = None
        self.eng_free = {}

    def _regions(self, reads, writes):
        out = []
        for ap in reads:
            if ap is not None:
                rg = region(ap)
                if rg is not None:
                    out.append((rg, False))
        for ap in writes:
            if ap is not None:
                rg = region(ap)
                if rg is not None:
                    out.append((rg, True))
        return out

    def _analyze(self, eng, regs, oid, commit):
        deps = set()
        for (name, is_ps, p0, p1, f0, f1), is_w in regs:
            lst = self.hist.get(name)
            if lst is None:
                lst = []
                if commit:
                    self.hist[name] = lst
            keep = []
            for a in lst:
                if a.op == oid:
                    keep.append(a)
                    continue
                overlap = not (a.p1 <= p0 or p1 <= a.p0 or a.f1 <= f0 or f1 <= a.f0)
                if is_ps and a.eng != eng:
                    conflict = True
                else:
                    conflict = overlap and (is_w or a.w)
                if conflict and not (a.eng == PE and eng == PE):
                    deps.add(a.op)
                covered = is_w and p0 <= a.p0 and a.p1 <= p1 and f0 <= a.f0 and a.f1 <= f1
                if covered or (is_ps and a.eng != eng):
                    continue
                keep.append(a)
            if commit:
                keep.append(Acc(oid, eng, is_w, p0, p1, f0, f1))
                self.hist[name] = keep
        best = {}
        nd = set()
        for d in deps:
            dop = self.ops[d]
            if dop["stream"] is not None:
                nd.add(d)
            else:
                e2 = dop["eng"]
                if best.get(e2, -1) < d:
                    best[e2] = d
        nd.update(best.values())
        return nd

    def _est_start(self, eng, deps):
        t = self.eng_free.get(eng, 0.0)
        for d in deps:
            dop = self.ops[d]
            lat = 60.0 if dop["eng"] == eng else 180.0
            if dop["fin"] + lat > t:
                t = dop["fin"] + lat
        return t

    def add(self, eng, fn, reads, writes, stream=None, cost=300.0):
        if self.cap is not None:
            self.cap.append((eng, fn, self._regions(reads, writes), stream, cost))
            return None
        return self._commit(eng, fn, self._regions(reads, writes), stream, cost)

    def _commit(self, eng, fn, regs, stream, cost, start=None):
        oid = len(self.ops)
        deps = self._analyze(eng, regs, oid, True)
        if start is None:
            start = self._est_start(eng, deps)
        if stream is not None:
            self.eng_free[eng] = start + 500.0
        else:
            self.eng_free[eng] = start + cost
        op = dict(id=oid, eng=eng, fn=fn, deps=deps, stream=stream, sig=False, dn=None, fin=start + cost)
        if stream is not None:
            n = self.dma_count.get(stream, 0) + 1
            self.dma_count[stream] = n
            op["dn"] = n
        self.ops.append(op)
        return oid

    def capture(self, f):
        assert self.cap is None
        self.cap = []
        try:
            f()
            out = self.cap
        finally:
            self.cap = None
        return out

    def merge(self, streams, prereq=None):
        n = len(streams)
        idx = [0] * n
        done = [len(streams[i]) == 0 for i in range(n)]
        remaining = sum(1 for d in done if not d)
        while remaining:
            best = None
            for i in range(n):
                if done[i]:
                    continue
                if prereq is not None and idx[i] == 0 and any(not done[j] for j in prereq[i]):
                    continue
                eng, fn, regs, stream, cost = streams[i][idx[i]]
                deps = self._analyze(eng, regs, -1, False)
                st = self._est_start(eng, deps)
                if best is None or st < best[0]:
                    best = (st, i)
            st, i = best
            eng, fn, regs, stream, cost = streams[i][idx[i]]
            self._commit(eng, fn, regs, stream, cost, start=st)
            idx[i] += 1
            if idx[i] >= len(streams[i]):
                done[i] = True
                remaining -= 1

    def emit(self, final_wait_streams):
        nc = self.nc
        ops = self.ops
        for op in ops:
            for d in op["deps"]:
                ops[d]["sig"] = True
        cnt = {e: 0 for e in ENGS}
        for op in ops:
            if op["stream"] is None and op["sig"]:
                cnt[op["eng"]] += 1
                op["sv"] = cnt[op["eng"]]
        from contextlib import ExitStack
        with ExitStack() as es:
            sems = {e: es.enter_context(nc.semaphore("sem_" + e)) for e in ENGS}
            dsems = {s: es.enter_context(nc.semaphore("dsem_" + s)) for s in self.dma_count}
            block = es.enter_context(nc.Block())
            per_eng = {e: [op for op in ops if op["eng"] == e] for e in ENGS}

            def run(eng_name, eng):
                waited = {}
                for op in per_eng[eng_name]:
                    need = {}
                    for d in op["deps"]:
                        dop = ops[d]
                        if dop["stream"] is not None:
                            key = ("d", dop["stream"]); val = 16 * dop["dn"]
                        else:
                            key = ("e", dop["eng"]); val = dop["sv"]
                        if need.get(key, 0) < val:
                            need[key] = val
                    for key, val in need.items():
                        if waited.get(key, 0) >= val:
                            continue
                        waited[key] = val
                        sem = dsems[key[1]] if key[0] == "d" else sems[key[1]]
                        eng.wait_ge(sem, val)
                    ins = op["fn"](eng)
                    if op["stream"] is not None:
                        ins.then_inc(dsems[op["stream"]], 16)
                    elif op["sig"]:
                        ins.then_inc(sems[eng_name], 1)
                if eng_name == SP:
                    for s in final_wait_streams:
                        eng.wait_ge(dsems[s], 16 * self.dma_count[s])

            block.tensor(lambda e: run(PE, e))
            block.scalar(lambda e: run(ACT, e))
            block.vector(lambda e: run(DVE, e))
            block.gpsimd(lambda e: run(POOL, e))
            block.sync(lambda e: run(SP, e))
        self.stats = {e: len(per_eng[e]) for e in ENGS}
        self.stats["sig"] = dict(cnt)

    def mm(self, out, lhsT, rhs, start=True, stop=True):
        n = _fsize(out)
        c = max(110.0, n * (4 if lhsT.dtype == F32 else 1) / 1.95 + 12)
        return self.add(PE, lambda e: e.matmul(out, lhsT=lhsT, rhs=rhs, start=start, stop=stop,
                                               skip_group_check=True),
                        [lhsT, rhs], [out], cost=c)

    def tr(self, out, in_, ident):
        return self.add(PE, lambda e: e.transpose(out, in_, ident), [in_, ident], [out], cost=70.0)

    def act(self, out, in_, func, bias=None, scale=None, accum_out=None, eng=ACT):
        kw = {}
        rd = [in_]
        if bias is not None:
            kw["bias"] = bias
            if not isinstance(bias, (int, float)):
                rd.append(bias)
        if scale is not None:
            kw["scale"] = scale
            if not isinstance(scale, (int, float)):
                rd.append(scale)
        wr = [out]
        if accum_out is not None:
            kw["accum_out"] = accum_out
            wr.append(accum_out)
        return self.add(ACT, lambda e: e.activation(out, in_, func, **kw), rd, wr, cost=200 + 0.83 * _fsize(out))

    def ts(self, eng, out, in0, s1, s2, op0, op1=None, accum_out=None):
        rd = [in0]
        if not isinstance(s1, (int, float)):
            rd.append(s1)
        if s2 is not None and not isinstance(s2, (int, float)):
            rd.append(s2)
        kw = {}
        wr = [out]
        if op1 is not None:
            kw["op1"] = op1
        if accum_out is not None:
            kw["accum_out"] = accum_out
            wr.append(accum_out)
        return self.add(eng, lambda e: e.tensor_scalar(out, in0, s1, s2, op0, **kw), rd, wr, cost=_vcost(eng, out))

    def stt(self, out, in0, scalar, in1, op0, op1, eng=DVE):
        rd = [in0, in1]
        if not isinstance(scalar, (int, float)):
            rd.append(scalar)
        return self.add(eng, lambda e: e.scalar_tensor_tensor(out, in0, scalar, in1, op0, op1), rd, [out], cost=_vcost(eng, out))

    def tt(self, eng, out, in0, in1, op):
        return self.add(eng, lambda e: e.tensor_tensor(out, in0, in1, op), [in0, in1], [out], cost=_vcost(eng, out))

    def copy(self, eng, out, in_):
        if eng == ACT:
            return self.add(ACT, lambda e: e.copy(out, in_), [in_], [out], cost=200 + 0.83 * _fsize(out))
        return self.add(eng, lambda e: e.tensor_copy(out, in_), [in_], [out], cost=_vcost(eng, out))

    def memset(self, eng, out, val):
        return self.add(eng, lambda e: e.memset(out, val), [], [out], cost=_vcost(eng, out))

    def dma(self, queue, out, in_, stream):
        return self.add(queue, lambda e: e.dma_start(out=out, in_=in_), [in_], [out], stream=stream,
                        cost=2500.0 + _fsize(out) * out.ap[0][1] * 4 / 150.0)


import numpy as np
import concourse.bass as bass
import concourse.mybir as mybir
from concourse.bass_utils import run_bass_kernel_spmd

D = 2048
KC = 16
SEQ = 4096
NMETA = 16
DFF = 5632
EPS = 1e-6
GW = 8192

OQ, OK_, OV, OZ, OB, OA, OMZ, OXS, OBM, OCM, ODT = 0, 2048, 4096, 6144, 8192, 8208, 8224, 10272, 12320, 12832, 13344


def group_list():
    gl = []
    for h in range(16):
        gl.append(("gdn", h))
    gl.append(("mbc", 0)); gl.append(("mbc", 1))
    for g in range(4):
        gl.append(("mx", g)); gl.append(("mz", g))
    for cb in range(4):
        for kg in range(2):
            gl.append(("wo", cb, kg))
    for u in range(22):
        gl.append(("up", u))
    for cb in range(4):
        for kg in range(3):
            gl.append(("dn", cb, kg))
    return gl


GL = group_list()
GIDX = {g: i for i, g in enumerate(GL)}
NG = len(GL)

PP = {}
_o = 0
for nm, n in [("nw_mix", 16), ("nw_ffn", 16), ("m2nw", 16), ("dnnw", 1), ("cw_dn", 48 * 4), ("cw_m2", 24 * 4),
              ("cb_m2", 24), ("cw_ff", 88 * 3)]:
    PP[nm] = (_o, n); _o += n
NPP = _o
RV = {}
_o = 0
for nm, n in [("nwf", 2048), ("dn_alog", 16), ("dn_dtb", 16), ("m2_alog", 32), ("m2_dtb", 32), ("m2_d", 32)]:
    RV[nm] = (_o, n); _o += n
NRV = _o


def prep_weights(inp):
    w_in = np.asarray(inp["w_in"][0]); w_out = np.asarray(inp["w_out"][0])
    up = np.asarray(inp["ffn_up"][0]); dn = np.asarray(inp["ffn_down"][0])
    wbig = np.zeros((NG, 128, GW), np.float32)

    def put(gi, W, rows0, nk, cols):
        blk = W[rows0:rows0 + nk * 128][:, cols]
        blk = blk.reshape(nk, 128, len(cols)).transpose(1, 0, 2)
        wbig[gi, :, :nk * 512] = blk.reshape(128, nk * 512)

    ar = np.arange
    for gi, g in enumerate(GL):
        if g[0] == "gdn":
            h = g[1]
            cols = np.concatenate([OQ + h * 128 + ar(128), OK_ + h * 128 + ar(128), OV + h * 128 + ar(128), OZ + h * 128 + ar(128)])
            put(gi, w_in, 0, 16, cols)
        elif g[0] == "mbc":
            p = g[1]
            cols = np.concatenate([OBM + (2 * p) * 128 + ar(128), OCM + (2 * p) * 128 + ar(128),
                                   OBM + (2 * p + 1) * 128 + ar(128), OCM + (2 * p + 1) * 128 + ar(128)])
            put(gi, w_in, 0, 16, cols)
        elif g[0] == "mx":
            put(gi, w_in, 0, 16, OXS + g[1] * 512 + ar(512))
        elif g[0] == "mz":
            put(gi, w_in, 0, 16, OMZ + g[1] * 512 + ar(512))
        elif g[0] == "wo":
            put(gi, w_out, g[2] * 2048, 16, g[1] * 512 + ar(512))
        elif g[0] == "up":
            u = g[1]
            cols = np.concatenate([u * 256 + ar(256), DFF + u * 256 + ar(256)])
            put(gi, up, 0, 16, cols)
        elif g[0] == "dn":
            kg = g[2]
            nk = 16 if kg < 2 else 12
            put(gi, dn, kg * 2048, nk, g[1] * 512 + ar(512))
    cols = np.concatenate([OB + ar(16), OA + ar(16), ODT + ar(32)])
    wsm = w_in[:, cols].reshape(16, 128, 64).transpose(1, 0, 2).reshape(128, 16 * 64).copy()
    pp = np.zeros((128, NPP), np.float32)

    def fm(v):
        return np.asarray(v).reshape(-1, 128).T

    def setp(nm, a):
        o, n = PP[nm]; pp[:, o:o + n] = a.reshape(128, n)
    setp("nw_mix", fm(inp["norm_mix_w"][0])); setp("nw_ffn", fm(inp["norm_ffn_w"][0]))
    setp("m2nw", fm(inp["m2_norm_w"][0])); setp("dnnw", np.asarray(inp["dn_norm_w"][0]).reshape(128, 1))

    def cw(w):
        w = np.asarray(w); K, C = w.shape
        return w.reshape(K, C // 128, 128).transpose(2, 1, 0).reshape(128, -1)
    setp("cw_dn", cw(inp["dn_conv_w"][0])); setp("cw_m2", cw(inp["m2_conv_w"][0]))
    setp("cb_m2", fm(inp["m2_conv_b"][0])); setp("cw_ff", cw(inp["ffn_conv_w"][0]))
    rv = np.zeros((1, NRV), np.float32)

    def setr(nm, a):
        o, n = RV[nm]; rv[0, o:o + n] = np.asarray(a).reshape(n)
    setr("nwf", inp["norm_final_w"]); setr("dn_alog", inp["dn_a_log"][0]); setr("dn_dtb", inp["dn_dt_bias"][0])
    setr("m2_alog", inp["m2_a_log"][0]); setr("m2_dtb", inp["m2_dt_bias"][0]); setr("m2_d", inp["m2_d"][0])
    return dict(wbig=wbig, wsm=wsm, pp=pp, rv=rv, meta=np.asarray(inp["meta_tokens"], np.float32))


import math

C_RAW, C_LNB, C_BETA, C_G, C_AM, C_GC, C_ACS, C_GLAST, C_ALAST = 0, 64, 80, 96, 112, 144, 160, 192, 208
C_EGC, C_EACS, C_EGLAST, C_EALAST, C_ED, C_ED2, C_BEGE, C_DT, C_TMP = 240, 256, 288, 304, 336, 352, 384, 400, 432
C_STA, C_STB = 432, 512
NS = 576


def build(seq=SEQ, T=384, do_gdn=True, do_m2=True, NWB=2, do_ffn=True):
    ntok = NMETA + seq
    nblk = (ntok + T - 1) // T
    blocks = []
    t0 = 0
    for b in range(nblk):
        tb = min(T, ((ntok - t0 + 127) // 128) * 128)
        blocks.append((t0, tb)); t0 += tb
    nc = bass.Bass("TRN2", target_bir_lowering=False)
    x = nc.dram_tensor("x", [seq, D], F32, kind="ExternalInput").ap()
    meta = nc.dram_tensor("meta", [NMETA, D], F32, kind="ExternalInput").ap()
    wbig = nc.dram_tensor("wbig", [NG, 128, GW], F32, kind="ExternalInput").ap()
    wsm_d = nc.dram_tensor("wsm", [128, 16 * 64], F32, kind="ExternalInput").ap()
    pp_d = nc.dram_tensor("pp", [128, NPP], F32, kind="ExternalInput").ap()
    rv_d = nc.dram_tensor("rv", [1, NRV], F32, kind="ExternalInput").ap()
    out = nc.dram_tensor("out", [seq, D], F32, kind="ExternalOutput").ap()
    NT = T // 128
    from contextlib import ExitStack
    with ExitStack() as es:
        def sb(name, shape, dt=F32):
            return es.enter_context(nc.sbuf_tensor(name, shape, dt))

        def psb(name, shape, dt=F32):
            return es.enter_context(nc.psum_tensor(name, shape, dt))
        S = Sched(nc)
        ident_f = sb("ident_f", [128, 128]); ident_b = sb("ident_b", [128, 128], BF16)
        ones_f = sb("ones_f", [128, 128]); ones_b = sb("ones_b", [128, 128], BF16)
        nones2 = sb("nones2", [128, 256])
        UT_f = sb("UT_f", [128, 128])
        masks = sb("masks", [128, 256])
        pp = sb("pp_sb", [128, NPP]); rv = sb("rv_sb", [128, NRV])
        nA = sb("nA", [128, 48])
        wsm_b = sb("wsm_b", [128, 16 * 64], BF16)
        h = sb("h", [128, NT, D])
        hnT = sb("hnT", [128, KC, T], BF16)
        bigT = sb("bigT", [128, 44, T], BF16)
        wb = [sb(f"wb{i}", [128, GW], BF16) for i in range(NWB)]
        hn_tm = sb("hn_tm", [128, D], BF16)
        stat = sb("stat", [128, 32])
        tails_ff = sb("tails_ff", [128, 88, 3])
        tails_dn = sb("tails_dn", [128, 48, 3])
        tails_m2 = sb("tails_m2", [128, 24, 3])
        pre = [sb(f"pre{i}", [128, 3 + T]) for i in range(4)]
        cv = [sb(f"cv{i}", [128, T]) for i in range(4)]
        ft = [sb(f"ft{i}", [128, 512]) for i in range(6)]
        bt = [None] * 4 + [sb(f"bt{i}", [128, 512], BF16) for i in range(4, 9)]
        negmask2 = sb("negmask2", [128, 256])
        Sst = sb("Sst", [128, 16, 128]); S_b = sb("S_b", [128, 16, 128], BF16)
        stT = sb("stT", [128, 4, 512]); stT_b = sb("stT_b", [128, 4, 512], BF16)
        tokS = [sb(f"tokS{i}", [128, NS]) for i in range(NT)]
        glT = sb("glT", [64, T], BF16); nglT = sb("nglT", [32, T], BF16); egcT = sb("egcT", [32, T], BF16)
        acsT = sb("acsT", [64, T], BF16); nacsT = sb("nacsT", [64, T], BF16)
        stg = sb("stg", [128, 256], BF16); stgf = sb("stgf", [128, 128])
        sels = sb("sels", [64, 96], BF16)
        bcT = sb("bcT", [128, 8, T], BF16)
        btm = sb("btm", [128, NT, 512], BF16)
        xs_tm = sb("xs_tm", [128, NT, 512], BF16)
        zs_tm = sb("zs_tm", [128, NT, 512], BF16)
        ps = [psb(f"ps{i}", [128, 512]) for i in range(8)]
        print("sbuf remaining", nc.sbuf_bytes_remaining)

        S.memset(POOL, ones_f[:], 1.0)
        S.memset(POOL, nones2[:], -1.0)
        S.memset(POOL, ones_b[:], 1.0)

        def asel(out_ap, in_ap, cmp, base):
            S.add(POOL, lambda e: e.affine_select(out_ap, in_ap, [[1, 128]], cmp, 0.0, base=base,
                                                  channel_multiplier=-1), [in_ap], [out_ap])
        asel(ident_f[:], ones_f[:], ALU.is_equal, 0)
        asel(UT_f[:], ones_f[:], ALU.is_ge, 0)
        S.copy(POOL, masks[:, 0:128], UT_f[:])
        asel(masks[:, 128:256], nones2[:, 0:128], ALU.is_ge, -1)
        S.copy(POOL, ident_b[:], ident_f[:])
        S.ts(DVE, negmask2[:, 0:128], masks[:, 0:128], -1.0, 30000.0, ALU.add, ALU.mult)
        S.ts(DVE, negmask2[:, 128:256], masks[:, 128:256], 1.0, -30000.0, ALU.add, ALU.mult)
        S.tt(DVE, sels[0:64, 0:16], ident_b[0:64, 0:16], ident_b[0:64, 32:48], ALU.add)
        S.tt(DVE, sels[0:64, 16:32], ident_b[0:64, 16:32], ident_b[0:64, 48:64], ALU.add)
        S.tt(DVE, sels[0:32, 32:48], ident_b[0:32, 0:16], ident_b[0:32, 16:32], ALU.add)
        S.tt(DVE, sels[0:64, 48:80], ident_b[0:64, 0:32], ident_b[0:64, 32:64], ALU.add)
        S.dma(SP, pp[:], pp_d[:, :], "pp")
        S.dma(SP, rv[:], rv_d[0:1, :].to_broadcast([128, NRV]), "rv")
        S.dma(POOL, wsm_b[:], wsm_d[:, :], "wsm")
        for t_ in (tails_ff, tails_dn, tails_m2, Sst, S_b, stT, stT_b):
            S.memset(POOL, t_[:], 0.0)

        def ppc(nm, i=0, n=1):
            o, _ = PP[nm]
            return pp[:, o + i:o + i + n]

        def rvc(nm):
            o, n = RV[nm]
            return rv[:, o:o + n]
        ncb = sb("ncb", [128, 24])
        S.ts(DVE, ncb[:], ppc("cb_m2", 0, 24), -1.0, None, ALU.mult)
        S.act(nA[:, 0:16], rvc("dn_alog"), AF.Exp)
        S.act(nA[:, 16:48], rvc("m2_alog"), AF.Exp)
        S.ts(DVE, nA[:], nA[:], -1.0, None, ALU.mult)

        wstate = dict(next=0)
        total_groups = []

        def plan_groups(bi):
            return [g for g in GL if (g[0] in ("up", "dn") and do_ffn) or (g[0] == "wo" and (do_gdn or do_m2))
                    or (g[0] == "gdn" and do_gdn) or (g[0] in ("mbc", "mx", "mz") and do_m2)]
        for bi in range(nblk):
            total_groups += plan_groups(bi)

        def issue_w(n):
            if n >= len(total_groups):
                return
            g = total_groups[n]
            nk = 12 if (g[0] == "dn" and g[2] == 2) else 16
            S.dma(POOL, wb[n % NWB][:, 0:nk * 512], wbig[GIDX[g], :, 0:nk * 512], f"w{n % NWB}")

        for n in range(NWB - 1):
            issue_w(n)

        def next_w(expect):
            n = wstate["next"]
            assert total_groups[n][0] == expect, (total_groups[n], expect)
            issue_w(n + NWB - 1)
            wstate["next"] = n + 1
            return wb[n % NWB][:].rearrange("p (k c) -> p k c", k=16)

        def rms_stats(src, tt, scale):
            c0 = 3 * tt
            S.act(hn_tm[:, 0:src.shape[1]], src, AF.Square, accum_out=stat[:, c0:c0 + 1])
            S.act(stat[:, c0 + 1:c0 + 2], stat[:, c0:c0 + 1], AF.Ln, bias=EPS, scale=scale)
            S.act(stat[:, c0 + 2:c0 + 3], stat[:, c0 + 1:c0 + 2], AF.Exp, scale=-0.5)
            return stat[:, c0 + 2:c0 + 3]

        def rmsnorm_to_T(nw_name, tb):
            ntl = tb // 128
            for tt in range(ntl):
                r = rms_stats(h[:, tt, :], tt, 1.0 / D)
                S.ts(DVE, hn_tm[:], h[:, tt, :], r, None, ALU.mult)
                for kq in range(4):
                    pv = ps[6 + (kq % 2)][:].bitcast(BF16)
                    for j in range(4):
                        kc = kq * 4 + j
                        S.tr(pv[:, j * 128:(j + 1) * 128], hn_tm[:, kc * 128:(kc + 1) * 128], ident_b[:])
                    for j in range(4):
                        kc = kq * 4 + j
                        dst = hnT[:, kc, tt * 128:(tt + 1) * 128]
                        if j % 2 == 0:
                            S.ts(DVE, dst, pv[:, j * 128:(j + 1) * 128], ppc(nw_name, kc), None, ALU.mult)
                        else:
                            S.act(dst, pv[:, j * 128:(j + 1) * 128], AF.Copy, scale=ppc(nw_name, kc))

        def silu_exp(out, src, tmp, bias=None, nbias=None):
            if bias is None:
                S.act(tmp, src, AF.Exp, scale=-1.0)
            else:
                S.act(tmp, src, AF.Exp, scale=-1.0, bias=nbias)
            S.act(tmp, tmp, AF.Ln, bias=1.0)
            S.act(tmp, tmp, AF.Exp, scale=-1.0)
            if bias is None:
                S.tt(DVE, out, src, tmp, ALU.mult)
            else:
                S.stt(out, src, bias, tmp, ALU.add, ALU.mult)

        def conv_chunk(psrc, tb, pr, c, tails, ch, cwname, K):
            S.copy(POOL, pr[:, 0:3], tails[:, ch, :])
            S.act(pr[:, 3:3 + tb], psrc, AF.Copy)
            S.copy(POOL, tails[:, ch, :], pr[:, tb:tb + 3])
            o, _ = PP[cwname]
            b0 = 4 - K
            S.ts(DVE, c[:, 0:tb], pr[:, b0:b0 + tb], pp[:, o + ch * K:o + ch * K + 1], None, ALU.mult)
            for j in range(1, K):
                S.stt(c[:, 0:tb], pr[:, b0 + j:b0 + j + tb], pp[:, o + ch * K + j:o + ch * K + j + 1], c[:, 0:tb],
                      ALU.mult, ALU.add)

        def small_proj(tb):
            ntl = tb // 128
            for tt in range(ntl):
                tk = tokS[tt]
                cs = slice(tt * 128, (tt + 1) * 128)
                for kc in range(16):
                    S.mm(ps[7][:, 0:64], hnT[:, kc, cs], wsm_b[:, kc * 64:(kc + 1) * 64], start=(kc == 0), stop=(kc == 15))
                S.copy(ACT, tk[:, 0:64], ps[7][:, 0:64])
                tmp = tk[:, C_TMP:C_TMP + 96]
                S.act(tmp[:, 0:16], tk[:, 0:16], AF.Exp, scale=-1.0)
                S.tt(DVE, tmp[:, 16:32], tk[:, 16:32], rvc("dn_dtb"), ALU.add)
                S.tt(DVE, tmp[:, 32:64], tk[:, 32:64], rvc("m2_dtb"), ALU.add)
                S.act(tmp[:, 16:64], tmp[:, 16:64], AF.Exp)
                S.act(tmp[:, 0:64], tmp[:, 0:64], AF.Ln, bias=1.0)
                S.ts(DVE, tk[:, C_LNB:C_LNB + 16], tmp[:, 0:16], -1.0, None, ALU.mult)
                S.act(tk[:, C_BETA:C_BETA + 16], tk[:, C_LNB:C_LNB + 16], AF.Exp)
                S.tt(DVE, tk[:, C_G:C_G + 16], tmp[:, 16:32], nA[:, 0:16], ALU.mult)
                S.copy(DVE, tk[:, C_DT:C_DT + 32], tmp[:, 32:64])
                S.tt(DVE, tk[:, C_AM:C_AM + 32], tmp[:, 32:64], nA[:, 16:48], ALU.mult)
                S.mm(ps[7][:, 64:112], UT_f[:], tk[:, C_G:C_G + 48])
                S.mm(ps[7][:, 112:160], ones_f[:], tk[:, C_G:C_G + 48])
                S.copy(ACT, tk[:, C_GC:C_GC + 96], ps[7][:, 64:160])
                S.act(tk[:, C_EGC:C_EGC + 96], tk[:, C_GC:C_GC + 96], AF.Exp)
                S.tt(DVE, tmp[:, 0:48], tk[:, C_GLAST:C_GLAST + 48], tk[:, C_GC:C_GC + 48], ALU.subtract)
                S.act(tk[:, C_ED:C_ED + 48], tmp[:, 0:48], AF.Exp)
                S.tt(DVE, tk[:, C_BEGE:C_BEGE + 16], tk[:, C_BETA:C_BETA + 16], tk[:, C_EGC:C_EGC + 16], ALU.mult)
                stA = tk[:, C_STA:C_STA + 80]; stB = tk[:, C_STB:C_STB + 64]
                S.copy(DVE, stA[:, 0:16], tk[:, C_GC:C_GC + 16])
                S.tt(DVE, stA[:, 16:32], tk[:, C_GC:C_GC + 16], tk[:, C_LNB:C_LNB + 16], ALU.add)
                S.ts(DVE, stA[:, 32:48], tk[:, C_GC:C_GC + 16], -1.0, None, ALU.mult)
                S.memset(DVE, stA[:, 48:64], 0.0)
                S.copy(DVE, stA[:, 64:80], tk[:, C_EGC:C_EGC + 16])
                S.copy(DVE, stB[:, 0:32], tk[:, C_ACS:C_ACS + 32])
                S.ts(DVE, stB[:, 32:64], tk[:, C_ACS:C_ACS + 32], -1.0, None, ALU.mult)
                srcs = [(stA[:, 0:32], 32, glT), (stA[:, 32:48], 16, nglT), (stA[:, 64:80], 16, egcT),
                        (stB[:, 0:32], 32, acsT), (stB[:, 32:64], 32, nacsT)]
                o = 0
                for (src, n, dst) in srcs:
                    hi = stg[:, o:o + n]; lo = stg[:, o + n:o + 2 * n]
                    S.copy(DVE, hi, src)
                    S.tt(DVE, lo, src, hi, ALU.subtract)
                    pvb = ps[7][:].bitcast(BF16)
                    S.tr(pvb[0:2 * n, 0:128], stg[:, o:o + 2 * n], ident_b[:])
                    S.copy(ACT, dst[0:2 * n, cs], pvb[0:2 * n, 0:128])
                    o += 2 * n

        def run_rr(gens):
            gens = list(gens)
            while gens:
                nxt = []
                for g_ in gens:
                    try:
                        next(g_); nxt.append(g_)
                    except StopIteration:
                        pass
                gens = nxt

        def slot128(i):
            if i < 12:
                return btm[:, i // 4, (i % 4) * 128:(i % 4 + 1) * 128]
            i -= 12
            return zs_tm[:, i // 4, (i % 4) * 128:(i % 4 + 1) * 128]

        def xp_tile(c, i):
            k = 3 * c + i
            return bcT[:, k, 0:384] if k < 8 else xs_tm[:, 0, 0:384]

        def gdn_prep_block():
            for c in range(3):
                S.copy(POOL, xp_tile(c, 0)[:, 0:128], ident_b[:])

        def gset(sidx):
            b0 = 32 + 4 * sidx
            return dict(qn=bigT[:, b0, :], kn=bigT[:, b0 + 1, :], v=bigT[:, b0 + 2, :], Qg=bigT[:, b0 + 3, :],
                        zsil=bt[6 + sidx])

        def cslot(i):
            return bigT[:, 16 + i // 3, (i % 3) * 128:(i % 3 + 1) * 128]

        def cbufs(c, pset):
            b0 = 24 * pset + 8 * c
            return dict(Kb=cslot(b0), Kd=cslot(b0 + 1), Vb=cslot(b0 + 2), QKm=cslot(b0 + 3),
                        nWT=cslot(b0 + 4), TT=cslot(b0 + 5), Em0=cslot(b0 + 6), Em1=cslot(b0 + 7))

        def gdn_stage1(hd, tb):
            G = gset(hd % 3)
            wv = next_w("gdn")
            PA, PBk = ps[6], ps[7]
            sq_b = bt[4]
            tmpf = ft[0]

            def proj(i, bank):
                for kc in range(16):
                    S.mm(bank[:, 0:tb], wv[:, kc, i * 128:(i + 1) * 128], hnT[:, kc, 0:tb], start=(kc == 0), stop=(kc == 15))
                    if kc % 8 == 7:
                        yield

            def convsilu(i, bank):
                conv_chunk(bank[:, 0:tb], tb, pre[i], cv[i], tails_dn, i * 16 + hd, "cw_dn", 4)
                yield
                S.act(cv[i][:, 0:tb], cv[i][:, 0:tb], AF.Silu)
                yield
            yield from proj(0, PA)
            yield from proj(1, PBk)
            yield from convsilu(0, PA)
            yield from proj(2, PA)
            yield from convsilu(1, PBk)
            yield from proj(3, PBk)
            yield from convsilu(2, PA)
            S.copy(ACT, ft[4][:, 0:tb], PBk[:, 0:tb]) if False else None
            S.act(G["zsil"][:, 0:tb], PBk[:, 0:tb], AF.Silu)
            S.mm(PA[:, 0:tb], sels[0:32, 32 + hd:33 + hd].to_broadcast([32, 128]), egcT[0:32, 0:tb])
            yield
            for i, dstb in enumerate([G["qn"], G["kn"]]):
                S.act(sq_b[:, 0:tb], cv[i][:, 0:tb], AF.Square)
                S.mm(PBk[:, 0:tb], ones_b[:], sq_b[:, 0:tb])
                yield
                S.act(tmpf[:, 0:tb], PBk[:, 0:tb], AF.Ln, bias=EPS)
                S.act(tmpf[:, 0:tb], tmpf[:, 0:tb], AF.Exp, scale=-0.5, bias=(math.log(128 ** -0.5) if i == 0 else 0.0))
                yield
                S.tt(DVE, dstb[:, 0:tb], cv[i][:, 0:tb], tmpf[:, 0:tb], ALU.mult)
                yield
            S.tt(DVE, G["Qg"][:, 0:tb], G["qn"][:, 0:tb], PA[:, 0:tb], ALU.mult)
            S.copy(ACT, G["v"][:, 0:tb], cv[2][:, 0:tb])
            yield

        def gdn_stage2(hd, tb):
            ntl = tb // 128
            G = gset(hd % 3)
            qn_b, kn_b, v_b = G["qn"], G["kn"], G["v"]
            Ecls = [ft[2][:, 0:256], ft[2][:, 256:512], ft[3][:, 0:256]]

            def bufs(c):
                return cbufs(c, hd % 2)

            def chainA(c):
                tt = c
                cs = slice(tt * 128, (tt + 1) * 128)
                tk = tokS[tt]
                B = bufs(c)
                bank = ps[c]
                trb = bank[:].bitcast(BF16)
                ev = ACT if (c % 2 == 0) else DVE

                def col(o):
                    return tk[:, o + hd:o + hd + 1]
                S.tr(trb[:, 768:896], kn_b[:, cs], ident_b[:])
                S.tr(trb[:, 896:1024], v_b[:, cs], ident_b[:])
                S.ts(DVE, B["Kb"], trb[:, 768:896], col(C_BEGE), None, ALU.mult)
                S.act(B["Kd"], trb[:, 768:896], AF.Copy, scale=col(C_ED))
                S.ts(DVE, B["Vb"], trb[:, 896:1024], col(C_BETA), None, ALU.mult)
                yield
                S.mm(bank[:, 0:128], kn_b[:, cs], qn_b[:, cs])
                S.mm(bank[:, 128:256], kn_b[:, cs], kn_b[:, cs])
                S.mm(bank[:, 256:512], nglT[0:32, cs], sels[0:32, 32 + hd:33 + hd].to_broadcast([32, 256]), start=True, stop=False)
                S.mm(bank[:, 256:384], sels[0:64, hd:hd + 1].to_broadcast([64, 128]), glT[0:64, cs], start=False, stop=False)
                S.mm(bank[:, 384:512], sels[0:64, 16 + hd:17 + hd].to_broadcast([64, 128]), glT[0:64, cs], start=False, stop=True)
                yield
                S.tt(DVE, Ecls[c], bank[:, 256:512], negmask2[:], ALU.min)
                yield
                S.act(B["Em0"], Ecls[c][:, 0:128], AF.Exp)
                S.act(B["Em1"], Ecls[c][:, 128:256], AF.Exp)
                yield
                XP = xp_tile(c, 0)
                S.tt(DVE, B["QKm"], bank[:, 0:128], B["Em0"], ALU.mult)
                S.stt(XP[:, 128:256], bank[:, 128:256], -1.0, B["Em1"], ALU.mult, ALU.mult)
                yield
                S.tr(trb[:, 0:128], XP[:, 128:256], ident_b[:])
                S.copy(ev, XP[:, 256:384], trb[:, 0:128])
                yield
                for j in range(7):
                    last = (j == 6)
                    if not last:
                        XPn = xp_tile(c, 1 + (j % 2))
                        S.mm(bank[:, 0:256], XP[:, 256:384], XP[:, 0:256], start=True, stop=False)
                        S.mm(bank[:, 0:128], ident_b[:], XP[:, 0:128], start=False, stop=True)
                        S.mm(bank[:, 256:384], XP[:, 128:256], XP[:, 256:384], start=True, stop=True)
                        S.copy(ev, XPn, bank[:, 0:384])
                        XP = XPn
                    else:
                        S.mm(bank[:, 0:128], XP[:, 256:384], XP[:, 0:128], start=True, stop=False)
                        S.mm(bank[:, 0:128], ident_b[:], XP[:, 0:128], start=False, stop=True)
                        S.copy(ev, B["TT"], bank[:, 0:128])
                    yield
                S.mm(bank[:, 0:128], B["Kb"], B["TT"])
                S.act(B["nWT"], bank[:, 0:128], AF.Copy, scale=-1.0)
                yield

            return [chainA(c) for c in range(ntl)]

        def gdn_stage3(hd, tb):
            ntl = tb // 128
            G = gset(hd % 3)
            Qg_b, zsil = G["Qg"], G["zsil"]
            oT = ps[3]
            vnew_b = btm[:, 0, 0:128]
            for tt in range(ntl):
                cs = slice(tt * 128, (tt + 1) * 128)
                tk = tokS[tt]
                B = cbufs(tt, hd % 2)
                S.mm(ps[4][:, 0:128], B["TT"], B["Vb"], start=True, stop=False)
                S.mm(ps[4][:, 0:128], B["nWT"], S_b[:, hd, :], start=False, stop=True)
                S.copy(ACT, vnew_b, ps[4][:, 0:128])
                yield
                S.mm(oT[:, cs], S_b[:, hd, :], Qg_b[:, cs], start=True, stop=False)
                S.mm(oT[:, cs], vnew_b, B["QKm"], start=False, stop=True)
                S.mm(ps[5][:, 0:128], B["Kd"], vnew_b)
                yield
                S.stt(Sst[:, hd, :], Sst[:, hd, :], tk[:, C_EGLAST + hd:C_EGLAST + hd + 1], ps[5][:, 0:128], ALU.mult, ALU.add)
                S.copy(ACT, S_b[:, hd, :], Sst[:, hd, :])
                yield
            sq3 = bt[5]
            tmpf3, tmpf2 = ft[4], ft[5]
            S.act(sq3[:, 0:tb], oT[:, 0:tb], AF.Square)
            S.mm(ps[4][:, 0:tb], ones_b[:], sq3[:, 0:tb])
            yield
            S.act(tmpf3[:, 0:tb], ps[4][:, 0:tb], AF.Ln, bias=EPS, scale=1.0 / 128)
            S.act(tmpf3[:, 0:tb], tmpf3[:, 0:tb], AF.Exp, scale=-0.5)
            yield
            S.tt(DVE, tmpf2[:, 0:tb], oT[:, 0:tb], tmpf3[:, 0:tb], ALU.mult)
            S.stt(bigT[:, hd, 0:tb], tmpf2[:, 0:tb], ppc("dnnw"), zsil[:, 0:tb], ALU.mult, ALU.mult)
            yield

        def exhaust(gen):
            for _ in gen:
                pass

        def gdn_all(tb):
            gdn_prep_block()
            for t in range(16 + 2):
                streams = []
                if 0 <= t - 2 < 16:
                    streams.append(S.capture(lambda: exhaust(gdn_stage3(t - 2, tb))))
                if 0 <= t - 1 < 16:
                    for ch in gdn_stage2(t - 1, tb):
                        streams.append(S.capture(lambda: exhaust(ch)))
                if t < 16:
                    streams.append(S.capture(lambda: exhaust(gdn_stage1(t, tb))))
                S.merge(streams)

        def mamba(tb):
            ntl = tb // 128
            for p in range(2):
                wv = next_w("mbc")
                for i in range(4):
                    for kc in range(16):
                        S.mm(ps[i][:, 0:tb], wv[:, kc, i * 128:(i + 1) * 128], hnT[:, kc, 0:tb], start=(kc == 0), stop=(kc == 15))
                for i in range(4):
                    g = 2 * p + i // 2
                    isC = i % 2
                    ch = (20 if isC else 16) + g
                    conv_chunk(ps[i][:, 0:tb], tb, pre[i], cv[i], tails_m2, ch, "cw_m2", 4)
                    silu_exp(bcT[:, 2 * g + isC, 0:tb], cv[i][:, 0:tb], ft[0][:, 0:tb], bias=ppc("cb_m2", ch), nbias=ncb[:, ch:ch + 1])
            for tt in range(ntl):
                cs = slice(tt * 128, (tt + 1) * 128)
                trb = ps[4][:].bitcast(BF16)
                for g in range(4):
                    S.tr(trb[:, g * 128:(g + 1) * 128], bcT[:, 2 * g, cs], ident_b[:])
                S.copy(ACT, btm[:, tt, :], trb[:, 0:512])
            def set_bufs(sidx):
                if sidx == 0:
                    return xs_tm, zs_tm
                xs1 = bigT[:, 40:44, :].rearrange("p a b -> p (a b)").rearrange("p (t c) -> p t c", t=3)
                zs1 = hn_tm[:, 512:2048].rearrange("p (t c) -> p t c", t=3)
                return xs1, zs1

            def m2_prep(g):
                sidx = g % 2
                XS, ZS = set_bufs(sidx)
                wv = next_w("mx")

                def xproj(i):
                    for kc in range(16):
                        S.mm(PAB[i % 2][:, 0:tb], wv[:, kc, i * 128:(i + 1) * 128], hnT[:, kc, 0:tb], start=(kc == 0), stop=(kc == 15))

                def xconv(i):
                    ch = 4 * g + i
                    conv_chunk(PAB[i % 2][:, 0:tb], tb, pre[i], cv[i], tails_m2, ch, "cw_m2", 4)
                    silu_exp(bigT[:, 32 + 4 * sidx + i, 0:tb], cv[i][:, 0:tb], ft[0][:, 0:tb], bias=ppc("cb_m2", ch), nbias=ncb[:, ch:ch + 1])
                xproj(0); xproj(1); xconv(0); xproj(2); xconv(1); xproj(3); xconv(2); xconv(3)
                for tt in range(ntl):
                    cs = slice(tt * 128, (tt + 1) * 128)
                    trb = ps[5][:].bitcast(BF16)
                    for i in range(4):
                        S.tr(trb[:, i * 128:(i + 1) * 128], bigT[:, 32 + 4 * sidx + i, cs], ident_b[:])
                    S.copy(ACT, XS[:, tt, :], trb[:, 0:512])
                wv = next_w("mz")
                for tt in range(ntl):
                    cs = slice(tt * 128, (tt + 1) * 128)
                    for kc in range(16):
                        S.mm(ps[6][:, :], hnT[:, kc, cs], wv[:, kc, :], start=(kc == 0), stop=(kc == 15))
                    silu_exp(ZS[:, tt, :], ps[6][:, :], ft[3][:, :])

            def m2_tiles(g):
                sidx = g % 2
                XS, ZS = set_bufs(sidx)
                Ecl4, y1, y2 = ft[1], ft[2], ft[4]
                cbs = ft[5][:, 0:128]
                xdt_b, xdtd_b, MT4_b, y_b, Eex4 = bt[4], bt[5], bt[6], bt[7], bt[8]
                for tt in range(ntl):
                    cs = slice(tt * 128, (tt + 1) * 128)
                    tk = tokS[tt]

                    def hb(o):
                        return tk[:, o + 8 * g:o + 8 * g + 8].unsqueeze(2).to_broadcast([128, 8, 64])

                    def v3(ap):
                        return ap.rearrange("p (h c) -> p h c", h=8)
                    xv = v3(XS[:, tt, :])
                    S.tt(POOL, v3(xdt_b[:]), xv, hb(C_DT), ALU.mult)
                    S.tt(POOL, v3(xdtd_b[:]), v3(xdt_b[:]), hb(C_ED2), ALU.mult)
                    S.mm(ps[3][:, 0:128], bcT[:, 2 * g, cs], bcT[:, 2 * g + 1, cs])
                    S.copy(ACT, cbs, ps[3][:, 0:128])
                    for hq in range(2):
                        h0 = 8 * g + 4 * hq
                        S.mm(ps[7][:, :], nacsT[0:64, cs],
                             sels[0:64, 48 + h0:48 + h0 + 4].unsqueeze(2).to_broadcast([64, 4, 128]), start=True, stop=False)
                        for i in range(4):
                            S.mm(ps[7][:, i * 128:(i + 1) * 128], sels[0:64, 48 + h0 + i:48 + h0 + i + 1].to_broadcast([64, 128]),
                                 acsT[0:64, cs], start=False, stop=(i == 3))
                        S.tt(DVE, Ecl4[:].rearrange("p (i l) -> p i l", i=4), ps[7][:, :].rearrange("p (i l) -> p i l", i=4),
                             negmask2[:, 0:128].unsqueeze(1).to_broadcast([128, 4, 128]), ALU.min)
                        S.act(Eex4[:], Ecl4[:], AF.Exp)
                        M4 = MT4_b[:].rearrange("p (i l) -> p i l", i=4)
                        S.tt(DVE, M4, Eex4[:].rearrange("p (i l) -> p i l", i=4),
                             cbs.unsqueeze(1).to_broadcast([128, 4, 128]), ALU.mult)
                        for i in range(4):
                            h8 = 4 * hq + i
                            S.mm(ps[0][:, h8 * 64:(h8 + 1) * 64], M4[:, i, :], xdt_b[:, h8 * 64:(h8 + 1) * 64], start=True, stop=True)
                    S.mm(ps[1][:, :], bcT[:, 2 * g + 1, cs], stT_b[:, g, :])
                    S.tt(DVE, v3(y1[:]), v3(ps[1][:, :]), hb(C_EACS), ALU.mult)
                    S.tt(DVE, y1[:], y1[:], ps[0][:, :], ALU.add)
                    o_d, _ = RV["m2_d"]
                    dbc = rv[:, o_d + 8 * g:o_d + 8 * g + 8].unsqueeze(2).to_broadcast([128, 8, 64])
                    S.tt(POOL, v3(y2[:]), xv, dbc, ALU.mult)
                    S.tt(POOL, y2[:], y2[:], y1[:], ALU.add)
                    S.tt(DVE, y2[:], y2[:], ZS[:, tt, :], ALU.mult)
                    r = rms_stats(y2[:], 4, 1.0 / 512)
                    S.act(y_b[:], y2[:], AF.Copy, scale=r)
                    trb = ps[2][:].bitcast(BF16)
                    for i in range(4):
                        S.tr(trb[:, i * 128:(i + 1) * 128], y_b[:, i * 128:(i + 1) * 128], ident_b[:])
                    for i in range(4):
                        dst = bigT[:, 16 + 4 * g + i, cs]
                        if i % 2 == 0:
                            S.ts(DVE, dst, trb[:, i * 128:(i + 1) * 128], ppc("m2nw", 4 * g + i), None, ALU.mult)
                        else:
                            S.act(dst, trb[:, i * 128:(i + 1) * 128], AF.Copy, scale=ppc("m2nw", 4 * g + i))
                    S.mm(ps[3][:, :], btm[:, tt, g * 128:(g + 1) * 128], xdtd_b[:])
                    S.tt(DVE, v3(stT[:, g, :]), v3(stT[:, g, :]), hb(C_EALAST), ALU.mult)
                    S.tt(DVE, stT[:, g, :], stT[:, g, :], ps[3][:, :], ALU.add)
                    S.copy(ACT, stT_b[:, g, :], stT[:, g, :])


            PAB = [ps[4], ps[5]]
            S.merge([S.capture(lambda: m2_prep(0))])
            for g in range(4):
                streams = [S.capture(lambda: m2_tiles(g))]
                if g + 1 < 4:
                    streams.append(S.capture(lambda: m2_prep(g + 1)))
                S.merge(streams)

        store_streams = []
        for bi, (tok0, tb) in enumerate(blocks):
            ntl = tb // 128
            for tt in range(ntl):
                a0 = tok0 + tt * 128
                r0 = a0 - NMETA
                lo, hi = max(r0, 0), min(r0 + 128, seq)
                if a0 == 0:
                    S.dma(SP, h[0:NMETA, tt, :], meta[:, :], f"h{tt}")
                    S.dma(SP, h[NMETA:128, tt, :], x[0:128 - NMETA, :], f"h{tt}")
                else:
                    if hi - lo < 128:
                        S.memset(POOL, h[:, tt, :], 0.0)
                    if hi > lo:
                        S.dma(SP, h[0:hi - lo, tt, :], x[lo:hi, :], f"h{tt}")
            if do_gdn or do_m2:
                rmsnorm_to_T("nw_mix", tb)
                small_proj(tb)
                if do_gdn:
                    gdn_all(tb)
                else:
                    S.memset(POOL, bigT[:, 0:16, :], 0.0)
                if do_m2:
                    mamba(tb)
                else:
                    S.memset(POOL, bigT[:, 16:32, :], 0.0)
                for cb in range(4):
                    for kg in range(2):
                        wv = next_w("wo")
                        for tt in range(ntl):
                            for k in range(16):
                                kc = kg * 16 + k
                                S.mm(ps[tt][:, :], bigT[:, kc, tt * 128:(tt + 1) * 128], wv[:, k, :], start=(kc == 0), stop=(kc == 31))
                    for tt in range(ntl):
                        S.tt(DVE, h[:, tt, cb * 512:(cb + 1) * 512], h[:, tt, cb * 512:(cb + 1) * 512], ps[tt][:, :], ALU.add)
            if do_ffn:
                rmsnorm_to_T("nw_ffn", tb)
                for u in range(22):
                    wv = next_w("up")
                    for half in range(2):
                        f_g = 2 * u + half
                        for which in range(2):
                            idx = half * 2 + which
                            pst = ps[idx]
                            co = which * 256 + half * 128
                            for kc in range(16):
                                S.mm(pst[:, 0:tb], wv[:, kc, co:co + 128], hnT[:, kc, 0:tb], start=(kc == 0), stop=(kc == 15))
                            conv_chunk(pst[:, 0:tb], tb, pre[idx], cv[idx], tails_ff, f_g + which * 44, "cw_ff", 3)
                        cg, cvv = cv[half * 2], cv[half * 2 + 1]
                        S.act(cg[:, 0:tb], cg[:, 0:tb], AF.Silu)
                        S.tt(DVE, bigT[:, f_g, 0:tb], cg[:, 0:tb], cvv[:, 0:tb], ALU.mult)
                for cb in range(4):
                    for kg in range(3):
                        wv = next_w("dn")
                        nk = 16 if kg < 2 else 12
                        for tt in range(ntl):
                            for k in range(nk):
                                kc = kg * 16 + k
                                S.mm(ps[tt][:, :], bigT[:, kc, tt * 128:(tt + 1) * 128], wv[:, k, :], start=(kc == 0), stop=(kc == 43))
                    for tt in range(ntl):
                        S.tt(DVE, h[:, tt, cb * 512:(cb + 1) * 512], h[:, tt, cb * 512:(cb + 1) * 512], ps[tt][:, :], ALU.add)
            for tt in range(ntl):
                r = rms_stats(h[:, tt, :], tt, 1.0 / D)
                S.stt(h[:, tt, :], h[:, tt, :], r, rvc("nwf"), ALU.mult, ALU.mult)
                a0 = tok0 + tt * 128
                r0 = a0 - NMETA
                lo, hi = max(r0, 0), min(r0 + 128, seq)
                if hi > lo:
                    S.dma(SP, out[lo:hi, :], h[lo - r0:hi - r0, tt, :], f"o{tt}")
                    if f"o{tt}" not in store_streams:
                        store_streams.append(f"o{tt}")
        S.emit(store_streams)
        print("ops", S.stats)
    return nc


_NC_CACHE = {}


def kernel(**inputs):
    inp = {k: np.asarray(v) for k, v in inputs.items()}
    x = inp["x"]
    B = x.shape[0]
    W = prep_weights(inp)
    if "nc" not in _NC_CACHE:
        _NC_CACHE["nc"] = build(seq=SEQ)
    nc = _NC_CACHE["nc"]
    in_maps = []
    for b in range(B):
        in_maps.append(dict(x=np.ascontiguousarray(x[b], dtype=np.float32), meta=W["meta"], wbig=W["wbig"],
                            wsm=W["wsm"], pp=W["pp"], rv=W["rv"]))
    res = run_bass_kernel_spmd(nc, in_maps, core_ids=list(range(B)))
    return np.stack([np.asarray(r["out"], dtype=np.float32) for r in res.results], axis=0)
```

```python
import numpy as np
import concourse.bass as bass
import concourse.mybir as mybir

F32 = mybir.dt.float32
BF16 = mybir.dt.bfloat16
AF = mybir.ActivationFunctionType
ALU = mybir.AluOpType

PE, ACT, DVE, POOL, SP = "pe", "act", "dve", "pool", "sp"
ENGS = [PE, ACT, DVE, POOL, SP]


class Acc:
    __slots__ = ("op", "eng", "w", "p0", "p1", "f0", "f1")

    def __init__(self, op, eng, w, p0, p1, f0, f1):
        self.op = op; self.eng = eng; self.w = w
        self.p0 = p0; self.p1 = p1; self.f0 = f0; self.f1 = f1


def region(ap):
    sp = str(ap.space)
    if "DRAM" in sp.upper():
        return None
    t = ap.tensor
    a = ap.ap
    ps = a[0][0]
    off = ap.offset
    if ps > 0:
        p0 = off // ps
        f0 = off % ps
    else:
        p0 = 0
        f0 = off
    p1 = p0 + a[0][1]
    ext = 0
    for st, cnt in a[1:]:
        ext += abs(st) * (cnt - 1)
    f1 = f0 + ext + 1
    return (t.name, "PSUM" in sp.upper() or sp.upper().startswith("PS"), p0, p1, f0, f1)


def _fsize(ap):
    n = 1
    for st, cnt in ap.ap[1:]:
        n *= cnt
    return n


def _vcost(eng, out):
    n = _fsize(out)
    if eng == POOL:
        return 250.0 + 1.1 * n
    return 110.0 + n / 0.96


class Sched:
    def __init__(self, nc):
        self.nc = nc
        self.ops = []
        self.hist = {}
        self.dma_count = {}
        self.cap = None
        self.eng_free = {}

    def _regions(self, reads, writes):
        out = []
        for ap in reads:
            if ap is not None:
                rg = region(ap)
                if rg is not None:
                    out.append((rg, False))
        for ap in writes:
            if ap is not None:
                rg = region(ap)
                if rg is not None:
                    out.append((rg, True))
        return out

    def _analyze(self, eng, regs, oid, commit):
        deps = set()
        for (name, is_ps, p0, p1, f0, f1), is_w in regs:
            lst = self.hist.get(name)
            if lst is None:
                lst = []
                if commit:
                    self.hist[name] = lst
            keep = []
            for a in lst:
                if a.op == oid:
                    keep.append(a)
                    continue
                overlap = not (a.p1 <= p0 or p1 <= a.p0 or a.f1 <= f0 or f1 <= a.f0)
                if is_ps and a.eng != eng:
                    conflict = True
                else:
                    conflict = overlap and (is_w or a.w)
                if conflict and not (a.eng == PE and eng == PE):
                    deps.add(a.op)
                covered = is_w and p0 <= a.p0 and a.p1 <= p1 and f0 <= a.f0 and a.f1 <= f1
                if covered or (is_ps and a.eng != eng):
                    continue
                keep.append(a)
            if commit:
                keep.append(Acc(oid, eng, is_w, p0, p1, f0, f1))
                self.hist[name] = keep
        best = {}
        nd = set()
        for d in deps:
            dop = self.ops[d]
            if dop["stream"] is not None:
                nd.add(d)
            else:
                e2 = dop["eng"]
                if best.get(e2, -1) < d:
                    best[e2] = d
        nd.update(best.values())
        return nd

    def _est_start(self, eng, deps):
        t = self.eng_free.get(eng, 0.0)
        for d in deps:
            dop = self.ops[d]
            lat = 60.0 if dop["eng"] == eng else 180.0
            if dop["fin"] + lat > t:
                t = dop["fin"] + lat
        return t

    def add(self, eng, fn, reads, writes, stream=None, cost=300.0):
        if self.cap is not None:
            self.cap.append((eng, fn, self._regions(reads, writes), stream, cost))
            return None
        return self._commit(eng, fn, self._regions(reads, writes), stream, cost)

    def _commit(self, eng, fn, regs, stream, cost, start=None):
        oid = len(self.ops)
        deps = self._analyze(eng, regs, oid, True)
        if start is None:
            start = self._est_start(eng, deps)
        if stream is not None:
            self.eng_free[eng] = start + 500.0
        else:
            self.eng_free[eng] = start + cost
        op = dict(id=oid, eng=eng, fn=fn, deps=deps, stream=stream, sig=False, dn=None, fin=start + cost)
        if stream is not None:
            n = self.dma_count.get(stream, 0) + 1
            self.dma_count[stream] = n
            op["dn"] = n
        self.ops.append(op)
        return oid

    def capture(self, f):
        assert self.cap is None
        self.cap = []
        try:
            f()
            out = self.cap
        finally:
            self.cap = None
        return out

    def merge(self, streams, prereq=None):
        n = len(streams)
        idx = [0] * n
        done = [len(streams[i]) == 0 for i in range(n)]
        remaining = sum(1 for d in done if not d)
        while remaining:
            best = None
            for i in range(n):
                if done[i]:
                    continue
                if prereq is not None and idx[i] == 0 and any(not done[j] for j in prereq[i]):
                    continue
                eng, fn, regs, stream, cost = streams[i][idx[i]]
                deps = self._analyze(eng, regs, -1, False)
                st = self._est_start(eng, deps)
                if best is None or st < best[0]:
                    best = (st, i)
            st, i = best
            eng, fn, regs, stream, cost = streams[i][idx[i]]
            self._commit(eng, fn, regs, stream, cost, start=st)
            idx[i] += 1
            if idx[i] >= len(streams[i]):
                done[i] = True
                remaining -= 1

    def emit(self, final_wait_streams):
        nc = self.nc
        ops = self.ops
        for op in ops:
            for d in op["deps"]:
                ops[d]["sig"] = True
        cnt = {e: 0 for e in ENGS}
        for op in ops:
            if op["stream"] is None and op["sig"]:
                cnt[op["eng"]] += 1
                op["sv"] = cnt[op["eng"]]
        from contextlib import ExitStack
        with ExitStack() as es:
            sems = {e: es.enter_context(nc.semaphore("sem_" + e)) for e in ENGS}
            dsems = {s: es.enter_context(nc.semaphore("dsem_" + s)) for s in self.dma_count}
            block = es.enter_context(nc.Block())
            per_eng = {e: [op for op in ops if op["eng"] == e] for e in ENGS}

            def run(eng_name, eng):
                waited = {}
                for op in per_eng[eng_name]:
                    need = {}
                    for d in op["deps"]:
                        dop = ops[d]
                        if dop["stream"] is not None:
                            key = ("d", dop["stream"]); val = 16 * dop["dn"]
                        else:
                            key = ("e", dop["eng"]); val = dop["sv"]
                        if need.get(key, 0) < val:
                            need[key] = val
                    for key, val in need.items():
                        if waited.get(key, 0) >= val:
                            continue
                        waited[key] = val
                        sem = dsems[key[1]] if key[0] == "d" else sems[key[1]]
                        eng.wait_ge(sem, val)
                    ins = op["fn"](eng)
                    if op["stream"] is not None:
                        ins.then_inc(dsems[op["stream"]], 16)
                    elif op["sig"]:
                        ins.then_inc(sems[eng_name], 1)
                if eng_name == SP:
                    for s in final_wait_streams:
                        eng.wait_ge(dsems[s], 16 * self.dma_count[s])

            block.tensor(lambda e: run(PE, e))
            block.scalar(lambda e: run(ACT, e))
            block.vector(lambda e: run(DVE, e))
            block.gpsimd(lambda e: run(POOL, e))
            block.sync(lambda e: run(SP, e))
        self.stats = {e: len(per_eng[e]) for e in ENGS}
        self.stats["sig"] = dict(cnt)

    def mm(self, out, lhsT, rhs, start=True, stop=True):
        n = _fsize(out)
        c = max(110.0, n * (4 if lhsT.dtype == F32 else 1) / 1.95 + 12)
        return self.add(PE, lambda e: e.matmul(out, lhsT=lhsT, rhs=rhs, start=start, stop=stop,
                                               skip_group_check=True),
                        [lhsT, rhs], [out], cost=c)

    def tr(self, out, in_, ident):
        return self.add(PE, lambda e: e.transpose(out, in_, ident), [in_, ident], [out], cost=70.0)

    def act(self, out, in_, func, bias=None, scale=None, accum_out=None, eng=ACT):
        kw = {}
        rd = [in_]
        if bias is not None:
            kw["bias"] = bias
            if not isinstance(bias, (int, float)):
                rd.append(bias)
        if scale is not None:
            kw["scale"] = scale
            if not isinstance(scale, (int, float)):
                rd.append(scale)
        wr = [out]
        if accum_out is not None:
            kw["accum_out"] = accum_out
            wr.append(accum_out)
        return self.add(ACT, lambda e: e.activation(out, in_, func, **kw), rd, wr, cost=200 + 0.83 * _fsize(out))

    def ts(self, eng, out, in0, s1, s2, op0, op1=None, accum_out=None):
        rd = [in0]
        if not isinstance(s1, (int, float)):
            rd.append(s1)
        if s2 is not None and not isinstance(s2, (int, float)):
            rd.append(s2)
        kw = {}
        wr = [out]
        if op1 is not None:
            kw["op1"] = op1
        if accum_out is not None:
            kw["accum_out"] = accum_out
            wr.append(accum_out)
        return self.add(eng, lambda e: e.tensor_scalar(out, in0, s1, s2, op0, **kw), rd, wr, cost=_vcost(eng, out))

    def stt(self, out, in0, scalar, in1, op0, op1, eng=DVE):
        rd = [in0, in1]
        if not isinstance(scalar, (int, float)):
            rd.append(scalar)
        return self.add(eng, lambda e: e.scalar_tensor_tensor(out, in0, scalar, in1, op0, op1), rd, [out], cost=_vcost(eng, out))

    def tt(self, eng, out, in0, in1, op):
        return self.add(eng, lambda e: e.tensor_tensor(out, in0, in1, op), [in0, in1], [out], cost=_vcost(eng, out))

    def copy(self, eng, out, in_):
        if eng == ACT:
            return self.add(ACT, lambda e: e.copy(out, in_), [in_], [out], cost=200 + 0.83 * _fsize(out))
        return self.add(eng, lambda e: e.tensor_copy(out, in_), [in_], [out], cost=_vcost(eng, out))

    def memset(self, eng, out, val):
        return self.add(eng, lambda e: e.memset(out, val), [], [out], cost=_vcost(eng, out))

    def dma(self, queue, out, in_, stream):
        return self.add(queue, lambda e: e.dma_start(out=out, in_=in_), [in_], [out], stream=stream,
                        cost=2500.0 + _fsize(out) * out.ap[0][1] * 4 / 150.0)


import numpy as np
import concourse.bass as bass
import concourse.mybir as mybir
from concourse.bass_utils import run_bass_kernel_spmd

D = 2048
KC = 16
SEQ = 4096
NMETA = 16
DFF = 5632
EPS = 1e-6
GW = 8192

OQ, OK_, OV, OZ, OB, OA, OMZ, OXS, OBM, OCM, ODT = 0, 2048, 4096, 6144, 8192, 8208, 8224, 10272, 12320, 12832, 13344


def group_list():
    gl = []
    for h in range(16):
        gl.append(("gdn", h))
    gl.append(("mbc", 0)); gl.append(("mbc", 1))
    for g in range(4):
        gl.append(("mx", g)); gl.append(("mz", g))
    for cb in range(4):
        for kg in range(2):
            gl.append(("wo", cb, kg))
    for u in range(22):
        gl.append(("up", u))
    for cb in range(4):
        for kg in range(3):
            gl.append(("dn", cb, kg))
    return gl


GL = group_list()
GIDX = {g: i for i, g in enumerate(GL)}
NG = len(GL)

PP = {}
_o = 0
for nm, n in [("nw_mix", 16), ("nw_ffn", 16), ("m2nw", 16), ("dnnw", 1), ("cw_dn", 48 * 4), ("cw_m2", 24 * 4),
              ("cb_m2", 24), ("cw_ff", 88 * 3)]:
    PP[nm] = (_o, n); _o += n
NPP = _o
RV = {}
_o = 0
for nm, n in [("nwf", 2048), ("dn_alog", 16), ("dn_dtb", 16), ("m2_alog", 32), ("m2_dtb", 32), ("m2_d", 32)]:
    RV[nm] = (_o, n); _o += n
NRV = _o


def prep_weights(inp):
    w_in = np.asarray(inp["w_in"][0]); w_out = np.asarray(inp["w_out"][0])
    up = np.asarray(inp["ffn_up"][0]); dn = np.asarray(inp["ffn_down"][0])
    wbig = np.zeros((NG, 128, GW), np.float32)

    def put(gi, W, rows0, nk, cols):
        blk = W[rows0:rows0 + nk * 128][:, cols]
        blk = blk.reshape(nk, 128, len(cols)).transpose(1, 0, 2)
        wbig[gi, :, :nk * 512] = blk.reshape(128, nk * 512)

    ar = np.arange
    for gi, g in enumerate(GL):
        if g[0] == "gdn":
            h = g[1]
            cols = np.concatenate([OQ + h * 128 + ar(128), OK_ + h * 128 + ar(128), OV + h * 128 + ar(128), OZ + h * 128 + ar(128)])
            put(gi, w_in, 0, 16, cols)
        elif g[0] == "mbc":
            p = g[1]
            cols = np.concatenate([OBM + (2 * p) * 128 + ar(128), OCM + (2 * p) * 128 + ar(128),
                                   OBM + (2 * p + 1) * 128 + ar(128), OCM + (2 * p + 1) * 128 + ar(128)])
            put(gi, w_in, 0, 16, cols)
        elif g[0] == "mx":
            put(gi, w_in, 0, 16, OXS + g[1] * 512 + ar(512))
        elif g[0] == "mz":
            put(gi, w_in, 0, 16, OMZ + g[1] * 512 + ar(512))
        elif g[0] == "wo":
            put(gi, w_out, g[2] * 2048, 16, g[1] * 512 + ar(512))
        elif g[0] == "up":
            u = g[1]
            cols = np.concatenate([u * 256 + ar(256), DFF + u * 256 + ar(256)])
            put(gi, up, 0, 16, cols)
        elif g[0] == "dn":
            kg = g[2]
            nk = 16 if kg < 2 else 12
            put(gi, dn, kg * 2048, nk, g[1] * 512 + ar(512))
    cols = np.concatenate([OB + ar(16), OA + ar(16), ODT + ar(32)])
    wsm = w_in[:, cols].reshape(16, 128, 64).transpose(1, 0, 2).reshape(128, 16 * 64).copy()
    pp = np.zeros((128, NPP), np.float32)

    def fm(v):
        return np.asarray(v).reshape(-1, 128).T

    def setp(nm, a):
        o, n = PP[nm]; pp[:, o:o + n] = a.reshape(128, n)
    setp("nw_mix", fm(inp["norm_mix_w"][0])); setp("nw_ffn", fm(inp["norm_ffn_w"][0]))
    setp("m2nw", fm(inp["m2_norm_w"][0])); setp("dnnw", np.asarray(inp["dn_norm_w"][0]).reshape(128, 1))

    def cw(w):
        w = np.asarray(w); K, C = w.shape
        return w.reshape(K, C // 128, 128).transpose(2, 1, 0).reshape(128, -1)
    setp("cw_dn", cw(inp["dn_conv_w"][0])); setp("cw_m2", cw(inp["m2_conv_w"][0]))
    setp("cb_m2", fm(inp["m2_conv_b"][0])); setp("cw_ff", cw(inp["ffn_conv_w"][0]))
    rv = np.zeros((1, NRV), np.float32)

    def setr(nm, a):
        o, n = RV[nm]; rv[0, o:o + n] = np.asarray(a).reshape(n)
    setr("nwf", inp["norm_final_w"]); setr("dn_alog", inp["dn_a_log"][0]); setr("dn_dtb", inp["dn_dt_bias"][0])
    setr("m2_alog", inp["m2_a_log"][0]); setr("m2_dtb", inp["m2_dt_bias"][0]); setr("m2_d", inp["m2_d"][0])
    return dict(wbig=wbig, wsm=wsm, pp=pp, rv=rv, meta=np.asarray(inp["meta_tokens"], np.float32))


import math

C_RAW, C_LNB, C_BETA, C_G, C_AM, C_GC, C_ACS, C_GLAST, C_ALAST = 0, 64, 80, 96, 112, 144, 160, 192, 208
C_EGC, C_EACS, C_EGLAST, C_EALAST, C_ED, C_ED2, C_BEGE, C_DT, C_TMP = 240, 256, 288, 304, 336, 352, 384, 400, 432
C_STA, C_STB = 432, 512
NS = 576


def build(seq=SEQ, T=384, do_gdn=True, do_m2=True, NWB=2, do_ffn=True):
    ntok = NMETA + seq
    nblk = (ntok + T - 1) // T
    blocks = []
    t0 = 0
    for b in range(nblk):
        tb = min(T, ((ntok - t0 + 127) // 128) * 128)
        blocks.append((t0, tb)); t0 += tb
    nc = bass.Bass("TRN2", target_bir_lowering=False)
    x = nc.dram_tensor("x", [seq, D], F32, kind="ExternalInput").ap()
    meta = nc.dram_tensor("meta", [NMETA, D], F32, kind="ExternalInput").ap()
    wbig = nc.dram_tensor("wbig", [NG, 128, GW], F32, kind="ExternalInput").ap()
    wsm_d = nc.dram_tensor("wsm", [128, 16 * 64], F32, kind="ExternalInput").ap()
    pp_d = nc.dram_tensor("pp", [128, NPP], F32, kind="ExternalInput").ap()
    rv_d = nc.dram_tensor("rv", [1, NRV], F32, kind="ExternalInput").ap()
    out = nc.dram_tensor("out", [seq, D], F32, kind="ExternalOutput").ap()
    NT = T // 128
    from contextlib import ExitStack
    with ExitStack() as es:
        def sb(name, shape, dt=F32):
            return es.enter_context(nc.sbuf_tensor(name, shape, dt))

        def psb(name, shape, dt=F32):
            return es.enter_context(nc.psum_tensor(name, shape, dt))
        S = Sched(nc)
        ident_f = sb("ident_f", [128, 128]); ident_b = sb("ident_b", [128, 128], BF16)
        ones_f = sb("ones_f", [128, 128]); ones_b = sb("ones_b", [128, 128], BF16)
        nones2 = sb("nones2", [128, 256])
        UT_f = sb("UT_f", [128, 128])
        masks = sb("masks", [128, 256])
        pp = sb("pp_sb", [128, NPP]); rv = sb("rv_sb", [128, NRV])
        nA = sb("nA", [128, 48])
        wsm_b = sb("wsm_b", [128, 16 * 64], BF16)
        h = sb("h", [128, NT, D])
        hnT = sb("hnT", [128, KC, T], BF16)
        bigT = sb("bigT", [128, 44, T], BF16)
        wb = [sb(f"wb{i}", [128, GW], BF16) for i in range(NWB)]
        hn_tm = sb("hn_tm", [128, D], BF16)
        stat = sb("stat", [128, 32])
        tails_ff = sb("tails_ff", [128, 88, 3])
        tails_dn = sb("tails_dn", [128, 48, 3])
        tails_m2 = sb("tails_m2", [128, 24, 3])
        pre = [sb(f"pre{i}", [128, 3 + T]) for i in range(4)]
        cv = [sb(f"cv{i}", [128, T]) for i in range(4)]
        ft = [sb(f"ft{i}", [128, 512]) for i in range(6)]
        bt = [None] * 4 + [sb(f"bt{i}", [128, 512], BF16) for i in range(4, 9)]
        negmask2 = sb("negmask2", [128, 256])
        Sst = sb("Sst", [128, 16, 128]); S_b = sb("S_b", [128, 16, 128], BF16)
        stT = sb("stT", [128, 4, 512]); stT_b = sb("stT_b", [128, 4, 512], BF16)
        tokS = [sb(f"tokS{i}", [128, NS]) for i in range(NT)]
        glT = sb("glT", [64, T], BF16); nglT = sb("nglT", [32, T], BF16); egcT = sb("egcT", [32, T], BF16)
        acsT = sb("acsT", [64, T], BF16); nacsT = sb("nacsT", [64, T], BF16)
        stg = sb("stg", [128, 256], BF16); stgf = sb("stgf", [128, 128])
        sels = sb("sels", [64, 96], BF16)
        bcT = sb("bcT", [128, 8, T], BF16)
        btm = sb("btm", [128, NT, 512], BF16)
        xs_tm = sb("xs_tm", [128, NT, 512], BF16)
        zs_tm = sb("zs_tm", [128, NT, 512], BF16)
        ps = [psb(f"ps{i}", [128, 512]) for i in range(8)]
        print("sbuf remaining", nc.sbuf_bytes_remaining)

        S.memset(POOL, ones_f[:], 1.0)
        S.memset(POOL, nones2[:], -1.0)
        S.memset(POOL, ones_b[:], 1.0)

        def asel(out_ap, in_ap, cmp, base):
            S.add(POOL, lambda e: e.affine_select(out_ap, in_ap, [[1, 128]], cmp, 0.0, base=base,
                                                  channel_multiplier=-1), [in_ap], [out_ap])
        asel(ident_f[:], ones_f[:], ALU.is_equal, 0)
        asel(UT_f[:], ones_f[:], ALU.is_ge, 0)
        S.copy(POOL, masks[:, 0:128], UT_f[:])
        asel(masks[:, 128:256], nones2[:, 0:128], ALU.is_ge, -1)
        S.copy(POOL, ident_b[:], ident_f[:])
        S.ts(DVE, negmask2[:, 0:128], masks[:, 0:128], -1.0, 30000.0, ALU.add, ALU.mult)
        S.ts(DVE, negmask2[:, 128:256], masks[:, 128:256], 1.0, -30000.0, ALU.add, ALU.mult)
        S.tt(DVE, sels[0:64, 0:16], ident_b[0:64, 0:16], ident_b[0:64, 32:48], ALU.add)
        S.tt(DVE, sels[0:64, 16:32], ident_b[0:64, 16:32], ident_b[0:64, 48:64], ALU.add)
        S.tt(DVE, sels[0:32, 32:48], ident_b[0:32, 0:16], ident_b[0:32, 16:32], ALU.add)
        S.tt(DVE, sels[0:64, 48:80], ident_b[0:64, 0:32], ident_b[0:64, 32:64], ALU.add)
        S.dma(SP, pp[:], pp_d[:, :], "pp")
        S.dma(SP, rv[:], rv_d[0:1, :].to_broadcast([128, NRV]), "rv")
        S.dma(POOL, wsm_b[:], wsm_d[:, :], "wsm")
        for t_ in (tails_ff, tails_dn, tails_m2, Sst, S_b, stT, stT_b):
            S.memset(POOL, t_[:], 0.0)

        def ppc(nm, i=0, n=1):
            o, _ = PP[nm]
            return pp[:, o + i:o + i + n]

        def rvc(nm):
            o, n = RV[nm]
            return rv[:, o:o + n]
        ncb = sb("ncb", [128, 24])
        S.ts(DVE, ncb[:], ppc("cb_m2", 0, 24), -1.0, None, ALU.mult)
        S.act(nA[:, 0:16], rvc("dn_alog"), AF.Exp)
        S.act(nA[:, 16:48], rvc("m2_alog"), AF.Exp)
        S.ts(DVE, nA[:], nA[:], -1.0, None, ALU.mult)

        wstate = dict(next=0)
        total_groups = []

        def plan_groups(bi):
            return [g for g in GL if (g[0] in ("up", "dn") and do_ffn) or (g[0] == "wo" and (do_gdn or do_m2))
                    or (g[0] == "gdn" and do_gdn) or (g[0] in ("mbc", "mx", "mz") and do_m2)]
        for bi in range(nblk):
            total_groups += plan_groups(bi)

        def issue_w(n):
            if n >= len(total_groups):
                return
            g = total_groups[n]
            nk = 12 if (g[0] == "dn" and g[2] == 2) else 16
            S.dma(POOL, wb[n % NWB][:, 0:nk * 512], wbig[GIDX[g], :, 0:nk * 512], f"w{n % NWB}")

        for n in range(NWB - 1):
            issue_w(n)

        def next_w(expect):
            n = wstate["next"]
            assert total_groups[n][0] == expect, (total_groups[n], expect)
            issue_w(n + NWB - 1)
            wstate["next"] = n + 1
            return wb[n % NWB][:].rearrange("p (k c) -> p k c", k=16)

        def rms_stats(src, tt, scale):
            c0 = 3 * tt
            n = src.shape[1]
            if n > 512:
                junk = bigT[:, 0:6, :].rearrange("p a b -> p (a b)")[:, 0:n]
            else:
                junk = hn_tm[:, 0:n]
            S.act(junk, src, AF.Square, accum_out=stat[:, c0:c0 + 1])
            S.act(stat[:, c0 + 1:c0 + 2], stat[:, c0:c0 + 1], AF.Ln, bias=EPS, scale=scale)
            S.act(stat[:, c0 + 2:c0 + 3], stat[:, c0 + 1:c0 + 2], AF.Exp, scale=-0.5)
            return stat[:, c0 + 2:c0 + 3]

        def rmsnorm_to_T(nw_name, tb):
            ntl = tb // 128

            def tile_stream(tt):
                r = rms_stats(h[:, tt, :], tt, 1.0 / D)
                for kq in range(4):
                    hp = hn_tm[:, tt * 512:(tt + 1) * 512]
                    S.ts(DVE, hp, h[:, tt, kq * 512:(kq + 1) * 512], r, None, ALU.mult)
                    pv = ps[4 + tt][:].bitcast(BF16)[:, (kq % 2) * 512:(kq % 2) * 512 + 512]
                    for j in range(4):
                        kc = kq * 4 + j
                        S.tr(pv[:, j * 128:(j + 1) * 128], hp[:, j * 128:(j + 1) * 128], ident_b[:])
                    for j in range(4):
                        kc = kq * 4 + j
                        dst = hnT[:, kc, tt * 128:(tt + 1) * 128]
                        if j % 2 == 0:
                            S.ts(DVE, dst, pv[:, j * 128:(j + 1) * 128], ppc(nw_name, kc), None, ALU.mult)
                        else:
                            S.act(dst, pv[:, j * 128:(j + 1) * 128], AF.Copy, scale=ppc(nw_name, kc))
            S.merge([S.capture(lambda: tile_stream(tt)) for tt in range(ntl)])

        def silu_exp(out, src, tmp, bias=None, nbias=None):
            if bias is None:
                S.act(tmp, src, AF.Exp, scale=-1.0)
            else:
                S.act(tmp, src, AF.Exp, scale=-1.0, bias=nbias)
            S.act(tmp, tmp, AF.Ln, bias=1.0)
            S.act(tmp, tmp, AF.Exp, scale=-1.0)
            if bias is None:
                S.tt(DVE, out, src, tmp, ALU.mult)
            else:
                S.stt(out, src, bias, tmp, ALU.add, ALU.mult)

        def conv_chunk(psrc, tb, pr, c, tails, ch, cwname, K):
            S.copy(POOL, pr[:, 0:3], tails[:, ch, :])
            S.act(pr[:, 3:3 + tb], psrc, AF.Copy)
            S.copy(POOL, tails[:, ch, :], pr[:, tb:tb + 3])
            o, _ = PP[cwname]
            b0 = 4 - K
            S.ts(DVE, c[:, 0:tb], pr[:, b0:b0 + tb], pp[:, o + ch * K:o + ch * K + 1], None, ALU.mult)
            for j in range(1, K):
                S.stt(c[:, 0:tb], pr[:, b0 + j:b0 + j + tb], pp[:, o + ch * K + j:o + ch * K + j + 1], c[:, 0:tb],
                      ALU.mult, ALU.add)

        def small_proj(tb):
            ntl = tb // 128
            for tt in range(ntl):
                tk = tokS[tt]
                cs = slice(tt * 128, (tt + 1) * 128)
                for kc in range(16):
                    S.mm(ps[7][:, 0:64], hnT[:, kc, cs], wsm_b[:, kc * 64:(kc + 1) * 64], start=(kc == 0), stop=(kc == 15))
                S.copy(ACT, tk[:, 0:64], ps[7][:, 0:64])
                tmp = tk[:, C_TMP:C_TMP + 96]
                S.act(tmp[:, 0:16], tk[:, 0:16], AF.Exp, scale=-1.0)
                S.tt(DVE, tmp[:, 16:32], tk[:, 16:32], rvc("dn_dtb"), ALU.add)
                S.tt(DVE, tmp[:, 32:64], tk[:, 32:64], rvc("m2_dtb"), ALU.add)
                S.act(tmp[:, 16:64], tmp[:, 16:64], AF.Exp)
                S.act(tmp[:, 0:64], tmp[:, 0:64], AF.Ln, bias=1.0)
                S.ts(DVE, tk[:, C_LNB:C_LNB + 16], tmp[:, 0:16], -1.0, None, ALU.mult)
                S.act(tk[:, C_BETA:C_BETA + 16], tk[:, C_LNB:C_LNB + 16], AF.Exp)
                S.tt(DVE, tk[:, C_G:C_G + 16], tmp[:, 16:32], nA[:, 0:16], ALU.mult)
                S.copy(DVE, tk[:, C_DT:C_DT + 32], tmp[:, 32:64])
                S.tt(DVE, tk[:, C_AM:C_AM + 32], tmp[:, 32:64], nA[:, 16:48], ALU.mult)
                S.mm(ps[7][:, 64:112], UT_f[:], tk[:, C_G:C_G + 48])
                S.mm(ps[7][:, 112:160], ones_f[:], tk[:, C_G:C_G + 48])
                S.copy(ACT, tk[:, C_GC:C_GC + 96], ps[7][:, 64:160])
                S.act(tk[:, C_EGC:C_EGC + 96], tk[:, C_GC:C_GC + 96], AF.Exp)
                S.tt(DVE, tmp[:, 0:48], tk[:, C_GLAST:C_GLAST + 48], tk[:, C_GC:C_GC + 48], ALU.subtract)
                S.act(tk[:, C_ED:C_ED + 48], tmp[:, 0:48], AF.Exp)
                S.tt(DVE, tk[:, C_BEGE:C_BEGE + 16], tk[:, C_BETA:C_BETA + 16], tk[:, C_EGC:C_EGC + 16], ALU.mult)
                stA = tk[:, C_STA:C_STA + 80]; stB = tk[:, C_STB:C_STB + 64]
                S.copy(DVE, stA[:, 0:16], tk[:, C_GC:C_GC + 16])
                S.tt(DVE, stA[:, 16:32], tk[:, C_GC:C_GC + 16], tk[:, C_LNB:C_LNB + 16], ALU.add)
                S.ts(DVE, stA[:, 32:48], tk[:, C_GC:C_GC + 16], -1.0, None, ALU.mult)
                S.memset(DVE, stA[:, 48:64], 0.0)
                S.copy(DVE, stA[:, 64:80], tk[:, C_EGC:C_EGC + 16])
                S.copy(DVE, stB[:, 0:32], tk[:, C_ACS:C_ACS + 32])
                S.ts(DVE, stB[:, 32:64], tk[:, C_ACS:C_ACS + 32], -1.0, None, ALU.mult)
                srcs = [(stA[:, 0:32], 32, glT), (stA[:, 32:48], 16, nglT), (stA[:, 64:80], 16, egcT),
                        (stB[:, 0:32], 32, acsT), (stB[:, 32:64], 32, nacsT)]
                o = 0
                for (src, n, dst) in srcs:
                    hi = stg[:, o:o + n]; lo = stg[:, o + n:o + 2 * n]
                    S.copy(DVE, hi, src)
                    S.tt(DVE, lo, src, hi, ALU.subtract)
                    pvb = ps[7][:].bitcast(BF16)
                    S.tr(pvb[0:2 * n, 0:128], stg[:, o:o + 2 * n], ident_b[:])
                    S.copy(ACT, dst[0:2 * n, cs], pvb[0:2 * n, 0:128])
                    o += 2 * n

        def run_rr(gens):
            gens = list(gens)
            while gens:
                nxt = []
                for g_ in gens:
                    try:
                        next(g_); nxt.append(g_)
                    except StopIteration:
                        pass
                gens = nxt

        def slot128(i):
            if i < 12:
                return btm[:, i // 4, (i % 4) * 128:(i % 4 + 1) * 128]
            i -= 12
            return zs_tm[:, i // 4, (i % 4) * 128:(i % 4 + 1) * 128]

        def xp_tile(c, i):
            k = 3 * c + i
            return bcT[:, k, 0:384] if k < 8 else xs_tm[:, 0, 0:384]

        def gdn_prep_block():
            for c in range(3):
                S.copy(POOL, xp_tile(c, 0)[:, 0:128], ident_b[:])

        def gset(sidx):
            b0 = 32 + 4 * sidx
            return dict(qn=bigT[:, b0, :], kn=bigT[:, b0 + 1, :], v=bigT[:, b0 + 2, :], Qg=bigT[:, b0 + 3, :],
                        zsil=bt[6 + sidx])

        def cslot(i):
            return bigT[:, 16 + i // 3, (i % 3) * 128:(i % 3 + 1) * 128]

        def cbufs(c, pset):
            b0 = 24 * pset + 8 * c
            return dict(Kb=cslot(b0), Kd=cslot(b0 + 1), Vb=cslot(b0 + 2), QKm=cslot(b0 + 3),
                        nWT=cslot(b0 + 4), TT=cslot(b0 + 5), Em0=cslot(b0 + 6), Em1=cslot(b0 + 7))

        def gdn_stage1(hd, tb):
            G = gset(hd % 3)
            wv = next_w("gdn")
            PA, PBk = ps[6], ps[7]
            sq_b = bt[4]
            tmpf = ft[0]

            def proj(i, bank):
                for kc in range(16):
                    S.mm(bank[:, 0:tb], wv[:, kc, i * 128:(i + 1) * 128], hnT[:, kc, 0:tb], start=(kc == 0), stop=(kc == 15))
                    if kc % 8 == 7:
                        yield

            def convsilu(i, bank):
                conv_chunk(bank[:, 0:tb], tb, pre[i], cv[i], tails_dn, i * 16 + hd, "cw_dn", 4)
                yield
                S.act(cv[i][:, 0:tb], cv[i][:, 0:tb], AF.Silu)
                yield
            yield from proj(0, PA)
            yield from proj(1, PBk)
            yield from convsilu(0, PA)
            yield from proj(2, PA)
            yield from convsilu(1, PBk)
            yield from proj(3, PBk)
            yield from convsilu(2, PA)
            S.copy(ACT, ft[4][:, 0:tb], PBk[:, 0:tb]) if False else None
            S.act(G["zsil"][:, 0:tb], PBk[:, 0:tb], AF.Silu)
            S.mm(PA[:, 0:tb], sels[0:32, 32 + hd:33 + hd].to_broadcast([32, 128]), egcT[0:32, 0:tb])
            yield
            for i, dstb in enumerate([G["qn"], G["kn"]]):
                S.act(sq_b[:, 0:tb], cv[i][:, 0:tb], AF.Square)
                S.mm(PBk[:, 0:tb], ones_b[:], sq_b[:, 0:tb])
                yield
                S.act(tmpf[:, 0:tb], PBk[:, 0:tb], AF.Ln, bias=EPS)
                S.act(tmpf[:, 0:tb], tmpf[:, 0:tb], AF.Exp, scale=-0.5, bias=(math.log(128 ** -0.5) if i == 0 else 0.0))
                yield
                S.tt(DVE, dstb[:, 0:tb], cv[i][:, 0:tb], tmpf[:, 0:tb], ALU.mult)
                yield
            S.tt(DVE, G["Qg"][:, 0:tb], G["qn"][:, 0:tb], PA[:, 0:tb], ALU.mult)
            S.copy(ACT, G["v"][:, 0:tb], cv[2][:, 0:tb])
            yield

        def gdn_stage2(hd, tb):
            ntl = tb // 128
            G = gset(hd % 3)
            qn_b, kn_b, v_b = G["qn"], G["kn"], G["v"]
            Ecls = [ft[2][:, 0:256], ft[2][:, 256:512], ft[3][:, 0:256]]

            def bufs(c):
                return cbufs(c, hd % 2)

            def chainA(c):
                tt = c
                cs = slice(tt * 128, (tt + 1) * 128)
                tk = tokS[tt]
                B = bufs(c)
                bank = ps[c]
                trb = bank[:].bitcast(BF16)
                ev = ACT if (c % 2 == 0) else DVE

                def col(o):
                    return tk[:, o + hd:o + hd + 1]
                S.tr(trb[:, 768:896], kn_b[:, cs], ident_b[:])
                S.tr(trb[:, 896:1024], v_b[:, cs], ident_b[:])
                S.ts(DVE, B["Kb"], trb[:, 768:896], col(C_BEGE), None, ALU.mult)
                S.act(B["Kd"], trb[:, 768:896], AF.Copy, scale=col(C_ED))
                S.ts(DVE, B["Vb"], trb[:, 896:1024], col(C_BETA), None, ALU.mult)
                yield
                S.mm(bank[:, 0:128], kn_b[:, cs], qn_b[:, cs])
                S.mm(bank[:, 128:256], kn_b[:, cs], kn_b[:, cs])
                S.mm(bank[:, 256:512], nglT[0:32, cs], sels[0:32, 32 + hd:33 + hd].to_broadcast([32, 256]), start=True, stop=False)
                S.mm(bank[:, 256:384], sels[0:64, hd:hd + 1].to_broadcast([64, 128]), glT[0:64, cs], start=False, stop=False)
                S.mm(bank[:, 384:512], sels[0:64, 16 + hd:17 + hd].to_broadcast([64, 128]), glT[0:64, cs], start=False, stop=True)
                yield
                S.tt(DVE, Ecls[c], bank[:, 256:512], negmask2[:], ALU.min)
                yield
                S.act(B["Em0"], Ecls[c][:, 0:128], AF.Exp)
                S.act(B["Em1"], Ecls[c][:, 128:256], AF.Exp)
                yield
                XP = xp_tile(c, 0)
                S.tt(DVE, B["QKm"], bank[:, 0:128], B["Em0"], ALU.mult)
                S.stt(XP[:, 128:256], bank[:, 128:256], -1.0, B["Em1"], ALU.mult, ALU.mult)
                yield
                S.tr(trb[:, 0:128], XP[:, 128:256], ident_b[:])
                S.copy(ev, XP[:, 256:384], trb[:, 0:128])
                yield
                for j in range(7):
                    last = (j == 6)
                    if not last:
                        XPn = xp_tile(c, 1 + (j % 2))
                        S.mm(bank[:, 0:256], XP[:, 256:384], XP[:, 0:256], start=True, stop=False)
                        S.mm(bank[:, 0:128], ident_b[:], XP[:, 0:128], start=False, stop=True)
                        S.mm(bank[:, 256:384], XP[:, 128:256], XP[:, 256:384], start=True, stop=True)
                        S.copy(ev, XPn, bank[:, 0:384])
                        XP = XPn
                    else:
                        S.mm(bank[:, 0:128], XP[:, 256:384], XP[:, 0:128], start=True, stop=False)
                        S.mm(bank[:, 0:128], ident_b[:], XP[:, 0:128], start=False, stop=True)
                        S.copy(ev, B["TT"], bank[:, 0:128])
                    yield
                S.mm(bank[:, 0:128], B["Kb"], B["TT"])
                S.act(B["nWT"], bank[:, 0:128], AF.Copy, scale=-1.0)
                yield

            return [chainA(c) for c in range(ntl)]

        def gdn_stage3(hd, tb):
            ntl = tb // 128
            G = gset(hd % 3)
            Qg_b, zsil = G["Qg"], G["zsil"]
            oT = ps[3]
            vnew_b = btm[:, 0, 0:128]
            for tt in range(ntl):
                cs = slice(tt * 128, (tt + 1) * 128)
                tk = tokS[tt]
                B = cbufs(tt, hd % 2)
                S.mm(ps[4][:, 0:128], B["TT"], B["Vb"], start=True, stop=False)
                S.mm(ps[4][:, 0:128], B["nWT"], S_b[:, hd, :], start=False, stop=True)
                S.copy(ACT, vnew_b, ps[4][:, 0:128])
                yield
                S.mm(oT[:, cs], S_b[:, hd, :], Qg_b[:, cs], start=True, stop=False)
                S.mm(oT[:, cs], vnew_b, B["QKm"], start=False, stop=True)
                S.mm(ps[5][:, 0:128], B["Kd"], vnew_b)
                yield
                S.stt(Sst[:, hd, :], Sst[:, hd, :], tk[:, C_EGLAST + hd:C_EGLAST + hd + 1], ps[5][:, 0:128], ALU.mult, ALU.add)
                S.copy(ACT, S_b[:, hd, :], Sst[:, hd, :])
                yield
            sq3 = bt[5]
            tmpf3, tmpf2 = ft[4], ft[5]
            S.act(sq3[:, 0:tb], oT[:, 0:tb], AF.Square)
            S.mm(ps[4][:, 0:tb], ones_b[:], sq3[:, 0:tb])
            yield
            S.act(tmpf3[:, 0:tb], ps[4][:, 0:tb], AF.Ln, bias=EPS, scale=1.0 / 128)
            S.act(tmpf3[:, 0:tb], tmpf3[:, 0:tb], AF.Exp, scale=-0.5)
            yield
            S.tt(DVE, tmpf2[:, 0:tb], oT[:, 0:tb], tmpf3[:, 0:tb], ALU.mult)
            S.stt(bigT[:, hd, 0:tb], tmpf2[:, 0:tb], ppc("dnnw"), zsil[:, 0:tb], ALU.mult, ALU.mult)
            yield

        def exhaust(gen):
            for _ in gen:
                pass

        def gdn_all(tb):
            gdn_prep_block()
            for t in range(16 + 2):
                streams = []
                if 0 <= t - 2 < 16:
                    streams.append(S.capture(lambda: exhaust(gdn_stage3(t - 2, tb))))
                if 0 <= t - 1 < 16:
                    for ch in gdn_stage2(t - 1, tb):
                        streams.append(S.capture(lambda: exhaust(ch)))
                if t < 16:
                    streams.append(S.capture(lambda: exhaust(gdn_stage1(t, tb))))
                S.merge(streams)

        def mamba(tb):
            ntl = tb // 128
            for p in range(2):
                wv = next_w("mbc")
                for i in range(4):
                    for kc in range(16):
                        S.mm(ps[i][:, 0:tb], wv[:, kc, i * 128:(i + 1) * 128], hnT[:, kc, 0:tb], start=(kc == 0), stop=(kc == 15))
                for i in range(4):
                    g = 2 * p + i // 2
                    isC = i % 2
                    ch = (20 if isC else 16) + g
                    conv_chunk(ps[i][:, 0:tb], tb, pre[i], cv[i], tails_m2, ch, "cw_m2", 4)
                    silu_exp(bcT[:, 2 * g + isC, 0:tb], cv[i][:, 0:tb], ft[0][:, 0:tb], bias=ppc("cb_m2", ch), nbias=ncb[:, ch:ch + 1])
            for tt in range(ntl):
                cs = slice(tt * 128, (tt + 1) * 128)
                trb = ps[4][:].bitcast(BF16)
                for g in range(4):
                    S.tr(trb[:, g * 128:(g + 1) * 128], bcT[:, 2 * g, cs], ident_b[:])
                S.copy(ACT, btm[:, tt, :], trb[:, 0:512])
            def set_bufs(sidx):
                if sidx == 0:
                    return xs_tm, zs_tm
                xs1 = bigT[:, 40:44, :].rearrange("p a b -> p (a b)").rearrange("p (t c) -> p t c", t=3)
                zs1 = hn_tm[:, 512:2048].rearrange("p (t c) -> p t c", t=3)
                return xs1, zs1

            def m2_prep(g):
                sidx = g % 2
                XS, ZS = set_bufs(sidx)
                wv = next_w("mx")

                def xproj(i):
                    for kc in range(16):
                        S.mm(PAB[i % 2][:, 0:tb], wv[:, kc, i * 128:(i + 1) * 128], hnT[:, kc, 0:tb], start=(kc == 0), stop=(kc == 15))

                def xconv(i):
                    ch = 4 * g + i
                    conv_chunk(PAB[i % 2][:, 0:tb], tb, pre[i], cv[i], tails_m2, ch, "cw_m2", 4)
                    silu_exp(bigT[:, 32 + 4 * sidx + i, 0:tb], cv[i][:, 0:tb], ft[0][:, 0:tb], bias=ppc("cb_m2", ch), nbias=ncb[:, ch:ch + 1])
                xproj(0); xproj(1); xconv(0); xproj(2); xconv(1); xproj(3); xconv(2); xconv(3)
                for tt in range(ntl):
                    cs = slice(tt * 128, (tt + 1) * 128)
                    trb = ps[5][:].bitcast(BF16)
                    for i in range(4):
                        S.tr(trb[:, i * 128:(i + 1) * 128], bigT[:, 32 + 4 * sidx + i, cs], ident_b[:])
                    S.copy(ACT, XS[:, tt, :], trb[:, 0:512])
                wv = next_w("mz")
                for tt in range(ntl):
                    cs = slice(tt * 128, (tt + 1) * 128)
                    for kc in range(16):
                        S.mm(ps[6][:, :], hnT[:, kc, cs], wv[:, kc, :], start=(kc == 0), stop=(kc == 15))
                    silu_exp(ZS[:, tt, :], ps[6][:, :], ft[3][:, :])

            def m2_tiles(g):
                sidx = g % 2
                XS, ZS = set_bufs(sidx)
                Ecl4, y1, y2 = ft[1], ft[2], ft[4]
                cbs = ft[5][:, 0:128]
                xdt_b, xdtd_b, MT4_b, y_b, Eex4 = bt[4], bt[5], bt[6], bt[7], bt[8]
                for tt in range(ntl):
                    cs = slice(tt * 128, (tt + 1) * 128)
                    tk = tokS[tt]

                    def hb(o):
                        return tk[:, o + 8 * g:o + 8 * g + 8].unsqueeze(2).to_broadcast([128, 8, 64])

                    def v3(ap):
                        return ap.rearrange("p (h c) -> p h c", h=8)
                    xv = v3(XS[:, tt, :])
                    S.tt(POOL, v3(xdt_b[:]), xv, hb(C_DT), ALU.mult)
                    S.tt(POOL, v3(xdtd_b[:]), v3(xdt_b[:]), hb(C_ED2), ALU.mult)
                    S.mm(ps[3][:, 0:128], bcT[:, 2 * g, cs], bcT[:, 2 * g + 1, cs])
                    S.copy(ACT, cbs, ps[3][:, 0:128])
                    for hq in range(2):
                        h0 = 8 * g + 4 * hq
                        S.mm(ps[7][:, :], nacsT[0:64, cs],
                             sels[0:64, 48 + h0:48 + h0 + 4].unsqueeze(2).to_broadcast([64, 4, 128]), start=True, stop=False)
                        for i in range(4):
                            S.mm(ps[7][:, i * 128:(i + 1) * 128], sels[0:64, 48 + h0 + i:48 + h0 + i + 1].to_broadcast([64, 128]),
                                 acsT[0:64, cs], start=False, stop=(i == 3))
                        S.tt(DVE, Ecl4[:].rearrange("p (i l) -> p i l", i=4), ps[7][:, :].rearrange("p (i l) -> p i l", i=4),
                             negmask2[:, 0:128].unsqueeze(1).to_broadcast([128, 4, 128]), ALU.min)
                        S.act(Eex4[:], Ecl4[:], AF.Exp)
                        M4 = MT4_b[:].rearrange("p (i l) -> p i l", i=4)
                        S.tt(DVE, M4, Eex4[:].rearrange("p (i l) -> p i l", i=4),
                             cbs.unsqueeze(1).to_broadcast([128, 4, 128]), ALU.mult)
                        for i in range(4):
                            h8 = 4 * hq + i
                            S.mm(ps[0][:, h8 * 64:(h8 + 1) * 64], M4[:, i, :], xdt_b[:, h8 * 64:(h8 + 1) * 64], start=True, stop=True)
                    S.mm(ps[1][:, :], bcT[:, 2 * g + 1, cs], stT_b[:, g, :])
                    S.tt(DVE, v3(y1[:]), v3(ps[1][:, :]), hb(C_EACS), ALU.mult)
                    S.tt(DVE, y1[:], y1[:], ps[0][:, :], ALU.add)
                    o_d, _ = RV["m2_d"]
                    dbc = rv[:, o_d + 8 * g:o_d + 8 * g + 8].unsqueeze(2).to_broadcast([128, 8, 64])
                    S.tt(POOL, v3(y2[:]), xv, dbc, ALU.mult)
                    S.tt(POOL, y2[:], y2[:], y1[:], ALU.add)
                    S.tt(DVE, y2[:], y2[:], ZS[:, tt, :], ALU.mult)
                    r = rms_stats(y2[:], 4, 1.0 / 512)
                    S.act(y_b[:], y2[:], AF.Copy, scale=r)
                    trb = ps[2][:].bitcast(BF16)
                    for i in range(4):
                        S.tr(trb[:, i * 128:(i + 1) * 128], y_b[:, i * 128:(i + 1) * 128], ident_b[:])
                    for i in range(4):
                        dst = bigT[:, 16 + 4 * g + i, cs]
                        if i % 2 == 0:
                            S.ts(DVE, dst, trb[:, i * 128:(i + 1) * 128], ppc("m2nw", 4 * g + i), None, ALU.mult)
                        else:
                            S.act(dst, trb[:, i * 128:(i + 1) * 128], AF.Copy, scale=ppc("m2nw", 4 * g + i))
                    S.mm(ps[3][:, :], btm[:, tt, g * 128:(g + 1) * 128], xdtd_b[:])
                    S.tt(DVE, v3(stT[:, g, :]), v3(stT[:, g, :]), hb(C_EALAST), ALU.mult)
                    S.tt(DVE, stT[:, g, :], stT[:, g, :], ps[3][:, :], ALU.add)
                    S.copy(ACT, stT_b[:, g, :], stT[:, g, :])


            PAB = [ps[4], ps[5]]
            S.merge([S.capture(lambda: m2_prep(0))])
            for g in range(4):
                streams = [S.capture(lambda: m2_tiles(g))]
                if g + 1 < 4:
                    streams.append(S.capture(lambda: m2_prep(g + 1)))
                S.merge(streams)

        store_streams = []
        for bi, (tok0, tb) in enumerate(blocks):
            ntl = tb // 128
            for tt in range(ntl):
                a0 = tok0 + tt * 128
                r0 = a0 - NMETA
                lo, hi = max(r0, 0), min(r0 + 128, seq)
                if a0 == 0:
                    S.dma(SP, h[0:NMETA, tt, :], meta[:, :], f"h{tt}")
                    S.dma(SP, h[NMETA:128, tt, :], x[0:128 - NMETA, :], f"h{tt}")
                else:
                    if hi - lo < 128:
                        S.memset(POOL, h[:, tt, :], 0.0)
                    if hi > lo:
                        S.dma(SP, h[0:hi - lo, tt, :], x[lo:hi, :], f"h{tt}")
            if do_gdn or do_m2:
                rmsnorm_to_T("nw_mix", tb)
                small_proj(tb)
                if do_gdn:
                    gdn_all(tb)
                else:
                    S.memset(POOL, bigT[:, 0:16, :], 0.0)
                if do_m2:
                    mamba(tb)
                else:
                    S.memset(POOL, bigT[:, 16:32, :], 0.0)
                for cb in range(4):
                    for kg in range(2):
                        wv = next_w("wo")
                        for tt in range(ntl):
                            for k in range(16):
                                kc = kg * 16 + k
                                S.mm(ps[tt][:, :], bigT[:, kc, tt * 128:(tt + 1) * 128], wv[:, k, :], start=(kc == 0), stop=(kc == 31))
                    for tt in range(ntl):
                        S.tt(DVE, h[:, tt, cb * 512:(cb + 1) * 512], h[:, tt, cb * 512:(cb + 1) * 512], ps[tt][:, :], ALU.add)
            if do_ffn:
                rmsnorm_to_T("nw_ffn", tb)
                for u in range(22):
                    wv = next_w("up")
                    for half in range(2):
                        f_g = 2 * u + half
                        for which in range(2):
                            idx = half * 2 + which
                            pst = ps[idx]
                            co = which * 256 + half * 128
                            for kc in range(16):
                                S.mm(pst[:, 0:tb], wv[:, kc, co:co + 128], hnT[:, kc, 0:tb], start=(kc == 0), stop=(kc == 15))
                            conv_chunk(pst[:, 0:tb], tb, pre[idx], cv[idx], tails_ff, f_g + which * 44, "cw_ff", 3)
                        cg, cvv = cv[half * 2], cv[half * 2 + 1]
                        S.act(cg[:, 0:tb], cg[:, 0:tb], AF.Silu)
                        S.tt(DVE, bigT[:, f_g, 0:tb], cg[:, 0:tb], cvv[:, 0:tb], ALU.mult)
                for cb in range(4):
                    for kg in range(3):
                        wv = next_w("dn")
                        nk = 16 if kg < 2 else 12
                        for tt in range(ntl):
                            for k in range(nk):
                                kc = kg * 16 + k
                                S.mm(ps[tt][:, :], bigT[:, kc, tt * 128:(tt + 1) * 128], wv[:, k, :], start=(kc == 0), stop=(kc == 43))
                    for tt in range(ntl):
                        S.tt(DVE, h[:, tt, cb * 512:(cb + 1) * 512], h[:, tt, cb * 512:(cb + 1) * 512], ps[tt][:, :], ALU.add)
            for tt in range(ntl):
                r = rms_stats(h[:, tt, :], tt, 1.0 / D)
                S.stt(h[:, tt, :], h[:, tt, :], r, rvc("nwf"), ALU.mult, ALU.mult)
                a0 = tok0 + tt * 128
                r0 = a0 - NMETA
                lo, hi = max(r0, 0), min(r0 + 128, seq)
                if hi > lo:
                    S.dma(SP, out[lo:hi, :], h[lo - r0:hi - r0, tt, :], f"o{tt}")
                    if f"o{tt}" not in store_streams:
                        store_streams.append(f"o{tt}")
        S.emit(store_streams)
        print("ops", S.stats)
    return nc


_NC_CACHE = {}


def kernel(**inputs):
    inp = {k: np.asarray(v) for k, v in inputs.items()}
    x = inp["x"]
    B = x.shape[0]
    W = prep_weights(inp)
    if "nc" not in _NC_CACHE:
        _NC_CACHE["nc"] = build(seq=SEQ)
    nc = _NC_CACHE["nc"]
    in_maps = []
    for b in range(B):
        in_maps.append(dict(x=np.ascontiguousarray(x[b], dtype=np.float32), meta=W["meta"], wbig=W["wbig"],
                            wsm=W["wsm"], pp=W["pp"], rv=W["rv"]))
    res = run_bass_kernel_spmd(nc, in_maps, core_ids=list(range(B)))
    return np.stack([np.asarray(r["out"], dtype=np.float32) for r in res.results], axis=0)
```
